# Optimizing a Trainium2 kernel written in Bass

```python
import math
import jax, jax.numpy as jnp
from jax import lax
import numpy as np

D_MODEL = 1024
BATCH = 4
SEQ = 8192
DEPTH = 1

D_MIX = D_MODEL
RET_WIDTH = D_MIX // 2
RET_HEADS = 4
RET_HEAD_DIM = RET_WIDTH // RET_HEADS
RET_CHUNK = 128
ROPE_BASE = 10000.0
SSM_WIDTH = D_MIX - RET_WIDTH
SSM_GROUP = 16
SSM_GROUPS = SSM_WIDTH // SSM_GROUP
SSM_STATE = 64
DT_MIN = 1e-3
DT_MAX = 1e-1
D_FF = 4 * D_MODEL
NORM_EPS = 1e-6
IN_COLS = 4 * RET_WIDTH + SSM_WIDTH

kernel_name = "hymba_retnet_s5_sandwich_block"


def rmsnorm(x, g):
    xf = x.astype(jnp.float32)
    y = xf * lax.rsqrt(jnp.mean(xf * xf, axis=-1, keepdims=True) + NORM_EPS) * g.astype(jnp.float32)
    return y.astype(x.dtype)


def rope(x):
    L, d = x.shape[1], x.shape[-1]
    half = d // 2
    inv_freq = ROPE_BASE ** (-jnp.arange(half, dtype=jnp.float32) / half)
    ang = jnp.arange(L, dtype=jnp.float32)[:, None] * inv_freq[None, :]
    cos = jnp.cos(ang)[None, :, None, :]
    sin = jnp.sin(ang)[None, :, None, :]
    x1, x2 = x[..., :half], x[..., half:]
    return jnp.concatenate([x1 * cos - x2 * sin, x1 * sin + x2 * cos], axis=-1)


def retention_chunkwise(q, k, v):
    B, L, H, d = q.shape
    C = RET_CHUNK
    nc = L // C
    log_gamma = jnp.log(1.0 - jnp.exp(jnp.linspace(math.log(1.0 / 32), math.log(1.0 / 512), H))).astype(jnp.float32)
    q = q.reshape(B, nc, C, H, d)
    k = k.reshape(B, nc, C, H, d)
    v = v.reshape(B, nc, C, H, d)
    idx = jnp.arange(C, dtype=jnp.float32)
    diff = idx[:, None] - idx[None, :]
    decay = jnp.where(diff[None] >= 0, jnp.exp(jnp.maximum(diff, 0.0)[None] * log_gamma[:, None, None]), 0.0)
    s = jnp.einsum('bnihk,bnjhk->bnhij', q, k) * decay[None, None]
    inner = jnp.einsum('bnhij,bnjhd->bnihd', s, v)
    zeta = jnp.exp((C - 1 - idx)[None, :] * log_gamma[:, None])
    S = jnp.einsum('bnjhk,bnjhd,hj->bnhkd', k, v, zeta)
    g_chunk = jnp.exp(C * log_gamma)[None, :, None, None]

    def step(R, S_i):
        return g_chunk * R + S_i, R

    R0 = jnp.zeros((B, H, d, d), jnp.float32)
    _, R_prev = lax.scan(step, R0, jnp.moveaxis(S, 1, 0))
    R_prev = jnp.moveaxis(R_prev, 0, 1)
    xi = jnp.exp((idx + 1.0)[None, :] * log_gamma[:, None])
    cross = jnp.einsum('bnihk,bnhkd,hi->bnihd', q, R_prev, xi)
    return (inner + cross).reshape(B, L, H, d)


def head_groupnorm(y, g):
    mu = jnp.mean(y, axis=-1, keepdims=True)
    var = jnp.mean(jnp.square(y - mu), axis=-1, keepdims=True)
    yn = (y - mu) * lax.rsqrt(var + NORM_EPS)
    return yn * g.astype(jnp.float32).reshape(RET_HEADS, RET_HEAD_DIM)


def s5_scan(u, lam_re, lam_im, log_dt, b_re, b_im, c_re, c_im, d_skip):
    B, L, _ = u.shape
    uf = u.astype(jnp.float32).reshape(B, L, SSM_GROUPS, SSM_GROUP)
    lam = lax.complex(jnp.minimum(lam_re.astype(jnp.float32), -1e-4), lam_im.astype(jnp.float32))
    dt = jnp.exp(log_dt.astype(jnp.float32))[:, None]
    lam_bar = jnp.exp(lam * dt)
    b_c = lax.complex(b_re.astype(jnp.float32), b_im.astype(jnp.float32))
    b_bar = ((lam_bar - 1.0) / lam)[:, :, None] * b_c
    bu = jnp.einsum('blgc,gpc->blgp', uf.astype(jnp.complex64), b_bar)
    a = jnp.broadcast_to(lam_bar, bu.shape)

    def combine(e1, e2):
        a1, x1 = e1
        a2, x2 = e2
        return a2 * a1, a2 * x1 + x2

    _, states = lax.associative_scan(combine, (a, bu), axis=1)
    c_c = lax.complex(c_re.astype(jnp.float32), c_im.astype(jnp.float32))
    y = jnp.real(jnp.einsum('blgp,gcp->blgc', states, c_c))
    y = y + d_skip.astype(jnp.float32).reshape(SSM_GROUPS, SSM_GROUP) * uf
    return y.reshape(B, L, SSM_WIDTH)


def setup_inputs(seed: int = 0) -> dict:
    key = jax.random.key(seed)
    ks = jax.random.split(key, 20)
    f32 = jnp.float32
    nrm = lambda k, shape, scale: (jax.random.normal(k, shape, f32) * scale)
    gain = lambda k, shape: 1.0 + 0.02 * jax.random.normal(k, shape, f32)
    x = jax.random.normal(ks[0], (BATCH, SEQ, D_MODEL), f32)
    lam_im_base = math.pi * jnp.arange(SSM_STATE, dtype=f32)
    return {
        "x": x,
        "norm_mix_pre": gain(ks[1], (DEPTH, D_MODEL)),
        "norm_mix_post": gain(ks[2], (DEPTH, D_MODEL)),
        "w_in": nrm(ks[3], (DEPTH, D_MODEL, IN_COLS), D_MODEL ** -0.5),
        "ret_gn_gain": gain(ks[4], (DEPTH, RET_WIDTH)),
        "ssm_lambda_re": -0.5 + 0.01 * jax.random.normal(ks[5], (DEPTH, SSM_GROUPS, SSM_STATE), f32),
        "ssm_lambda_im": lam_im_base + 0.01 * jax.random.normal(ks[6], (DEPTH, SSM_GROUPS, SSM_STATE), f32),
        "ssm_log_dt": jax.random.uniform(ks[7], (DEPTH, SSM_GROUPS), f32, math.log(DT_MIN), math.log(DT_MAX)),
        "ssm_b_re": nrm(ks[8], (DEPTH, SSM_GROUPS, SSM_STATE, SSM_GROUP), (2 * SSM_GROUP) ** -0.5),
        "ssm_b_im": nrm(ks[9], (DEPTH, SSM_GROUPS, SSM_STATE, SSM_GROUP), (2 * SSM_GROUP) ** -0.5),
        "ssm_c_re": nrm(ks[10], (DEPTH, SSM_GROUPS, SSM_GROUP, SSM_STATE), (2 * SSM_STATE) ** -0.5),
        "ssm_c_im": nrm(ks[11], (DEPTH, SSM_GROUPS, SSM_GROUP, SSM_STATE), (2 * SSM_STATE) ** -0.5),
        "ssm_d": nrm(ks[12], (DEPTH, SSM_WIDTH), 1.0),
        "w_glu": nrm(ks[13], (DEPTH, SSM_WIDTH, 2 * SSM_WIDTH), SSM_WIDTH ** -0.5),
        "w_out": nrm(ks[14], (DEPTH, D_MIX, D_MODEL), D_MIX ** -0.5),
        "norm_mlp_pre": gain(ks[15], (DEPTH, D_MODEL)),
        "norm_mlp_post": gain(ks[16], (DEPTH, D_MODEL)),
        "w_ff1": nrm(ks[17], (DEPTH, D_MODEL, D_FF), D_MODEL ** -0.5),
        "w_ff2": nrm(ks[18], (DEPTH, D_FF, D_MODEL), D_FF ** -0.5),
    }


def reference(x, norm_mix_pre, norm_mix_post, w_in, ret_gn_gain, ssm_lambda_re, ssm_lambda_im,
              ssm_log_dt, ssm_b_re, ssm_b_im, ssm_c_re, ssm_c_im, ssm_d, w_glu, w_out,
              norm_mlp_pre, norm_mlp_post, w_ff1, w_ff2):
    B, L, _ = x.shape
    for i in range(DEPTH):
        h = rmsnorm(x, norm_mix_pre[i])
        proj = h @ w_in[i]
        q, k, v, gate, u = jnp.split(proj, [RET_WIDTH, 2 * RET_WIDTH, 3 * RET_WIDTH, 4 * RET_WIDTH], axis=-1)
        heads = lambda t: t.astype(jnp.float32).reshape(B, L, RET_HEADS, RET_HEAD_DIM)
        qh = rope(heads(q))
        kh = rope(heads(k)) * (RET_HEAD_DIM ** -0.5)
        vh = heads(v)
        y_ret = head_groupnorm(retention_chunkwise(qh, kh, vh), ret_gn_gain[i]).reshape(B, L, RET_WIDTH)
        y_ret = (jax.nn.silu(gate.astype(jnp.float32)) * y_ret).astype(x.dtype)

        y_ssm = jax.nn.gelu(s5_scan(u, ssm_lambda_re[i], ssm_lambda_im[i], ssm_log_dt[i], ssm_b_re[i],
                                    ssm_b_im[i], ssm_c_re[i], ssm_c_im[i], ssm_d[i])).astype(x.dtype)
        glu_a, glu_b = jnp.split(y_ssm @ w_glu[i], 2, axis=-1)
        y_ssm = glu_a * jax.nn.sigmoid(glu_b)

        mix = jnp.concatenate([y_ret, y_ssm], axis=-1) @ w_out[i]
        x = x + rmsnorm(mix, norm_mix_post[i])

        h = rmsnorm(x, norm_mlp_pre[i])
        m = jnp.square(jax.nn.relu(h @ w_ff1[i])) @ w_ff2[i]
        x = x + rmsnorm(m, norm_mlp_post[i])
    return x
```

```python
import math
import os
STOP = os.environ.get('KSTOP')
from contextlib import ExitStack

import numpy as np
import concourse.bass as bass
import concourse.mybir as mybir
from concourse.bass_utils import run_bass_kernel_spmd

F32 = mybir.dt.float32
BF16 = mybir.dt.bfloat16
I32 = mybir.dt.int32
AF = mybir.ActivationFunctionType
ALU = mybir.AluOpType

ENGS = ("pe", "act", "dve", "pool", "sp")
SAME_ENG_WAITS = os.environ.get('KSAME', '1') == '1'
EPOCH = 30000
NDMA = 8


class Sched:
    def __init__(self):
        self.prog = {e: [] for e in ENGS}
        self.cnt = {e: 0 for e in ENGS}
        self.seen = {e: {} for e in ENGS}
        self.lw = {}
        self.rd = {}
        self.dma_i = {}
        self.dma_val = {}
        self.semkeys = set()

    def _deps(self, reads, writes):
        d = {}

        def add(x):
            if x is None:
                return
            s, v = x
            if d.get(s, 0) < v:
                d[s] = v

        for k in reads:
            add(self.lw.get(k))
        for k in writes:
            add(self.lw.get(k))
            for r in self.rd.get(k, ()):
                add(r)
        return d

    def _emit(self, eng, d, fn, my, inc):
        waits = []
        for s, v in d.items():
            if self.seen[eng].get(s, 0) < v:
                self.seen[eng][s] = v
                if s[0] == eng and (eng == "pe" or not SAME_ENG_WAITS):
                    continue
                waits.append((s, v))
        self.prog[eng].append((waits, fn, my, inc))
        if my is not None:
            self.semkeys.add(my[0])

    def _update(self, reads, writes, my):
        for k in writes:
            self.lw[k] = my
            self.rd[k] = []
        for k in reads:
            self.rd.setdefault(k, []).append(my)

    def op(self, eng, fn, reads=(), writes=()):
        self.nrec = getattr(self, 'nrec', 0) + 1
        if self.nrec > int(os.environ.get('KMAX', '100000000')):
            return
        d = self._deps(reads, writes)
        c = self.cnt[eng]
        self.cnt[eng] = c + 1
        my = ((eng, c // EPOCH), c % EPOCH + 1)
        self._emit(eng, d, fn, my, 1)
        self._update(reads, writes, my)

    def dma(self, fn, reads=(), writes=(), q="sp", slow=False):
        if slow and os.environ.get('KNOSLOW'):
            return
        self.nrec = getattr(self, 'nrec', 0) + 1
        if self.nrec > int(os.environ.get('KMAX', '100000000')):
            return
        d = self._deps(reads, writes)
        i = self.dma_i.get(q, 0)
        self.dma_i[q] = (i + 1) % NDMA
        sk = ("dma_" + q, i)
        pv = self.dma_val.get(sk, 0)
        if pv > 0:
            d[sk] = max(d.get(sk, 0), pv)
        self.dma_val[sk] = pv + 16
        my = (sk, pv + 16)
        self._emit(q, d, fn, my, 16)
        self._update(reads, writes, my)

    def barrier(self):
        allv = {}
        for e in ENGS:
            c = self.cnt[e]
            if c > 0:
                allv[(e, (c - 1) // EPOCH)] = (c - 1) % EPOCH + 1
        for sk, v in self.dma_val.items():
            if v > 0:
                allv[sk] = v
        for e in ENGS:
            waits = []
            for s, v in allv.items():
                if self.seen[e].get(s, 0) < v:
                    self.seen[e][s] = v
                    waits.append((s, v))
            self.prog[e].append((waits, None, None, 0))
        self.lw = {}
        self.rd = {}

    def replay(self, eng, e, sems):
        for waits, fn, my, inc in self.prog[eng]:
            for s, v in waits:
                e.wait_ge(sems[s], v)
            if fn is not None:
                ins = fn(e)
                ins.then_inc(sems[my[0]], inc)


def bc(ap, axis, n):
    a = ap.unsqueeze(axis)
    shp = list(a.shape)
    shp[axis] = n
    return a.broadcast_to(shp)


class Arena:
    def __init__(self, A):
        self.A = A
        self.Ab = A.bitcast(BF16)
        self.Ai = A.bitcast(I32)
        self.top = 0

    def f32(self, n):
        o = self.top
        self.top += n
        return self.A[:, o:o + n]

    def i32(self, n):
        o = self.top
        self.top += n
        return self.Ai[:, o:o + n]

    def bf(self, n):
        o = self.top
        self.top += (n + 1) // 2
        return self.Ab[:, 2 * o:2 * o + n]


LNG = [math.log(1.0 - math.exp(v)) for v in np.linspace(math.log(1.0 / 32), math.log(1.0 / 512), 4)]
GHEAD = [math.exp(128 * v) for v in LNG]
INVF = (np.float32(10000.0) ** (-(np.arange(64, dtype=np.float32) / np.float32(64)))).astype(np.float32)
TWO_PI = 2.0 * math.pi
C1 = 6.28125
C2 = TWO_PI - C1
AW = 52224


DBG = {}


def build(NPRE, NMAIN):
    nc = bass.Bass("TRN2", target_bir_lowering=False)
    NT = NMAIN * 1024
    dr = lambda n, s, dt=F32, kind="ExternalInput": nc.dram_tensor(n, s, dt, kind=kind).ap()
    x_own = dr("x_own", [NT, 1024])
    x_pre = dr("x_pre", [max(NPRE, 1) * 1024, 1024])
    posb = dr("posb", [128, 2])
    g_pre_d = dr("norm_mix_pre", [8, 128]); g_post_d = dr("norm_mix_post", [1, 1024])
    w_in_d = dr("w_in", [1024, 2560]); ggn_d = dr("ret_gn_gain", [1, 512])
    lre_d = dr("ssm_lambda_re", [32, 64]); lim_d = dr("ssm_lambda_im", [32, 64]); ldt_d = dr("ssm_log_dt", [1, 32])
    bre_d = dr("ssm_b_re", [32, 64, 16]); bim_d = dr("ssm_b_im", [32, 64, 16])
    cre_d = dr("ssm_c_re", [32, 16, 64]); cim_d = dr("ssm_c_im", [32, 16, 64]); sd_d = dr("ssm_d", [32, 16])
    w_glu_d = dr("w_glu", [512, 1024]); w_out_d = dr("w_out", [1024, 1024])
    g2pre_d = dr("norm_mlp_pre", [8, 128]); g2post_d = dr("norm_mlp_post", [1, 1024])
    w1_d = dr("w_ff1", [1024, 4096]); w2_d = dr("w_ff2", [4096, 1024])
    out_d = dr("out", [NT, 1024], kind="ExternalOutput")
    x1s = dr("x1s", [NT, 1024], kind="Internal")

    S = Sched()
    es = ExitStack()
    A_t = es.enter_context(nc.sbuf_tensor("arena", [128, AW], F32))
    PSt = [es.enter_context(nc.psum_tensor(f"ps{i}", [128, 512], F32)) for i in range(8)]
    ar = Arena(A_t)

    def psv(i, n=1):
        assert n == 1
        return PSt[i][:, :]

    def dbc(ap1, n=128):
        return bass.AP(ap1.tensor, ap1.offset, [[0, n]] + [list(d) for d in ap1.ap[1:]])

    DV, AC, PL, PE = "dve", "act", "pool", "pe"
    TT = lambda o, a, b, op: (lambda e: e.tensor_tensor(out=o, in0=a, in1=b, op=op))
    TS = lambda o, a, s1, s2, op0, op1=None: (lambda e: e.tensor_scalar(out=o, in0=a, scalar1=s1, scalar2=s2, op0=op0, op1=op1) if op1 is not None
                                              else e.tensor_scalar(out=o, in0=a, scalar1=s1, scalar2=None, op0=op0))
    STT = lambda o, a, s, b, op0, op1: (lambda e: e.scalar_tensor_tensor(out=o, in0=a, scalar=s, in1=b, op0=op0, op1=op1))
    CP = lambda o, a: (lambda e: e.tensor_copy(out=o, in_=a))
    ACT = lambda o, a, f, **kw: (lambda e: e.activation(out=o, in_=a, func=f, **kw))
    MM = lambda o, l, r, st, sp: (lambda e: e.matmul(o, lhsT=l, rhs=r, start=st, stop=sp))
    TR = lambda o, a, idn: (lambda e: e.transpose(o, a, idn))
    DMA = lambda o, a, **kw: (lambda e: e.dma_start(out=o, in_=a, **kw))
    MS = lambda o, v: (lambda e: e.memset(o, v))

    Win = ar.bf(8 * 2560); Win3 = Win.rearrange("p (k c) -> p k c", k=8)
    Wintra = ar.bf(32 * 128); Wintra3 = Wintra.rearrange("p (g c) -> p g c", g=32)
    Wst = ar.bf(32 * 128); Wst4 = Wst.rearrange("p (g r q) -> p g r q", g=32, r=2)
    Wcr = ar.bf(2 * 16 * 128); Wcr4 = Wcr.rearrange("p (r g c) -> p r g c", r=2, g=16)
    COS = ar.f32(16 * 64); COS3 = COS.rearrange("p (g m) -> p g m", g=16)
    SIN = ar.f32(16 * 64); SIN3 = SIN.rearrange("p (g m) -> p g m", g=16)
    R8 = ar.f32(16)
    CAR = ar.f32(32); CAR3 = CAR.rearrange("p (r g) -> p r g", r=2)
    identb = ar.bf(128); identf = ar.f32(128)
    mask01 = ar.f32(128)
    MASKH = ar.f32(512); MASKH3 = MASKH.rearrange("p (h i) -> p h i", h=4)
    zcol = ar.f32(4); xi = ar.f32(4); xi2 = ar.f32(4); pidx = ar.f32(1); pidx1 = ar.f32(1)
    GPOST = ar.f32(1024); GGN = ar.f32(512)
    gpre = ar.f32(8); g2pre = ar.f32(8)
    invf = ar.f32(64)
    iopc = ar.f32(8)
    posc = ar.f32(2)
    Rst = ar.f32(512); Rst3 = Rst.rearrange("p (h d) -> p h d", h=4)
    Rb = ar.bf(512); Rb3 = Rb.rearrange("p (h d) -> p h d", h=4)
    yrT = ar.bf(4 * 1024); yrT3 = yrT.rearrange("p (k t) -> p k t", k=4)
    Tb = ar.bf(4096)
    Rb2 = ar.bf(512)
    Wglu = ar.bf(4096); Wglu3 = Wglu.rearrange("p (k c) -> p k c", k=4)
    Wout = ar.bf(8192); Wout3 = Wout.rearrange("p (k c) -> p k c", k=8)
    mhalf = ar.f32(8)
    P_BASE = ar.top

    m0 = ar.top
    ioi = ar.i32(128); iof = ar.f32(128)
    S.op(PL, lambda e: e.iota(ioi, pattern=[[1, 128]], base=0, channel_multiplier=-1), writes=["ioi"])
    S.op(DV, CP(iof, ioi), reads=["ioi"], writes=["iof"])
    S.op(DV, lambda e: e.tensor_single_scalar(identf, iof, 0.0, op=ALU.is_equal), reads=["iof"], writes=["identf"])
    S.op(DV, CP(identb, identf), reads=["identf"], writes=["identb"])
    S.op(DV, lambda e: e.tensor_single_scalar(mask01, iof, 0.0, op=ALU.is_ge), reads=["iof"], writes=["mask01"])
    pii = ar.i32(1)
    S.op(PL, lambda e: e.iota(pii, pattern=[[0, 1]], base=0, channel_multiplier=1), writes=["pii"])
    S.op(DV, CP(pidx, pii), reads=["pii"], writes=["pidx"])
    S.op(DV, TS(pidx1, pidx, 1.0, None, ALU.add), reads=["pidx"], writes=["pidx1"])
    pm = ar.f32(1)
    S.op(DV, TS(pm, pidx, -1.0, 127.0, ALU.mult, ALU.add), reads=["pidx"], writes=["pm"])
    gcol = ar.f32(4)
    DH = 128.0 ** -0.5
    for h in range(4):
        S.op(AC, ACT(gcol[:, h:h + 1], pidx1, AF.Exp, scale=-LNG[h]), reads=["pidx1"], writes=["gcol"])
        S.op(AC, ACT(zcol[:, h:h + 1], pm, AF.Exp, scale=LNG[h]), reads=["pm"], writes=["zcol"])
        S.op(AC, ACT(xi[:, h:h + 1], pidx1, AF.Exp, scale=LNG[h]), reads=["pidx1"], writes=["xi"])
    S.op(DV, TS(gcol, gcol, DH, None, ALU.mult), reads=["gcol"], writes=["gcol"])
    S.op(DV, TS(zcol, zcol, DH, None, ALU.mult), reads=["zcol"], writes=["zcol"])
    S.op(DV, TT(xi2, xi, xi, ALU.mult), reads=["xi"], writes=["xi2"])
    for h in range(4):
        S.op(DV, TS(MASKH3[:, h, :], mask01, gcol[:, h:h + 1], None, ALU.mult), reads=["mask01", "gcol"], writes=["MASKH"])
    for j in range(64):
        S.op(PL, MS(invf[:, j:j + 1], float(INVF[j])), writes=["invf"])
    ioci = ar.i32(8)
    S.op(PL, lambda e: e.iota(ioci, pattern=[[128, 8]], base=0, channel_multiplier=1), writes=["ioci"])
    S.op(DV, CP(iopc, ioci), reads=["ioci"], writes=["iopc"])
    S.dma(DMA(posc, posb), writes=["posc"])
    S.dma(DMA(GPOST, dbc(g_post_d)), writes=["GPOST"])
    S.dma(DMA(GGN, dbc(ggn_d)), writes=["GGN"])
    g8 = ar.f32(128)
    for (src, dst, nm) in ((g_pre_d, gpre, "gpre"), (g2pre_d, g2pre, "g2pre")):
        S.dma(DMA(g8[0:8, :], src), writes=["g8"])
        S.op(PE, TR(PSt[0][:, 0:8], g8[0:8, :], identf[0:8, 0:8]), reads=["g8", "identf"], writes=["ps0"])
        S.op(DV, CP(dst, PSt[0][:, 0:8]), reads=["ps0"], writes=[nm])
    w_in_v = w_in_d.rearrange("(k p) c -> p k c", p=128)
    wg_v = w_glu_d.rearrange("(k p) c -> p k c", p=128)
    wo_v = w_out_d.rearrange("(k p) c -> p k c", p=128)
    for kt in range(8):
        S.dma(DMA(Win3[:, kt, :], w_in_v[:, kt, :], max_dma_last_dim=4096), q="pool", writes=["Win"])
    LRE = ar.f32(16); LIM = ar.f32(16); DT = ar.f32(16)
    BRE = ar.f32(256); BIM = ar.f32(256); CRE = ar.f32(256); CIM = ar.f32(256)
    Dcol = ar.f32(32)
    for par in range(2):
        ps_ = slice(64 * par, 64 * par + 64)
        for (dst, src, nm) in ((LRE, lre_d, "LRE"), (LIM, lim_d, "LIM")):
            S.dma(DMA(dst[ps_, :], bass.AP(src.tensor, par * 64, [[1, 64], [128, 16]]), allow_slow_non_contiguous=True), slow=True, writes=[nm])
        S.dma(DMA(DT[ps_, :], bass.AP(ldt_d.tensor, par, [[0, 64], [2, 16]]), allow_slow_non_contiguous=True), slow=True, writes=["DT"])
        for (dst, src, nm) in ((BRE, bre_d, "BRE"), (BIM, bim_d, "BIM")):
            S.dma(DMA(dst[ps_, :].rearrange("p (g c) -> p g c", g=16), bass.AP(src.tensor, par * 1024, [[16, 64], [2048, 16], [1, 16]])), writes=[nm])
    for par in range(2):
        ps_ = slice(64 * par, 64 * par + 64)
        for (dst, src, nm) in ((CRE, cre_d, "CRE"), (CIM, cim_d, "CIM")):
            for gg in range(16):
                S.dma(DMA(dst[ps_, gg * 16:(gg + 1) * 16], bass.AP(src.tensor, par * 1024 + gg * 2048, [[1, 64], [64, 16]]), allow_slow_non_contiguous=True),
                      slow=True, writes=[nm], q=("sp" if nm == "CRE" else "act"))
    for sp_ in range(8):
        S.dma(DMA(Dcol[16 * sp_:16 * sp_ + 16, :], bass.AP(sd_d.tensor, 0, [[1, 16], [16, 32]]), allow_slow_non_contiguous=True), slow=True, writes=["Dcol"])
    cnt = [0]

    def tmp(n):
        return ar.f32(n)

    def vop(fn, reads, writes):
        cnt[0] += 1
        S.op(DV, fn, reads=reads, writes=writes)

    a_ = tmp(16); th = tmp(16); rr = tmp(16); kf = tmp(16); ki = ar.i32(16); red = tmp(16); sn = tmp(16); cs = tmp(16); ab = tmp(16)
    vop(TS(LRE, LRE, -1e-4, None, ALU.min), ["LRE"], ["LRE"])
    S.op(AC, ACT(DT, DT, AF.Exp), reads=["DT"], writes=["DT"])
    vop(TT(a_, LRE, DT, ALU.mult), ["LRE", "DT"], ["a_"])
    vop(TT(th, LIM, DT, ALU.mult), ["LIM", "DT"], ["th"])
    S.op(AC, ACT(rr, a_, AF.Exp), reads=["a_"], writes=["rr"])
    vop(TS(kf, th, 1.0 / TWO_PI, None, ALU.mult), ["th"], ["kf"])
    vop(CP(ki, kf), ["kf"], ["ki"])
    vop(CP(kf, ki), ["ki"], ["kf"])
    vop(STT(red, kf, -C1, th, ALU.mult, ALU.add), ["kf", "th"], ["red"])
    vop(STT(red, kf, -C2, red, ALU.mult, ALU.add), ["kf", "red"], ["red"])
    vop(TS(red, red, math.pi, -math.pi, ALU.min, ALU.max), ["red"], ["red"])
    S.op(AC, ACT(sn, red, AF.Sin), reads=["red"], writes=["sn"])
    vop(TS(ab, red, -1.0, None, ALU.mult), ["red"], ["ab"])
    vop(TT(ab, ab, red, ALU.max), ["ab", "red"], ["ab"])
    vop(TS(ab, ab, -1.0, math.pi / 2, ALU.mult, ALU.add), ["ab"], ["ab"])
    S.op(AC, ACT(cs, ab, AF.Sin), reads=["ab"], writes=["cs"])
    PR = tmp(9 * 16); PI = tmp(9 * 16); QR = tmp(8 * 16); QI = tmp(8 * 16)
    PR3 = PR.rearrange("p (j g) -> p j g", g=16); PI3 = PI.rearrange("p (j g) -> p j g", g=16)
    QR3 = QR.rearrange("p (j g) -> p j g", g=16); QI3 = QI.rearrange("p (j g) -> p j g", g=16)
    t1 = tmp(16); t2 = tmp(16); ivr = tmp(16); ivi = tmp(16); r2 = tmp(16)
    vop(MS(PR3[:, 0, :], 1.0), [], ["PR"]); vop(MS(PI3[:, 0, :], 0.0), [], ["PI"])
    vop(MS(QR3[:, 0, :], 1.0), [], ["QR"]); vop(MS(QI3[:, 0, :], 0.0), [], ["QI"])
    vop(TT(PR3[:, 1, :], rr, cs, ALU.mult), ["rr", "cs", "PR"], ["PR"])
    vop(TT(PI3[:, 1, :], rr, sn, ALU.mult), ["rr", "sn", "PI"], ["PI"])
    vop(TT(r2, rr, rr, ALU.mult), ["rr"], ["r2"])
    vop(lambda e: e.reciprocal(out=r2, in_=r2), ["r2"], ["r2"])
    vop(TT(ivr, PR3[:, 1, :], r2, ALU.mult), ["PR", "r2"], ["ivr"])
    vop(TT(ivi, PI3[:, 1, :], r2, ALU.mult), ["PI", "r2"], ["ivi"])
    vop(TS(ivi, ivi, -1.0, None, ALU.mult), ["ivi"], ["ivi"])

    def cmul(outr, outi, ar_, ai_, br_, bi_, rk, wk):
        vop(TT(t1, ar_, br_, ALU.mult), rk, ["t1"]); vop(TT(t2, ai_, bi_, ALU.mult), rk, ["t2"])
        vop(TT(outr, t1, t2, ALU.subtract), ["t1", "t2"] + wk, wk)
        vop(TT(t1, ar_, bi_, ALU.mult), rk + ["t1"], ["t1"]); vop(TT(t2, ai_, br_, ALU.mult), rk + ["t2"], ["t2"])
        vop(TT(outi, t1, t2, ALU.add), ["t1", "t2"] + wk, wk)

    for j in range(1, 8):
        cmul(PR3[:, j + 1, :], PI3[:, j + 1, :], PR3[:, j, :], PI3[:, j, :], PR3[:, 1, :], PI3[:, 1, :], ["PR", "PI"], ["PR", "PI"])
    vop(CP(QR3[:, 1, :], ivr), ["ivr", "QR"], ["QR"]); vop(CP(QI3[:, 1, :], ivi), ["ivi", "QI"], ["QI"])
    for j in range(1, 7):
        cmul(QR3[:, j + 1, :], QI3[:, j + 1, :], QR3[:, j, :], QI3[:, j, :], ivr, ivi, ["QR", "QI", "ivr", "ivi"], ["QR", "QI"])
    nr = tmp(16); ni = tmp(16); den = tmp(16); lb1 = tmp(16); cr = tmp(16); ci = tmp(16)
    vop(TS(lb1, PR3[:, 1, :], -1.0, None, ALU.add), ["PR"], ["lb1"])
    vop(TT(t1, lb1, LRE, ALU.mult), ["lb1", "LRE"], ["t1"]); vop(TT(t2, PI3[:, 1, :], LIM, ALU.mult), ["PI", "LIM"], ["t2"])
    vop(TT(nr, t1, t2, ALU.add), ["t1", "t2"], ["nr"])
    vop(TT(t1, PI3[:, 1, :], LRE, ALU.mult), ["PI", "LRE", "t1"], ["t1"]); vop(TT(t2, lb1, LIM, ALU.mult), ["lb1", "LIM", "t2"], ["t2"])
    vop(TT(ni, t1, t2, ALU.subtract), ["t1", "t2"], ["ni"])
    vop(TT(t1, LRE, LRE, ALU.mult), ["LRE", "t1"], ["t1"]); vop(TT(t2, LIM, LIM, ALU.mult), ["LIM", "t2"], ["t2"])
    vop(TT(den, t1, t2, ALU.add), ["t1", "t2"], ["den"])
    vop(lambda e: e.reciprocal(out=den, in_=den), ["den"], ["den"])
    vop(TT(cr, nr, den, ALU.mult), ["nr", "den"], ["cr"]); vop(TT(ci, ni, den, ALU.mult), ["ni", "den"], ["ci"])
    BBR = tmp(256); BBI = tmp(256); u1 = tmp(256); u2 = tmp(256)
    v3 = lambda a: a.rearrange("p (g c) -> p g c", g=16)
    b16 = lambda a: bc(a, 2, 16)

    def cmul3(outr, outi, sr, si, xr, xi_, rk, wk, neg_im=False):
        vop(TT(v3(u1), v3(xr), b16(sr), ALU.mult), rk, ["u1"]); vop(TT(v3(u2), v3(xi_), b16(si), ALU.mult), rk, ["u2"])
        vop(TT(outr, v3(u1), v3(u2), ALU.subtract), ["u1", "u2"] + wk, wk)
        vop(TT(v3(u1), v3(xi_), b16(sr), ALU.mult), rk + ["u1"], ["u1"]); vop(TT(v3(u2), v3(xr), b16(si), ALU.mult), rk + ["u2"], ["u2"])
        if neg_im:
            vop(STT(outi, v3(u1), -1.0, v3(u2), ALU.mult, ALU.subtract), ["u1", "u2"] + wk, wk)
        else:
            vop(TT(outi, v3(u1), v3(u2), ALU.add), ["u1", "u2"] + wk, wk)

    cmul3(v3(BBR), v3(BBI), cr, ci, BRE, BIM, ["cr", "ci", "BRE", "BIM"], ["BB"])
    EN = ar.bf(16 * 256); EQ = tmp(16 * 256); G = tmp(16 * 2 * 9 * 16)
    EN5 = EN.rearrange("p (g r s c) -> p g r s c", g=16, r=2, s=8)
    EQ5 = EQ.rearrange("p (g r s c) -> p g r s c", g=16, r=2, s=8)
    G5 = G.rearrange("p (g r j c) -> p g r j c", g=16, r=2, j=9)
    for s_ in range(8):
        cmul3(EN5[:, :, 0, s_, :], EN5[:, :, 1, s_, :], PR3[:, 7 - s_, :], PI3[:, 7 - s_, :], BBR, BBI, ["PR", "PI", "BB"], ["EN"])
        cmul3(EQ5[:, :, 0, s_, :], EQ5[:, :, 1, s_, :], QR3[:, s_, :], QI3[:, s_, :], BBR, BBI, ["QR", "QI", "BB"], ["EQ"])
    for j in range(9):
        cmul3(G5[:, :, 0, j, :], G5[:, :, 1, j, :], PR3[:, j, :], PI3[:, j, :], CRE, CIM, ["PR", "PI", "CRE", "CIM"], ["G"], neg_im=True)
    for ri in range(2):
        vop(CP(Wcr4[:, ri, :, :].rearrange("p g (s c) -> p g s c", s=8), G5[:, :, ri, 1:9, :]), ["G"], ["Wcr"])
    Wst5 = Wst.rearrange("p (g a r q) -> p g a r q", g=16, a=2, r=2)
    ENb5 = EN5
    ptbs = [PSt[5].bitcast(BF16), PSt[4].bitcast(BF16)]
    for g2 in range(16):
        for ri in range(2):
            k = ri
            S.op(PE, TR(ptbs[k][:, 0:128], ENb5[:, g2, ri, :, :].rearrange("p s c -> p (s c)"), identb), reads=["EN", "identb"], writes=[f"psb{k}"])
            S.op(AC, ACT(Wst5[:, g2, :, ri, :], ptbs[k][:, 0:128].rearrange("p (a q) -> p a q", a=2), AF.Copy),
                 reads=[f"psb{k}"], writes=["Wst"])
    si_ = ar.i32(128); sf_ = tmp(128); ri_ = ar.i32(1); rf_ = tmp(1); BLK = tmp(128); wtmp = [tmp(128), tmp(128)]
    S.op(PL, lambda e: e.iota(si_, pattern=[[1, 128]], base=0, channel_multiplier=0), writes=["si_"])
    vop(CP(sf_, si_), ["si_"], ["sf_"])
    vop(TS(sf_, sf_, 1.0 / 16, -0.46875, ALU.mult, ALU.add), ["sf_"], ["sf_"])
    vop(CP(si_, sf_), ["sf_"], ["si_"])
    vop(CP(sf_, si_), ["si_"], ["sf_"])
    vop(TS(rf_, pidx, 1.0 / 16, -0.46875, ALU.mult, ALU.add), ["pidx"], ["rf_"])
    vop(CP(ri_, rf_), ["rf_"], ["ri_"])
    vop(CP(rf_, ri_), ["ri_"], ["rf_"])
    vop(TS(BLK, sf_, rf_[:, 0:1], None, ALU.is_ge), ["sf_", "rf_"], ["BLK"])
    for g in range(32):
        par, g2 = g % 2, g // 2
        ps_ = slice(64 * par, 64 * par + 64)
        k = 2 + g % 2
        for ri in range(2):
            S.op(PE, MM(PSt[k][:, 0:128], EQ5[ps_, g2, ri, :, :].rearrange("p s c -> p (s c)"),
                        G5[ps_, g2, ri, 0:8, :].rearrange("p s c -> p (s c)"), ri == 0, ri == 1),
                 reads=["EQ", "G"], writes=[f"ps{k}"])
        S.op(DV, TT(wtmp[g % 2], PSt[k][:, 0:128], BLK, ALU.mult), reads=[f"ps{k}", "BLK"], writes=[f"wtmp{g % 2}"])
        S.op(DV, STT(Wintra3[:, g, :], identf, Dcol[:, g:g + 1], wtmp[g % 2], ALU.mult, ALU.add), reads=[f"wtmp{g % 2}", "identf", "Dcol"], writes=["Wintra"])
    ur = tmp(16); ui = tmp(16); wr = tmp(16); wi = tmp(16); w2r = tmp(16); w2i = tmp(16); a8 = tmp(16)
    vop(TS(a8, a_, 8.0, None, ALU.mult), ["a_"], ["a8"])
    S.op(AC, ACT(R8, a8, AF.Exp), reads=["a8"], writes=["R8"])
    S.op(AC, ACT(a8, a8, AF.Exp, scale=-1.0), reads=["a8"], writes=["a8"])
    vop(TT(ur, PR3[:, 8, :], a8, ALU.mult), ["PR", "a8"], ["ur"]); vop(TT(ui, PI3[:, 8, :], a8, ALU.mult), ["PI", "a8"], ["ui"])
    vop(CP(COS3[:, :, 0], ur), ["ur"], ["COS"]); vop(CP(SIN3[:, :, 0], ui), ["ui"], ["SIN"])
    vop(CP(wr, ur), ["ur"], ["wr"]); vop(CP(wi, ui), ["ui"], ["wi"])
    e1 = tmp(16 * 32); e2 = tmp(16 * 32)
    for k in range(6):
        n = 1 << k
        e1v = e1[:, 0:16 * n].rearrange("p (g m) -> p g m", g=16); e2v = e2[:, 0:16 * n].rearrange("p (g m) -> p g m", g=16)
        wrb = bc(wr, 2, n); wib = bc(wi, 2, n)
        vop(TT(e1v, COS3[:, :, 0:n], wrb, ALU.mult), ["COS", "wr"], ["e1"]); vop(TT(e2v, SIN3[:, :, 0:n], wib, ALU.mult), ["SIN", "wi"], ["e2"])
        vop(TT(COS3[:, :, n:2 * n], e1v, e2v, ALU.subtract), ["e1", "e2", "COS"], ["COS"])
        vop(TT(e1v, COS3[:, :, 0:n], wib, ALU.mult), ["COS", "wi", "e1"], ["e1"]); vop(TT(e2v, SIN3[:, :, 0:n], wrb, ALU.mult), ["SIN", "wr", "e2"], ["e2"])
        vop(TT(SIN3[:, :, n:2 * n], e1v, e2v, ALU.add), ["e1", "e2", "SIN"], ["SIN"])
        if k < 5:
            cmul(w2r, w2i, wr, wi, wr, wi, ["wr", "wi"], ["w2"])
            vop(CP(wr, w2r), ["w2"], ["wr"]); vop(CP(wi, w2i), ["w2"], ["wi"])
    DBG.update(ur=ur, ui=ui, a8=a8, wr=wr, wi=wi, PR=PR, PI=PI, e1=e1, e2=e2)
    DBG.update(Wintra=Wintra, Wst=Wst, Wcr=Wcr, COS=COS, SIN=SIN, R8=R8, MASKH=MASKH, Win=Win, zcol=zcol, xi=xi, gpre=gpre, invf=invf, identb=identb, GPOST=GPOST)
    S.op(PL, MS(mhalf, -0.5), writes=["mhalf"])
    S.op(PL, MS(Rst, 0.0), writes=["R"]); S.op(PL, MS(Rb, 0.0), writes=["Rb0"]); S.op(PL, MS(Rb2, 0.0), writes=["Rb1"]); S.op(PL, MS(CAR, 0.0), writes=["CAR"])
    S.barrier()
    ar.top = P_BASE
    EPS = 1e-6
    pt_t = PSt[4]
    ptb = pt_t.bitcast(BF16)
    pss, pso, psu = PSt[5], PSt[6], PSt[7]
    mA = ar.top
    hT = ar.bf(8 * 1024); hT3 = hT.rearrange("p (k t) -> p k t", k=8)
    xs = [ar.f32(1024), ar.f32(1024), ar.f32(1024)]
    hb = [ar.bf(1024), ar.bf(1024)]
    junk = ar.bf(1024)
    ssq = ar.f32(8); rstd = ar.f32(8)
    rcos = ar.f32(512); rsin = ar.f32(512); rang = ar.f32(512); rk = ar.f32(512); rki = ar.i32(512); rm = ar.f32(512); posf = ar.f32(8)
    rcos3 = rcos.rearrange("p (c j) -> p c j", c=8); rsin3 = rsin.rearrange("p (c j) -> p c j", c=8)
    qkr = [ar.bf(1024), ar.bf(1024)]
    qkT = [ar.bf(1024), ar.bf(1024)]
    vb = [ar.bf(512), ar.bf(512)]; vz = [ar.bf(512), ar.bf(512)]; sTb = ar.bf(512)
    h4 = lambda a: a.rearrange("p (h d) -> p h d", h=4)
    a8v = lambda a: a.rearrange("p (a d) -> p a d", a=8)
    sTb3 = h4(sTb)
    rt = [ar.f32(512) for _ in range(4)]
    yn = ar.f32(512); sgg = [ar.f32(512), ar.f32(512)]; yr = ar.bf(512)
    bst = ar.f32(24); mv = ar.f32(8); rs = ar.f32(4); vtmp = ar.f32(4)
    mR_end = ar.top
    ar.top = mA
    U8 = ar.bf(4096); U83 = U8.rearrange("p (g n) -> p g n", g=32)
    Y8 = ar.bf(4096); Y83 = Y8.rearrange("p (g n) -> p g n", g=32)
    SPV = ar.bf(2 * 16 * 128); SPV4 = SPV.rearrange("p (r g n) -> p r g n", r=2, g=16)
    PRE = [[ar.f32(256), ar.f32(256)] for _ in range(2)]; SCN = [[ar.f32(256), ar.f32(256)] for _ in range(2)]
    st_ = [[ar.f32(256) for _ in range(4)] for _ in range(2)]
    FUL = [[ar.f32(256), ar.f32(256)] for _ in range(2)]
    ysT = [ar.bf(512), ar.bf(512)]; ys2 = [ar.bf(512), ar.bf(512)]; ys2T = [ar.bf(512), ar.bf(512)]
    sig = ar.f32(512); xs2 = [ar.f32(1024), ar.f32(1024)]; tm = [ar.f32(1024), ar.f32(1024)]
    junk2 = ar.bf(1024); ss2 = [ar.f32(2), ar.f32(2)]; rs2 = [ar.f32(2), ar.f32(2)]
    mS_end = ar.top
    assert max(mR_end, mS_end) <= AW, (mR_end, mS_end)
    v34 = lambda a: a.rearrange("p (g m) -> p g m", g=4)
    Tb4 = Tb.rearrange("p (g s c) -> p g s c", g=32, s=8)
    YT4 = Tb.rearrange("p (s g c) -> p s g c", s=8, g=32)
    YT3 = Tb.rearrange("p (s f) -> p s f", s=8)
    Rbs = [Rb, Rb2]

    mh = [mhalf]

    def rsqrt_cols(dst, src, scale, nm_src, nm_dst):
        n = dst.shape[1]
        S.op(DV, TS(dst, src, scale, EPS, ALU.mult, ALU.add), reads=[nm_src], writes=[nm_dst])
        S.op(PL, TT(dst, dst, mh[0][:, 0:n], ALU.pow), reads=[nm_dst, "mhalf"], writes=[nm_dst])

    rcnt = [0]
    for kt in range(4):
        S.dma(DMA(Wglu3[:, kt, :], wg_v[:, kt, :], max_dma_last_dim=4096), q="pool", writes=["Wglu"])
    for kt in range(8):
        S.dma(DMA(Wout3[:, kt, :], wo_v[:, kt, :], max_dma_last_dim=4096), q="pool", writes=["Wout"])
    for sbi in range((NPRE + NMAIN) if STOP is None else int(STOP)):
        pre = sbi < NPRE
        xsrc = x_pre if pre else x_own
        row0 = 1024 * (sbi if pre else sbi - NPRE)
        pcol = 0 if pre else 1
        xv = xsrc[row0:row0 + 1024, :].rearrange("(n s) d -> n s d", s=8)
        S.op(DV, TS(posf, iopc, posc[:, pcol:pcol + 1], float(row0), ALU.add, ALU.add), reads=["iopc", "posc", "rang"], writes=["posf"])
        rang3 = rang.rearrange("p (c j) -> p c j", c=8)
        S.op(DV, TT(rang3, bc(posf, 2, 64), bc(invf, 1, 8), ALU.mult), reads=["posf", "invf", "rcos"], writes=["rang"])
        S.op(DV, TS(rk, rang, 1.0 / TWO_PI, None, ALU.mult), reads=["rang"], writes=["rk"])
        S.op(DV, CP(rki, rk), reads=["rk"], writes=["rki"])
        S.op(DV, CP(rk, rki), reads=["rki"], writes=["rk"])
        S.op(DV, STT(rm, rk, -C1, rang, ALU.mult, ALU.add), reads=["rk", "rang"], writes=["rm"])
        S.op(DV, STT(rm, rk, -C2, rm, ALU.mult, ALU.add), reads=["rk", "rm"], writes=["rm"])
        S.op(DV, TS(rm, rm, math.pi, -math.pi, ALU.min, ALU.max), reads=["rm"], writes=["rm"])
        S.op(AC, ACT(rsin, rm, AF.Sin), reads=["rm"], writes=["rsin"])
        S.op(DV, TS(rk, rm, -1.0, None, ALU.mult), reads=["rm", "rk"], writes=["rk"])
        S.op(DV, TT(rk, rk, rm, ALU.max), reads=["rm", "rk"], writes=["rk"])
        S.op(DV, TS(rk, rk, -1.0, math.pi / 2, ALU.mult, ALU.add), reads=["rk"], writes=["rk"])
        S.op(AC, ACT(rcos, rk, AF.Sin), reads=["rk"], writes=["rcos"])

        def ht1(s):
            b = s % 3
            S.dma(DMA(xs[b], xsrc[row0 + 128 * s:row0 + 128 * s + 128, :]), writes=[f"xs{b}"])
            S.op(AC, ACT(junk, xs[b], AF.Square, accum_out=ssq[:, s:s + 1]), reads=[f"xs{b}"], writes=["junk", f"ssq{s}"])

        def ht2(s):
            b = s % 2; bx = s % 3
            rsqrt_cols(rstd[:, s:s + 1], ssq[:, s:s + 1], 1.0 / 1024, f"ssq{s}", f"rstd{s}")
            S.op(AC, ACT(hb[b], xs[bx], AF.Copy, scale=rstd[:, s:s + 1]), reads=[f"xs{bx}", f"rstd{s}"], writes=[f"hb{b}"])
            tbk, tnm = (ptb, "pt") if s % 2 == 0 else (PSt[5].bitcast(BF16), "pss")
            for kt in range(8):
                S.op(PE, TR(tbk[:, kt * 128:(kt + 1) * 128], hb[b][:, kt * 128:(kt + 1) * 128], identb), reads=[f"hb{b}", "identb"], writes=[tnm])
            S.op(DV, TT(hT3[:, :, 128 * s:128 * s + 128], tbk.rearrange("p (k n) -> p k n", k=8), bc(gpre, 2, 128), ALU.mult), reads=[tnm, "gpre"], writes=["hT"])

        if os.environ.get("KSEQ_H"):
            for s in range(8):
                ht1(s); ht2(s)
        else:
            ht1(0); ht1(1)
            for s in range(8):
                if s + 2 < 8:
                    ht1(s + 2)
                ht2(s)

        tkc = lambda c: slice(128 * c, 128 * c + 128)
        if pre:
            def A(c):
                bk, bv = (PSt[0], PSt[1]) if c % 2 == 0 else (PSt[2], PSt[3])
                nk, nv = ("ps0", "ps1") if c % 2 == 0 else ("ps2", "ps3")
                for (bank, col0, nm) in ((bk, 512, nk), (bv, 1024, nv)):
                    for kt in range(8):
                        S.op(PE, MM(bank[:, :], hT3[:, kt, tkc(c)], Win3[:, kt, col0:col0 + 512], kt == 0, kt == 7), reads=["hT", "Win"], writes=[nm])

            def B(c):
                b = c % 2
                bk, bv = (PSt[0], PSt[1]) if c % 2 == 0 else (PSt[2], PSt[3])
                nk, nv = ("ps0", "ps1") if c % 2 == 0 else ("ps2", "ps3")
                cosb = bc(rcos3[:, c, :], 1, 4); sinb = bc(rsin3[:, c, :], 1, 4)
                xq = bk[:, :].rearrange("p (h a j) -> p h a j", h=4, a=2)
                x1 = xq[:, :, 0, :]; x2 = xq[:, :, 1, :]
                r3 = [t_[:, 0:256].rearrange("p (h j) -> p h j", h=4) for t_ in rt]
                S.op(DV, TT(r3[0], x1, cosb, ALU.mult), reads=[nk, "rcos"], writes=["rt0"])
                S.op(DV, TT(r3[1], x2, sinb, ALU.mult), reads=[nk, "rsin"], writes=["rt1"])
                S.op(DV, TT(r3[2], x1, sinb, ALU.mult), reads=[nk, "rsin"], writes=["rt2"])
                S.op(DV, TT(r3[3], x2, cosb, ALU.mult), reads=[nk, "rcos"], writes=["rt3"])
                q4 = qkr[b].rearrange("p (a h j) -> p a h j", a=8, h=2)
                S.op(PL, TT(q4[:, 4:8, 0, :], r3[0], r3[1], ALU.subtract), reads=["rt0", "rt1"], writes=[f"qkr{b}"])
                S.op(PL, TT(q4[:, 4:8, 1, :], r3[2], r3[3], ALU.add), reads=["rt2", "rt3"], writes=[f"qkr{b}"])
                S.op(DV, TT(h4(vz[b]), h4(bv[:, :]), bc(zcol, 2, 128), ALU.mult), reads=[nv, "zcol"], writes=[f"vz{b}"])

            def ST(c):
                b = c % 2
                psu3 = h4(psu[:, :])
                for h in range(4):
                    S.op(PE, MM(psu3[:, h, :], a8v(qkr[b])[:, 4 + h, :], h4(vz[b])[:, h, :], True, True), reads=[f"qkr{b}", f"vz{b}"], writes=["psu"])
                for h in range(4):
                    S.op(DV, STT(Rst3[:, h, :], Rst3[:, h, :], GHEAD[h], psu3[:, h, :], ALU.mult, ALU.add), reads=["psu", "R"], writes=["R"])

            A(0)
            for c in range(8):
                if c + 1 < 8:
                    A(c + 1)
                B(c)
                ST(c)
            nb = rcnt[0] % 2
            S.op(AC, ACT(Rbs[nb], Rst, AF.Copy), reads=["R"], writes=[f"Rb{nb}"])
        else:

            def A1(c):
                for (col0, off) in ((0, 0), (512, 512)):
                    for kt in range(8):
                        S.op(PE, MM(PSt[off // 512][:, :], hT3[:, kt, tkc(c)], Win3[:, kt, col0:col0 + 512], kt == 0, kt == 7), reads=["hT", "Win"], writes=["ps0" if off == 0 else "ps1"])

            def A2(c):
                for (col0, off) in ((1024, 0), (1536, 512)):
                    for kt in range(8):
                        S.op(PE, MM(PSt[2 + off // 512][:, :], hT3[:, kt, tkc(c)], Win3[:, kt, col0:col0 + 512], kt == 0, kt == 7), reads=["hT", "Win"], writes=["ps2" if off == 0 else "ps3"])

            def B1(c):
                b = c % 2
                cosb = bc(rcos3[:, c, :], 1, 4); sinb = bc(rsin3[:, c, :], 1, 4)
                q4 = qkr[b].rearrange("p (a t j) -> p a t j", a=8, t=2)
                for half in range(2):
                    nm = "ps0" if half == 0 else "ps1"
                    xq = PSt[half][:, :].rearrange("p (a t j) -> p a t j", a=4, t=2)
                    x1 = xq[:, :, 0, :]; x2 = xq[:, :, 1, :]
                    r3 = [t_[:, 256 * half:256 * half + 256].rearrange("p (a j) -> p a j", a=4) for t_ in rt]
                    S.op(DV, TT(r3[0], x1, cosb, ALU.mult), reads=[nm, "rcos"], writes=[f"rt0{half}"])
                    S.op(DV, TT(r3[1], x2, sinb, ALU.mult), reads=[nm, "rsin"], writes=[f"rt1{half}"])
                    S.op(DV, TT(r3[2], x1, sinb, ALU.mult), reads=[nm, "rsin"], writes=[f"rt2{half}"])
                    S.op(DV, TT(r3[3], x2, cosb, ALU.mult), reads=[nm, "rcos"], writes=[f"rt3{half}"])
                    S.op(PL, TT(q4[:, 4 * half:4 * half + 4, 0, :], r3[0], r3[1], ALU.subtract), reads=[f"rt0{half}", f"rt1{half}"], writes=[f"qkr{b}"])
                    S.op(PL, TT(q4[:, 4 * half:4 * half + 4, 1, :], r3[2], r3[3], ALU.add), reads=[f"rt2{half}", f"rt3{half}"], writes=[f"qkr{b}"])

            def B2(c):
                b = c % 2
                for h in range(4):
                    S.op(AC, ACT(h4(vz[b])[:, h, :], h4(PSt[2][:, :])[:, h, :], AF.Copy, scale=zcol[:, h:h + 1]), reads=["ps2", "zcol"], writes=[f"vz{b}"])
                S.op(AC, ACT(vb[b], PSt[2][:, :], AF.Copy), reads=["ps2"], writes=[f"vb{b}"])
                S.op(AC, ACT(sgg[b], PSt[3][:, :], AF.Silu), reads=["ps3"], writes=[f"sgg{b}"])
                S.op(PL, TT(sgg[b], sgg[b], GGN, ALU.mult), reads=[f"sgg{b}", "GGN"], writes=[f"sgg{b}"])

            def Tqk(c):
                b = c % 2
                for a in range(8):
                    S.op(PE, TR(ptb[:, a * 128:(a + 1) * 128], a8v(qkr[b])[:, a, :], identb), reads=[f"qkr{b}", "identb"], writes=["pt"])
                S.op(AC, ACT(qkT[b], ptb[:, :], AF.Copy), reads=["pt"], writes=[f"qkT{b}"])

            def SC(c):
                b = c % 2
                pss3 = h4(pss[:, :]); qT = a8v(qkT[b])
                for h in range(4):
                    S.op(PE, MM(pss3[:, h, :], qT[:, 4 + h, :], qT[:, h, :], True, True), reads=[f"qkT{b}"], writes=["pss"])
                S.op(DV, TT(sTb3, pss3, MASKH3, ALU.mult), reads=["pss", "MASKH"], writes=["sTb"])

            def ST(c):
                b = c % 2
                psu3 = h4(psu[:, :])
                for h in range(4):
                    S.op(PE, MM(psu3[:, h, :], a8v(qkr[b])[:, 4 + h, :], h4(vz[b])[:, h, :], True, True), reads=[f"qkr{b}", f"vz{b}"], writes=["psu"])
                for h in range(4):
                    S.op(DV, STT(Rst3[:, h, :], Rst3[:, h, :], GHEAD[h], psu3[:, h, :], ALU.mult, ALU.add), reads=["psu", "R"], writes=["R"])
                nb = (rcnt[0] + c + 1) % 2
                S.op(AC, ACT(Rbs[nb], Rst, AF.Copy), reads=["R"], writes=[f"Rb{nb}"])

            def OUT(c):
                b = c % 2
                cb = (rcnt[0] + c) % 2
                pso3 = h4(pso[:, :]); qT = a8v(qkT[b])
                for h in range(4):
                    S.op(PE, MM(pso3[:, h, :], sTb3[:, h, :], h4(vb[b])[:, h, :], True, False), reads=["sTb", f"vb{b}"], writes=["pso"])
                    S.op(PE, MM(pso3[:, h, :], qT[:, h, :], h4(Rbs[cb])[:, h, :], False, True), reads=[f"qkT{b}", f"Rb{cb}"], writes=["pso"])

            def GN(c):
                b = c % 2
                pso3 = h4(pso[:, :])
                bst3 = bst.rearrange("p (h k) -> p h k", h=4); mv3 = mv.rearrange("p (h k) -> p h k", h=4)
                for h in range(4):
                    S.op(DV, lambda e, h=h: e.bn_stats(out=bst3[:, h, :], in_=pso3[:, h, :]), reads=["pso"], writes=["bst"])
                    S.op(DV, lambda e, h=h: e.bn_aggr(out=mv3[:, h, :], in_=bst3[:, h, :]), reads=["bst"], writes=["mv"])
                S.op(DV, TT(vtmp, mv3[:, :, 1], xi2, ALU.mult), reads=["mv", "xi2"], writes=["vtmp"])
                rsqrt_cols(rs, vtmp, 1.0, "vtmp", "rs")
                S.op(DV, TT(rs, rs, xi, ALU.mult), reads=["rs", "xi"], writes=["rs"])
                yn3 = h4(yn)
                S.op(DV, STT(vtmp, mv3[:, :, 0], -1.0, rs, ALU.mult, ALU.mult), reads=["mv", "rs", "vtmp"], writes=["vtmp"])
                for h in range(4):
                    S.op(AC, ACT(yn3[:, h, :], pso3[:, h, :], AF.Identity, scale=rs[:, h:h + 1], bias=vtmp[:, h:h + 1]), reads=["pso", "vtmp", "rs"], writes=["yn"])
                S.op(PL, TT(yr, yn, sgg[b], ALU.mult), reads=["yn", f"sgg{b}"], writes=["yr"])

            def Tyr(c):
                for h in range(4):
                    S.op(PE, TR(ptb[:, h * 128:(h + 1) * 128], yr[:, h * 128:(h + 1) * 128], identb), reads=["yr", "identb"], writes=["pt"])
                S.op(AC, ACT(yrT3[:, :, tkc(c)], ptb[:, 0:512].rearrange("p (h i) -> p h i", h=4), AF.Copy), reads=["pt"], writes=["yrT"])

            if os.environ.get("KSEQ_R"):
                for c in range(8):
                    A1(c); A2(c); B1(c); B2(c); Tqk(c); SC(c); ST(c); OUT(c); GN(c); Tyr(c)
            else:
                A1(0); A2(0); B1(0); B2(0)
                for c in range(8):
                    Tqk(c)
                    if c + 1 < 8:
                        A1(c + 1)
                    if c > 0:
                        Tyr(c - 1)
                    SC(c); ST(c)
                    if c + 1 < 8:
                        B1(c + 1)
                        A2(c + 1)
                    OUT(c)
                    if c + 1 < 8:
                        B2(c + 1)
                    GN(c)
                Tyr(7)
        rcnt[0] += (0 if pre else 8)
        for s in range(8):
            bank, nm = (pss, "pss") if s % 2 == 0 else (pso, "pso")
            for kt in range(8):
                S.op(PE, MM(bank[:, :], hT3[:, kt, s::8], Win3[:, kt, 2048:2560], kt == 0, kt == 7), reads=["hT", "Win"], writes=[nm])
            S.op(AC, ACT(Tb4[:, :, s, :], bank[:, :].rearrange("p (g c) -> p g c", g=32), AF.Copy), reads=[nm], writes=["T"])
        S.barrier()
        Tflat = Tb.rearrange("p (g f) -> p g f", g=32)
        ptb2 = PSt[5].bitcast(BF16)
        for gq in range(4):
            tb_, nm = (ptb, "pt") if gq % 2 == 0 else (ptb2, "pss")
            for j in range(8):
                S.op(PE, TR(tb_[:, j * 128:(j + 1) * 128], Tflat[:, 8 * gq + j, :], identb), reads=["T", "identb"], writes=[nm])
            S.op(AC if gq % 2 else DV, (ACT(U83[:, 8 * gq:8 * gq + 8, :], tb_.rearrange("p (g n) -> p g n", g=8), AF.Copy) if gq % 2 else
                                        CP(U83[:, 8 * gq:8 * gq + 8, :], tb_.rearrange("p (g n) -> p g n", g=8))), reads=[nm], writes=["U8"])
        its = [(hf, gb) for hf in range(2) for gb in range(4)]

        def Mm(it):
            hf, gb = its[it]; p = it % 2
            nsl = slice(64 * hf, 64 * hf + 64)
            psr3 = PSt[2 * p][:, 0:256].rearrange("p (g m) -> p g m", g=4); psi3 = PSt[2 * p + 1][:, 0:256].rearrange("p (g m) -> p g m", g=4)
            for j in range(8):
                g = 8 * gb + j; par = j % 2; slot = j // 2
                ps_ = slice(64 * par, 64 * par + 64)
                S.op(PE, MM(psr3[ps_, slot, :], Wst4[:, g, 0, :], U83[:, g, nsl], True, True), reads=["U8", "Wst"], writes=[f"ps{2 * p}"])
                S.op(PE, MM(psi3[ps_, slot, :], Wst4[:, g, 1, :], U83[:, g, nsl], True, True), reads=["U8", "Wst"], writes=[f"ps{2 * p + 1}"])

        def R1(it):
            hf, gb = its[it]; p = it % 2
            g2s = slice(4 * gb, 4 * gb + 4)
            psr3 = PSt[2 * p][:, 0:256].rearrange("p (g m) -> p g m", g=4); psi3 = PSt[2 * p + 1][:, 0:256].rearrange("p (g m) -> p g m", g=4)
            cosv = COS3[:, g2s, :]; sinv = SIN3[:, g2s, :]
            nr_, ni_ = f"ps{2 * p}", f"ps{2 * p + 1}"
            S.op(DV, TT(v34(st_[p][0]), psr3, cosv, ALU.mult), reads=[nr_, "COS"], writes=[f"st{p}0"])
            S.op(DV, TT(v34(st_[p][1]), psi3, sinv, ALU.mult), reads=[ni_, "SIN"], writes=[f"st{p}1"])
            S.op(DV, TT(v34(st_[p][2]), psi3, cosv, ALU.mult), reads=[ni_, "COS"], writes=[f"st{p}2"])
            S.op(DV, TT(v34(st_[p][3]), psr3, sinv, ALU.mult), reads=[nr_, "SIN"], writes=[f"st{p}3"])
            S.op(PL, TT(PRE[p][0], st_[p][0], st_[p][1], ALU.add), reads=[f"st{p}0", f"st{p}1"], writes=[f"PRE{p}0"])
            S.op(PL, TT(PRE[p][1], st_[p][2], st_[p][3], ALU.subtract), reads=[f"st{p}2", f"st{p}3"], writes=[f"PRE{p}1"])

        def SCAN(it):
            hf, gb = its[it]; p = it % 2
            g2s = slice(4 * gb, 4 * gb + 4)
            if not pre:
                for ri in range(2):
                    S.op(AC, ACT(SPV4[:, ri, g2s, 64 * hf:64 * hf + 1], CAR3[:, ri, g2s].unsqueeze(2), AF.Copy), reads=[f"CAR{gb}"], writes=["SPV"])
            for ri in range(2):
                for slot in range(4):
                    g2 = 4 * gb + slot
                    S.op(DV, lambda e, ri=ri, slot=slot, g2=g2, p=p: e.tensor_tensor_scan(
                        out=v34(SCN[p][ri])[:, slot, :], data0=R8[:, g2:g2 + 1].broadcast_to([128, 64]), data1=v34(PRE[p][ri])[:, slot, :],
                        initial=CAR3[:, ri, g2:g2 + 1], op0=ALU.mult, op1=ALU.add), reads=[f"PRE{p}{ri}", f"CAR{gb}", "R8"], writes=[f"SCN{p}{ri}"])

        def R2(it):
            hf, gb = its[it]; p = it % 2
            g2s = slice(4 * gb, 4 * gb + 4)
            cl = slice(63, 64) if pre else slice(0, 64)
            cosv = COS3[:, g2s, cl]; sinv = SIN3[:, g2s, cl]
            w = lambda a: v34(a)[:, :, cl]
            S.op(DV, TT(w(st_[p][0]), w(SCN[p][0]), cosv, ALU.mult), reads=[f"SCN{p}0", "COS"], writes=[f"st{p}0"])
            S.op(DV, TT(w(st_[p][1]), w(SCN[p][1]), sinv, ALU.mult), reads=[f"SCN{p}1", "SIN"], writes=[f"st{p}1"])
            S.op(DV, TT(w(st_[p][2]), w(SCN[p][1]), cosv, ALU.mult), reads=[f"SCN{p}1", "COS"], writes=[f"st{p}2"])
            S.op(DV, TT(w(st_[p][3]), w(SCN[p][0]), sinv, ALU.mult), reads=[f"SCN{p}0", "SIN"], writes=[f"st{p}3"])
            S.op(PL, TT(w(FUL[p][0]), w(st_[p][0]), w(st_[p][1]), ALU.subtract), reads=[f"st{p}0", f"st{p}1"], writes=[f"FUL{p}0"])
            S.op(PL, TT(w(FUL[p][1]), w(st_[p][2]), w(st_[p][3]), ALU.add), reads=[f"st{p}2", f"st{p}3"], writes=[f"FUL{p}1"])
            for ri in range(2):
                S.op(PL, CP(CAR3[:, ri, g2s], v34(FUL[p][ri])[:, :, 63]), reads=[f"FUL{p}{ri}", f"SCN{p}0", f"SCN{p}1", "SPV"], writes=[f"CAR{gb}"])
                if not pre:
                    S.op(AC, ACT(SPV4[:, ri, g2s, 64 * hf + 1:64 * hf + 64], v34(FUL[p][ri])[:, :, 0:63], AF.Copy), reads=[f"FUL{p}{ri}"], writes=["SPV"])

        if os.environ.get("KSEQ_S"):
            for it in range(8):
                Mm(it); R1(it); SCAN(it); R2(it)
        else:
            Mm(0); Mm(1); R1(0)
            for it in range(8):
                if it + 1 < 8:
                    R1(it + 1)
                SCAN(it)
                R2(it)
                if it + 2 < 8:
                    Mm(it + 2)
        if not pre:
            for gq in range(8):
                py, nm = (PSt[0], "ps0") if gq % 2 == 0 else (PSt[1], "ps1")
                py3 = py[:, :].rearrange("p (g n) -> p g n", g=4)
                for j in range(4):
                    g = 4 * gq + j; par = g % 2; g2 = g // 2
                    ps_ = slice(64 * par, 64 * par + 64)
                    S.op(PE, MM(py3[:, j, :], Wintra3[:, g, :], U83[:, g, :], True, False), reads=["U8", "Wintra"], writes=[nm])
                    S.op(PE, MM(py3[:, j, :], Wcr4[ps_, 0, g2, :], SPV4[ps_, 0, g2, :], False, False), reads=["SPV", "Wcr"], writes=[nm])
                    S.op(PE, MM(py3[:, j, :], Wcr4[ps_, 1, g2, :], SPV4[ps_, 1, g2, :], False, True), reads=["SPV", "Wcr"], writes=[nm])
                S.op(AC, ACT(Y83[:, 4 * gq:4 * gq + 4, :], py3, AF.Gelu_apprx_tanh), reads=[nm], writes=["Y8"])
            for gq in range(4):
                tb_, nm = (ptb, "pt") if gq % 2 == 0 else (ptb2, "pss")
                for j in range(8):
                    S.op(PE, TR(tb_[:, j * 128:(j + 1) * 128], Y83[:, 8 * gq + j, :], identb), reads=["Y8", "identb"], writes=[nm])
                S.op(DV, CP(YT4[:, :, 8 * gq:8 * gq + 8, :].rearrange("p s g c -> p g s c"), tb_.rearrange("p (g s c) -> p g s c", g=8, s=8)), reads=[nm, "U8"], writes=["T"])
            pg0, pg1 = PSt[2], PSt[3]
            pms = [(PSt[0], PSt[1], "ps0", "ps1"), (PSt[6], PSt[7], "pso", "psu")]

            def T1(s):
                b = s % 2
                for kt in range(4):
                    S.op(PE, TR(ptb[:, kt * 128:(kt + 1) * 128], YT3[:, s, kt * 128:(kt + 1) * 128], identb), reads=["T", "identb"], writes=["pt"])
                S.op(AC, ACT(ysT[b], ptb[:, 0:512], AF.Copy), reads=["pt"], writes=[f"ysT{b}"])

            def GLU(s):
                b = s % 2
                ysT3 = ysT[b].rearrange("p (k n) -> p k n", k=4)
                for hfc, bank, nm in ((0, pg0, "ps2"), (1, pg1, "ps3")):
                    for kt in range(4):
                        S.op(PE, MM(bank[:, :], ysT3[:, kt, :], Wglu3[:, kt, hfc * 512:(hfc + 1) * 512], kt == 0, kt == 3), reads=[f"ysT{b}", "Wglu"], writes=[nm])
                S.op(AC, ACT(sig, pg1[:, :], AF.Sigmoid), reads=["ps3"], writes=["sig"])
                S.op(DV, TT(ys2[b], pg0[:, :], sig, ALU.mult), reads=["ps2", "sig"], writes=[f"ys2{b}"])

            def T2(s):
                b = s % 2
                for kt in range(4):
                    S.op(PE, TR(ptb2[:, kt * 128:(kt + 1) * 128], ys2[b][:, kt * 128:(kt + 1) * 128], identb), reads=[f"ys2{b}", "identb"], writes=["pss"])
                S.op(AC, ACT(ys2T[b], ptb2[:, 0:512], AF.Copy), reads=["pss"], writes=[f"ys2T{b}"])

            def WO(s):
                b = s % 2
                pm0, pm1, n0, n1 = pms[b]
                y2T3 = ys2T[b].rearrange("p (k n) -> p k n", k=4)
                for hfc, bank, nm in ((0, pm0, n0), (1, pm1, n1)):
                    for kt in range(8):
                        lhs = yrT3[:, kt, s::8] if kt < 4 else y2T3[:, kt - 4, :]
                        S.op(PE, MM(bank[:, :], lhs, Wout3[:, kt, hfc * 512:(hfc + 1) * 512], kt == 0, kt == 7), reads=["yrT", f"ys2T{b}", "Wout"], writes=[nm])
                S.dma(DMA(xs2[b], xv[:, s, :]), writes=[f"xs2{b}"])
                for hfc, bank, nm in ((0, pm0, n0), (1, pm1, n1)):
                    S.op(AC, ACT(junk2[:, 0:512], bank[:, :], AF.Square, accum_out=ss2[b][:, hfc:hfc + 1]), reads=[nm], writes=["junk2", f"ss2{b}{hfc}"])
                S.op(DV, TT(ss2[b][:, 0:1], ss2[b][:, 0:1], ss2[b][:, 1:2], ALU.add), reads=[f"ss2{b}0", f"ss2{b}1"], writes=[f"ss2{b}0"])
                rsqrt_cols(rs2[b][:, 0:1], ss2[b][:, 0:1], 1.0 / 1024, f"ss2{b}0", f"rs2{b}")
                for hfc, bank, nm in ((0, pm0, n0), (1, pm1, n1)):
                    cs_ = slice(hfc * 512, hfc * 512 + 512)
                    S.op(DV, STT(tm[b][:, cs_], bank[:, :], rs2[b][:, 0:1], GPOST[:, cs_], ALU.mult, ALU.mult), reads=[nm, f"rs2{b}", "GPOST"], writes=[f"tm{b}"])
                S.op(PL, TT(xs2[b], xs2[b], tm[b], ALU.add), reads=[f"tm{b}", f"xs2{b}"], writes=[f"xs2{b}"])
                r0 = 1024 * (sbi - NPRE)
                S.dma(DMA(x1s[r0:r0 + 1024, :].rearrange("(n s) d -> n s d", s=8)[:, s, :], xs2[b]), reads=[f"xs2{b}"], writes=["x1s"])

            if os.environ.get("KSEQ_W"):
                for s in range(8):
                    T1(s); GLU(s); T2(s); WO(s)
            else:
                T1(0); GLU(0); T1(1); T2(0)
                for s in range(1, 8):
                    GLU(s)
                    WO(s - 1)
                    if s + 1 < 8:
                        T1(s + 1)
                    T2(s)
                WO(7)
        S.barrier()

    ar.top = 0
    W1b = ar.bf(8 * 4096); W1b3 = W1b.rearrange("p (k c) -> p k c", k=8)
    W2b = ar.bf(32 * 1024); W2b3 = W2b.rearrange("p (k c) -> p k c", k=32)
    identb2 = ar.bf(128); GPOST2 = ar.f32(1024); g2c = ar.f32(8)
    X1g = ar.f32(4096); X1g3 = X1g.rearrange("p (j d) -> p j d", j=4)
    h2T = ar.bf(8 * 512); h2T3 = h2T.rearrange("p (k t) -> p k t", k=8)
    aT = ar.bf(32 * 512); aT3 = aT.rearrange("p (k t) -> p k t", k=32)
    hb2 = ar.bf(1024); jk = ar.bf(1024); rl = [ar.f32(512), ar.f32(512)]; tmb = ar.f32(1024); sq = ar.f32(4); rq = ar.f32(4); so = ar.f32(2); ro = ar.f32(2)
    mhalf2 = ar.f32(8)
    assert ar.top <= AW, ar.top
    stgB = aT.bitcast(F32) if False else None
    cst = X1g
    S.op(DV, CP(cst[:, 0:8], g2pre), reads=[], writes=["cst"])
    S.op(DV, CP(jk[:, 0:128], identb), reads=[], writes=["jk"])
    S.barrier()
    S.op(DV, CP(g2c, cst[:, 0:8]), reads=["cst"], writes=["g2c"])
    S.op(DV, CP(identb2, jk[:, 0:128]), reads=["jk"], writes=["identb2"])
    S.dma(DMA(GPOST2, dbc(g2post_d)), writes=["GPOST2"])
    S.op(PL, MS(mhalf2, -0.5), writes=["mhalf"])
    mh[0] = mhalf2
    S.barrier()
    w1_v = w1_d.rearrange("(k p) c -> p k c", p=128)
    w2_v = w2_d.rearrange("(k p) c -> p k c", p=128)
    for blk in range(8):
        S.dma(DMA(W1b3[:, :, blk * 512:(blk + 1) * 512], w1_v[:, :, blk * 512:(blk + 1) * 512], max_dma_last_dim=4096), q="pool", writes=[f"W1b{blk}"])
    for k4 in range(8):
        S.dma(DMA(W2b3[:, 4 * k4:4 * k4 + 4, :], w2_v[:, 4 * k4:4 * k4 + 4, :], max_dma_last_dim=4096), q="pool", writes=[f"W2b{k4}"])
    pf = [PSt[0], PSt[1], PSt[2], PSt[3]]
    pmo_pairs = [((PSt[5], PSt[6]), ("pmo0", "pmo1")), ((PSt[7], PSt[4]), ("pm7", "pt"))]
    for gi in range(NT // 512 if STOP is None else 0):
        S.dma(DMA(X1g3, x1s[512 * gi:512 * gi + 512, :].rearrange("(p j) d -> p j d", j=4)), writes=["X1g"])
        for j in range(4):
            S.op(AC, ACT(jk, X1g3[:, j, :], AF.Square, accum_out=sq[:, j:j + 1]), reads=["X1g"], writes=["jk", "sq"])
        rsqrt_cols(rq, sq, 1.0 / 1024, "sq", "rq")
        for j in range(4):
            S.op(AC, ACT(hb2, X1g3[:, j, :], AF.Copy, scale=rq[:, j:j + 1]), reads=["X1g", "rq"], writes=["hb2"])
            for kt in range(8):
                S.op(PE, TR(ptb[:, kt * 128:(kt + 1) * 128], hb2[:, kt * 128:(kt + 1) * 128], identb2), reads=["hb2", "identb2"], writes=["pt"])
            S.op(DV, TT(h2T3[:, :, j * 128:(j + 1) * 128], ptb.rearrange("p (k n) -> p k n", k=8), bc(g2c, 2, 128), ALU.mult), reads=["pt", "g2c"], writes=["h2T"])
        for ft in range(32):
            bk = ft % 4
            for kt in range(8):
                S.op(PE, MM(pf[bk][:, :], W1b3[:, kt, ft * 128:(ft + 1) * 128], h2T3[:, kt, :], kt == 0, kt == 7), reads=["h2T", f"W1b{ft // 4}"], writes=[f"pf{bk}"])
            S.op(AC, ACT(rl[ft % 2], pf[bk][:, :], AF.Relu), reads=[f"pf{bk}"], writes=[f"rl{ft % 2}"])
            S.op(PL if ft % 2 else DV, TT(aT3[:, ft, :], rl[ft % 2], rl[ft % 2], ALU.mult), reads=[f"rl{ft % 2}"], writes=["aT"])
        for j in range(4):
            pmo, pmn = pmo_pairs[j % 2]
            for hfc in range(2):
                for kt in range(32):
                    S.op(PE, MM(pmo[hfc][:, :], aT3[:, kt, j * 128:(j + 1) * 128], W2b3[:, kt, hfc * 512:(hfc + 1) * 512], kt == 0, kt == 31), reads=["aT", f"W2b{kt // 4}"], writes=[pmn[hfc]])
            for hfc in range(2):
                S.op(AC, ACT(jk[:, 0:512], pmo[hfc][:, :], AF.Square, accum_out=so[:, hfc:hfc + 1]), reads=[pmn[hfc]], writes=["jk", f"so{hfc}"])
            S.op(DV, TT(so[:, 0:1], so[:, 0:1], so[:, 1:2], ALU.add), reads=["so0", "so1"], writes=["so0"])
            rsqrt_cols(ro[:, 0:1], so[:, 0:1], 1.0 / 1024, "so0", "ro")
            for hfc in range(2):
                cs_ = slice(hfc * 512, hfc * 512 + 512)
                S.op(DV, STT(tmb[:, cs_], pmo[hfc][:, :], ro[:, 0:1], GPOST2[:, cs_], ALU.mult, ALU.mult), reads=[pmn[hfc], "ro", "GPOST2"], writes=["tmb"])
            S.op(PL, TT(X1g3[:, j, :], X1g3[:, j, :], tmb, ALU.add), reads=["tmb", "X1g"], writes=["X1g"])
        S.dma(DMA(out_d[512 * gi:512 * gi + 512, :].rearrange("(p j) d -> p j d", j=4), X1g3), reads=["X1g"], writes=["out"])
    S.barrier()
    sems = {k: es.enter_context(nc.semaphore(f"s_{k[0]}_{k[1]}")) for k in sorted(S.semkeys)}
    with nc.Block() as block:
        @block.tensor
        def _(e):
            S.replay("pe", e, sems)

        @block.scalar
        def _(e):
            S.replay("act", e, sems)

        @block.vector
        def _(e):
            S.replay("dve", e, sems)

        @block.gpsimd
        def _(e):
            S.replay("pool", e, sems)

        @block.sync
        def _(e):
            S.replay("sp", e, sems)
    es.close()
    return nc


def _run(x, params, NPRE, NMAIN, n_cores, core_plan):
    nc = build(NPRE, NMAIN)
    NT = NMAIN * 1024
    f = lambda a: np.ascontiguousarray(np.asarray(a, dtype=np.float32))
    base = {
        "norm_mix_pre": f(params["norm_mix_pre"]).reshape(8, 128), "norm_mix_post": f(params["norm_mix_post"]).reshape(1, 1024),
        "w_in": f(params["w_in"]).reshape(1024, 2560), "ret_gn_gain": f(params["ret_gn_gain"]).reshape(1, 512),
        "ssm_lambda_re": f(params["ssm_lambda_re"]).reshape(32, 64), "ssm_lambda_im": f(params["ssm_lambda_im"]).reshape(32, 64),
        "ssm_log_dt": f(params["ssm_log_dt"]).reshape(1, 32),
        "ssm_b_re": f(params["ssm_b_re"]).reshape(32, 64, 16), "ssm_b_im": f(params["ssm_b_im"]).reshape(32, 64, 16),
        "ssm_c_re": f(params["ssm_c_re"]).reshape(32, 16, 64), "ssm_c_im": f(params["ssm_c_im"]).reshape(32, 16, 64),
        "ssm_d": f(params["ssm_d"]).reshape(32, 16),
        "w_glu": f(params["w_glu"]).reshape(512, 1024), "w_out": f(params["w_out"]).reshape(1024, 1024),
        "norm_mlp_pre": f(params["norm_mlp_pre"]).reshape(8, 128), "norm_mlp_post": f(params["norm_mlp_post"]).reshape(1, 1024),
        "w_ff1": f(params["w_ff1"]).reshape(1024, 4096), "w_ff2": f(params["w_ff2"]).reshape(4096, 1024),
    }
    in_maps = []
    for (b, st) in core_plan:
        m = dict(base)
        m["x_own"] = f(x[b, st:st + NT])
        if st > 0:
            m["x_pre"] = f(x[b, st - NPRE * 1024:st])
            pb = np.array([st - NPRE * 1024, st], np.float32)
        else:
            m["x_pre"] = np.zeros((max(NPRE, 1) * 1024, 1024), np.float32)
            pb = np.array([0.0, 0.0], np.float32)
        m["posb"] = np.ascontiguousarray(np.broadcast_to(pb[None, :], (128, 2)))
        in_maps.append(m)
    res = run_bass_kernel_spmd(nc, in_maps, core_ids=list(range(n_cores)))
    out = np.zeros(x.shape, np.float32)
    for i, (b, st) in enumerate(core_plan):
        out[b, st:st + NT] = res.results[i]["out"]
    return out


def kernel(x, **params):
    x = np.asarray(x, dtype=np.float32)
    plan = [(b, h * 4096) for b in range(4) for h in range(2)]
    return _run(x, params, 4, 4, 8, plan)
```

```python
import math
import os
STOP = os.environ.get('KSTOP')
from contextlib import ExitStack

import numpy as np
import concourse.bass as bass
import concourse.mybir as mybir
from concourse.bass_utils import run_bass_kernel_spmd

F32 = mybir.dt.float32
BF16 = mybir.dt.bfloat16
I32 = mybir.dt.int32
AF = mybir.ActivationFunctionType
ALU = mybir.AluOpType

ENGS = ("pe", "act", "dve", "pool", "sp")
SAME_ENG_WAITS = os.environ.get('KSAME', '1') == '1'
EPOCH = 30000
NDMA = 8


class Sched:
    def __init__(self):
        self.prog = {e: [] for e in ENGS}
        self.cnt = {e: 0 for e in ENGS}
        self.seen = {e: {} for e in ENGS}
        self.lw = {}
        self.rd = {}
        self.dma_i = {}
        self.dma_val = {}
        self.semkeys = set()

    def _deps(self, reads, writes):
        d = {}

        def add(x):
            if x is None:
                return
            s, v = x
            if d.get(s, 0) < v:
                d[s] = v

        for k in reads:
            add(self.lw.get(k))
        for k in writes:
            add(self.lw.get(k))
            for r in self.rd.get(k, ()):
                add(r)
        return d

    def _emit(self, eng, d, fn, my, inc):
        waits = []
        for s, v in d.items():
            if self.seen[eng].get(s, 0) < v:
                self.seen[eng][s] = v
                if s[0] == eng and (eng == "pe" or not SAME_ENG_WAITS):
                    continue
                waits.append((s, v))
        self.prog[eng].append((waits, fn, my, inc))
        if my is not None:
            self.semkeys.add(my[0])

    def _update(self, reads, writes, my):
        for k in writes:
            self.lw[k] = my
            self.rd[k] = []
        for k in reads:
            self.rd.setdefault(k, []).append(my)

    def op(self, eng, fn, reads=(), writes=()):
        self.nrec = getattr(self, 'nrec', 0) + 1
        if self.nrec > int(os.environ.get('KMAX', '100000000')):
            return
        d = self._deps(reads, writes)
        c = self.cnt[eng]
        self.cnt[eng] = c + 1
        my = ((eng, c // EPOCH), c % EPOCH + 1)
        self._emit(eng, d, fn, my, 1)
        self._update(reads, writes, my)

    def dma(self, fn, reads=(), writes=(), q="sp", slow=False):
        if slow and os.environ.get('KNOSLOW'):
            return
        self.nrec = getattr(self, 'nrec', 0) + 1
        if self.nrec > int(os.environ.get('KMAX', '100000000')):
            return
        d = self._deps(reads, writes)
        i = self.dma_i.get(q, 0)
        self.dma_i[q] = (i + 1) % NDMA
        sk = ("dma_" + q, i)
        pv = self.dma_val.get(sk, 0)
        if pv > 0:
            d[sk] = max(d.get(sk, 0), pv)
        self.dma_val[sk] = pv + 16
        my = (sk, pv + 16)
        self._emit(q, d, fn, my, 16)
        self._update(reads, writes, my)

    def barrier(self):
        allv = {}
        for e in ENGS:
            c = self.cnt[e]
            if c > 0:
                allv[(e, (c - 1) // EPOCH)] = (c - 1) % EPOCH + 1
        for sk, v in self.dma_val.items():
            if v > 0:
                allv[sk] = v
        for e in ENGS:
            waits = []
            for s, v in allv.items():
                if self.seen[e].get(s, 0) < v:
                    self.seen[e][s] = v
                    waits.append((s, v))
            self.prog[e].append((waits, None, None, 0))
        self.lw = {}
        self.rd = {}

    def replay(self, eng, e, sems):
        for waits, fn, my, inc in self.prog[eng]:
            for s, v in waits:
                e.wait_ge(sems[s], v)
            if fn is not None:
                ins = fn(e)
                ins.then_inc(sems[my[0]], inc)


def bc(ap, axis, n):
    a = ap.unsqueeze(axis)
    shp = list(a.shape)
    shp[axis] = n
    return a.broadcast_to(shp)


class Arena:
    def __init__(self, A):
        self.A = A
        self.Ab = A.bitcast(BF16)
        self.Ai = A.bitcast(I32)
        self.top = 0

    def f32(self, n):
        o = self.top
        self.top += n
        return self.A[:, o:o + n]

    def i32(self, n):
        o = self.top
        self.top += n
        return self.Ai[:, o:o + n]

    def bf(self, n):
        o = self.top
        self.top += (n + 1) // 2
        return self.Ab[:, 2 * o:2 * o + n]


LNG = [math.log(1.0 - math.exp(v)) for v in np.linspace(math.log(1.0 / 32), math.log(1.0 / 512), 4)]
GHEAD = [math.exp(128 * v) for v in LNG]
INVF = (np.float32(10000.0) ** (-(np.arange(64, dtype=np.float32) / np.float32(64)))).astype(np.float32)
TWO_PI = 2.0 * math.pi
C1 = 6.28125
C2 = TWO_PI - C1
AW = 52224


DBG = {}


def build(NPRE, NMAIN):
    nc = bass.Bass("TRN2", target_bir_lowering=False)
    NT = NMAIN * 1024
    dr = lambda n, s, dt=F32, kind="ExternalInput": nc.dram_tensor(n, s, dt, kind=kind).ap()
    x_own = dr("x_own", [NT, 1024])
    x_pre = dr("x_pre", [max(NPRE, 1) * 1024, 1024])
    posb = dr("posb", [128, 2])
    g_pre_d = dr("norm_mix_pre", [8, 128]); g_post_d = dr("norm_mix_post", [1, 1024])
    w_in_d = dr("w_in", [1024, 2560]); ggn_d = dr("ret_gn_gain", [1, 512])
    lre_d = dr("ssm_lambda_re", [32, 64]); lim_d = dr("ssm_lambda_im", [32, 64]); ldt_d = dr("ssm_log_dt", [1, 32])
    bre_d = dr("ssm_b_re", [32, 64, 16]); bim_d = dr("ssm_b_im", [32, 64, 16])
    cre_d = dr("ssm_c_re", [32, 16, 64]); cim_d = dr("ssm_c_im", [32, 16, 64]); sd_d = dr("ssm_d", [32, 16])
    w_glu_d = dr("w_glu", [512, 1024]); w_out_d = dr("w_out", [1024, 1024])
    g2pre_d = dr("norm_mlp_pre", [8, 128]); g2post_d = dr("norm_mlp_post", [1, 1024])
    w1_d = dr("w_ff1", [1024, 4096]); w2_d = dr("w_ff2", [4096, 1024])
    out_d = dr("out", [NT, 1024], kind="ExternalOutput")
    x1s = dr("x1s", [NT, 1024], kind="Internal")

    S = Sched()
    es = ExitStack()
    A_t = es.enter_context(nc.sbuf_tensor("arena", [128, AW], F32))
    PSt = [es.enter_context(nc.psum_tensor(f"ps{i}", [128, 512], F32)) for i in range(8)]
    ar = Arena(A_t)

    def psv(i, n=1):
        assert n == 1
        return PSt[i][:, :]

    def dbc(ap1, n=128):
        return bass.AP(ap1.tensor, ap1.offset, [[0, n]] + [list(d) for d in ap1.ap[1:]])

    DV, AC, PL, PE = "dve", "act", "pool", "pe"
    TT = lambda o, a, b, op: (lambda e: e.tensor_tensor(out=o, in0=a, in1=b, op=op))
    TS = lambda o, a, s1, s2, op0, op1=None: (lambda e: e.tensor_scalar(out=o, in0=a, scalar1=s1, scalar2=s2, op0=op0, op1=op1) if op1 is not None
                                              else e.tensor_scalar(out=o, in0=a, scalar1=s1, scalar2=None, op0=op0))
    STT = lambda o, a, s, b, op0, op1: (lambda e: e.scalar_tensor_tensor(out=o, in0=a, scalar=s, in1=b, op0=op0, op1=op1))
    CP = lambda o, a: (lambda e: e.tensor_copy(out=o, in_=a))
    ACT = lambda o, a, f, **kw: (lambda e: e.activation(out=o, in_=a, func=f, **kw))
    MM = lambda o, l, r, st, sp: (lambda e: e.matmul(o, lhsT=l, rhs=r, start=st, stop=sp))
    TR = lambda o, a, idn: (lambda e: e.transpose(o, a, idn))
    DMA = lambda o, a, **kw: (lambda e: e.dma_start(out=o, in_=a, **kw))
    MS = lambda o, v: (lambda e: e.memset(o, v))

    Win = ar.bf(8 * 2560); Win3 = Win.rearrange("p (k c) -> p k c", k=8)
    Wintra = ar.bf(32 * 128); Wintra3 = Wintra.rearrange("p (g c) -> p g c", g=32)
    Wst = ar.bf(32 * 128); Wst4 = Wst.rearrange("p (g r q) -> p g r q", g=32, r=2)
    Wcr = ar.bf(2 * 16 * 128); Wcr4 = Wcr.rearrange("p (r g c) -> p r g c", r=2, g=16)
    COS = ar.f32(16 * 64); COS3 = COS.rearrange("p (g m) -> p g m", g=16)
    SIN = ar.f32(16 * 64); SIN3 = SIN.rearrange("p (g m) -> p g m", g=16)
    R8 = ar.f32(16)
    CAR = ar.f32(32); CAR3 = CAR.rearrange("p (r g) -> p r g", r=2)
    identb = ar.bf(128); identf = ar.f32(128)
    mask01 = ar.f32(128)
    MASKH = ar.f32(512); MASKH3 = MASKH.rearrange("p (h i) -> p h i", h=4)
    zcol = ar.f32(4); xi = ar.f32(4); xi2 = ar.f32(4); pidx = ar.f32(1); pidx1 = ar.f32(1)
    GPOST = ar.f32(1024); GGN = ar.f32(512)
    gpre = ar.f32(8); g2pre = ar.f32(8)
    invf = ar.f32(64)
    iopc = ar.f32(8)
    posc = ar.f32(2)
    Rst = ar.f32(512); Rst3 = Rst.rearrange("p (h d) -> p h d", h=4)
    Rb = ar.bf(512); Rb3 = Rb.rearrange("p (h d) -> p h d", h=4)
    yrT = ar.bf(4 * 1024); yrT3 = yrT.rearrange("p (k t) -> p k t", k=4)
    Tb = ar.bf(4096)
    Rb2 = ar.bf(512)
    Wglu = ar.bf(4096); Wglu3 = Wglu.rearrange("p (k c) -> p k c", k=4)
    Wout = ar.bf(8192); Wout3 = Wout.rearrange("p (k c) -> p k c", k=8)
    mhalf = ar.f32(8)
    P_BASE = ar.top

    m0 = ar.top
    ioi = ar.i32(128); iof = ar.f32(128)
    S.op(PL, lambda e: e.iota(ioi, pattern=[[1, 128]], base=0, channel_multiplier=-1), writes=["ioi"])
    S.op(DV, CP(iof, ioi), reads=["ioi"], writes=["iof"])
    S.op(DV, lambda e: e.tensor_single_scalar(identf, iof, 0.0, op=ALU.is_equal), reads=["iof"], writes=["identf"])
    S.op(DV, CP(identb, identf), reads=["identf"], writes=["identb"])
    S.op(DV, lambda e: e.tensor_single_scalar(mask01, iof, 0.0, op=ALU.is_ge), reads=["iof"], writes=["mask01"])
    pii = ar.i32(1)
    S.op(PL, lambda e: e.iota(pii, pattern=[[0, 1]], base=0, channel_multiplier=1), writes=["pii"])
    S.op(DV, CP(pidx, pii), reads=["pii"], writes=["pidx"])
    S.op(DV, TS(pidx1, pidx, 1.0, None, ALU.add), reads=["pidx"], writes=["pidx1"])
    pm = ar.f32(1)
    S.op(DV, TS(pm, pidx, -1.0, 127.0, ALU.mult, ALU.add), reads=["pidx"], writes=["pm"])
    gcol = ar.f32(4)
    DH = 128.0 ** -0.5
    for h in range(4):
        S.op(AC, ACT(gcol[:, h:h + 1], pidx1, AF.Exp, scale=-LNG[h]), reads=["pidx1"], writes=["gcol"])
        S.op(AC, ACT(zcol[:, h:h + 1], pm, AF.Exp, scale=LNG[h]), reads=["pm"], writes=["zcol"])
        S.op(AC, ACT(xi[:, h:h + 1], pidx1, AF.Exp, scale=LNG[h]), reads=["pidx1"], writes=["xi"])
    S.op(DV, TS(gcol, gcol, DH, None, ALU.mult), reads=["gcol"], writes=["gcol"])
    S.op(DV, TS(zcol, zcol, DH, None, ALU.mult), reads=["zcol"], writes=["zcol"])
    S.op(DV, TT(xi2, xi, xi, ALU.mult), reads=["xi"], writes=["xi2"])
    for h in range(4):
        S.op(DV, TS(MASKH3[:, h, :], mask01, gcol[:, h:h + 1], None, ALU.mult), reads=["mask01", "gcol"], writes=["MASKH"])
    for j in range(64):
        S.op(PL, MS(invf[:, j:j + 1], float(INVF[j])), writes=["invf"])
    ioci = ar.i32(8)
    S.op(PL, lambda e: e.iota(ioci, pattern=[[128, 8]], base=0, channel_multiplier=1), writes=["ioci"])
    S.op(DV, CP(iopc, ioci), reads=["ioci"], writes=["iopc"])
    S.dma(DMA(posc, posb), writes=["posc"])
    S.dma(DMA(GPOST, dbc(g_post_d)), writes=["GPOST"])
    S.dma(DMA(GGN, dbc(ggn_d)), writes=["GGN"])
    g8 = ar.f32(128)
    for (src, dst, nm) in ((g_pre_d, gpre, "gpre"), (g2pre_d, g2pre, "g2pre")):
        S.dma(DMA(g8[0:8, :], src), writes=["g8"])
        S.op(PE, TR(PSt[0][:, 0:8], g8[0:8, :], identf[0:8, 0:8]), reads=["g8", "identf"], writes=["ps0"])
        S.op(DV, CP(dst, PSt[0][:, 0:8]), reads=["ps0"], writes=[nm])
    w_in_v = w_in_d.rearrange("(k p) c -> p k c", p=128)
    wg_v = w_glu_d.rearrange("(k p) c -> p k c", p=128)
    wo_v = w_out_d.rearrange("(k p) c -> p k c", p=128)
    for kt in range(8):
        S.dma(DMA(Win3[:, kt, :], w_in_v[:, kt, :], max_dma_last_dim=4096), q="pool", writes=["Win"])
    LRE = ar.f32(16); LIM = ar.f32(16); DT = ar.f32(16)
    BRE = ar.f32(256); BIM = ar.f32(256); CRE = ar.f32(256); CIM = ar.f32(256)
    Dcol = ar.f32(32)
    for par in range(2):
        ps_ = slice(64 * par, 64 * par + 64)
        for (dst, src, nm) in ((LRE, lre_d, "LRE"), (LIM, lim_d, "LIM")):
            S.dma(DMA(dst[ps_, :], bass.AP(src.tensor, par * 64, [[1, 64], [128, 16]]), allow_slow_non_contiguous=True), slow=True, writes=[nm])
        S.dma(DMA(DT[ps_, :], bass.AP(ldt_d.tensor, par, [[0, 64], [2, 16]]), allow_slow_non_contiguous=True), slow=True, writes=["DT"])
        for (dst, src, nm) in ((BRE, bre_d, "BRE"), (BIM, bim_d, "BIM")):
            S.dma(DMA(dst[ps_, :].rearrange("p (g c) -> p g c", g=16), bass.AP(src.tensor, par * 1024, [[16, 64], [2048, 16], [1, 16]])), writes=[nm])
    for par in range(2):
        ps_ = slice(64 * par, 64 * par + 64)
        for (dst, src, nm) in ((CRE, cre_d, "CRE"), (CIM, cim_d, "CIM")):
            for gg in range(16):
                S.dma(DMA(dst[ps_, gg * 16:(gg + 1) * 16], bass.AP(src.tensor, par * 1024 + gg * 2048, [[1, 64], [64, 16]]), allow_slow_non_contiguous=True),
                      slow=True, writes=[nm], q=("sp" if nm == "CRE" else "act"))
    for sp_ in range(8):
        S.dma(DMA(Dcol[16 * sp_:16 * sp_ + 16, :], bass.AP(sd_d.tensor, 0, [[1, 16], [16, 32]]), allow_slow_non_contiguous=True), slow=True, writes=["Dcol"])
    cnt = [0]

    def tmp(n):
        return ar.f32(n)

    def vop(fn, reads, writes):
        cnt[0] += 1
        S.op(DV, fn, reads=reads, writes=writes)

    a_ = tmp(16); th = tmp(16); rr = tmp(16); kf = tmp(16); ki = ar.i32(16); red = tmp(16); sn = tmp(16); cs = tmp(16); ab = tmp(16)
    vop(TS(LRE, LRE, -1e-4, None, ALU.min), ["LRE"], ["LRE"])
    S.op(AC, ACT(DT, DT, AF.Exp), reads=["DT"], writes=["DT"])
    vop(TT(a_, LRE, DT, ALU.mult), ["LRE", "DT"], ["a_"])
    vop(TT(th, LIM, DT, ALU.mult), ["LIM", "DT"], ["th"])
    S.op(AC, ACT(rr, a_, AF.Exp), reads=["a_"], writes=["rr"])
    vop(TS(kf, th, 1.0 / TWO_PI, None, ALU.mult), ["th"], ["kf"])
    vop(CP(ki, kf), ["kf"], ["ki"])
    vop(CP(kf, ki), ["ki"], ["kf"])
    vop(STT(red, kf, -C1, th, ALU.mult, ALU.add), ["kf", "th"], ["red"])
    vop(STT(red, kf, -C2, red, ALU.mult, ALU.add), ["kf", "red"], ["red"])
    vop(TS(red, red, math.pi, -math.pi, ALU.min, ALU.max), ["red"], ["red"])
    S.op(AC, ACT(sn, red, AF.Sin), reads=["red"], writes=["sn"])
    vop(TS(ab, red, -1.0, None, ALU.mult), ["red"], ["ab"])
    vop(TT(ab, ab, red, ALU.max), ["ab", "red"], ["ab"])
    vop(TS(ab, ab, -1.0, math.pi / 2, ALU.mult, ALU.add), ["ab"], ["ab"])
    S.op(AC, ACT(cs, ab, AF.Sin), reads=["ab"], writes=["cs"])
    PR = tmp(9 * 16); PI = tmp(9 * 16); QR = tmp(8 * 16); QI = tmp(8 * 16)
    PR3 = PR.rearrange("p (j g) -> p j g", g=16); PI3 = PI.rearrange("p (j g) -> p j g", g=16)
    QR3 = QR.rearrange("p (j g) -> p j g", g=16); QI3 = QI.rearrange("p (j g) -> p j g", g=16)
    t1 = tmp(16); t2 = tmp(16); ivr = tmp(16); ivi = tmp(16); r2 = tmp(16)
    vop(MS(PR3[:, 0, :], 1.0), [], ["PR"]); vop(MS(PI3[:, 0, :], 0.0), [], ["PI"])
    vop(MS(QR3[:, 0, :], 1.0), [], ["QR"]); vop(MS(QI3[:, 0, :], 0.0), [], ["QI"])
    vop(TT(PR3[:, 1, :], rr, cs, ALU.mult), ["rr", "cs", "PR"], ["PR"])
    vop(TT(PI3[:, 1, :], rr, sn, ALU.mult), ["rr", "sn", "PI"], ["PI"])
    vop(TT(r2, rr, rr, ALU.mult), ["rr"], ["r2"])
    vop(lambda e: e.reciprocal(out=r2, in_=r2), ["r2"], ["r2"])
    vop(TT(ivr, PR3[:, 1, :], r2, ALU.mult), ["PR", "r2"], ["ivr"])
    vop(TT(ivi, PI3[:, 1, :], r2, ALU.mult), ["PI", "r2"], ["ivi"])
    vop(TS(ivi, ivi, -1.0, None, ALU.mult), ["ivi"], ["ivi"])

    def cmul(outr, outi, ar_, ai_, br_, bi_, rk, wk):
        vop(TT(t1, ar_, br_, ALU.mult), rk, ["t1"]); vop(TT(t2, ai_, bi_, ALU.mult), rk, ["t2"])
        vop(TT(outr, t1, t2, ALU.subtract), ["t1", "t2"] + wk, wk)
        vop(TT(t1, ar_, bi_, ALU.mult), rk + ["t1"], ["t1"]); vop(TT(t2, ai_, br_, ALU.mult), rk + ["t2"], ["t2"])
        vop(TT(outi, t1, t2, ALU.add), ["t1", "t2"] + wk, wk)

    for j in range(1, 8):
        cmul(PR3[:, j + 1, :], PI3[:, j + 1, :], PR3[:, j, :], PI3[:, j, :], PR3[:, 1, :], PI3[:, 1, :], ["PR", "PI"], ["PR", "PI"])
    vop(CP(QR3[:, 1, :], ivr), ["ivr", "QR"], ["QR"]); vop(CP(QI3[:, 1, :], ivi), ["ivi", "QI"], ["QI"])
    for j in range(1, 7):
        cmul(QR3[:, j + 1, :], QI3[:, j + 1, :], QR3[:, j, :], QI3[:, j, :], ivr, ivi, ["QR", "QI", "ivr", "ivi"], ["QR", "QI"])
    nr = tmp(16); ni = tmp(16); den = tmp(16); lb1 = tmp(16); cr = tmp(16); ci = tmp(16)
    vop(TS(lb1, PR3[:, 1, :], -1.0, None, ALU.add), ["PR"], ["lb1"])
    vop(TT(t1, lb1, LRE, ALU.mult), ["lb1", "LRE"], ["t1"]); vop(TT(t2, PI3[:, 1, :], LIM, ALU.mult), ["PI", "LIM"], ["t2"])
    vop(TT(nr, t1, t2, ALU.add), ["t1", "t2"], ["nr"])
    vop(TT(t1, PI3[:, 1, :], LRE, ALU.mult), ["PI", "LRE", "t1"], ["t1"]); vop(TT(t2, lb1, LIM, ALU.mult), ["lb1", "LIM", "t2"], ["t2"])
    vop(TT(ni, t1, t2, ALU.subtract), ["t1", "t2"], ["ni"])
    vop(TT(t1, LRE, LRE, ALU.mult), ["LRE", "t1"], ["t1"]); vop(TT(t2, LIM, LIM, ALU.mult), ["LIM", "t2"], ["t2"])
    vop(TT(den, t1, t2, ALU.add), ["t1", "t2"], ["den"])
    vop(lambda e: e.reciprocal(out=den, in_=den), ["den"], ["den"])
    vop(TT(cr, nr, den, ALU.mult), ["nr", "den"], ["cr"]); vop(TT(ci, ni, den, ALU.mult), ["ni", "den"], ["ci"])
    BBR = tmp(256); BBI = tmp(256); u1 = tmp(256); u2 = tmp(256)
    v3 = lambda a: a.rearrange("p (g c) -> p g c", g=16)
    b16 = lambda a: bc(a, 2, 16)

    def cmul3(outr, outi, sr, si, xr, xi_, rk, wk, neg_im=False):
        vop(TT(v3(u1), v3(xr), b16(sr), ALU.mult), rk, ["u1"]); vop(TT(v3(u2), v3(xi_), b16(si), ALU.mult), rk, ["u2"])
        vop(TT(outr, v3(u1), v3(u2), ALU.subtract), ["u1", "u2"] + wk, wk)
        vop(TT(v3(u1), v3(xi_), b16(sr), ALU.mult), rk + ["u1"], ["u1"]); vop(TT(v3(u2), v3(xr), b16(si), ALU.mult), rk + ["u2"], ["u2"])
        if neg_im:
            vop(STT(outi, v3(u1), -1.0, v3(u2), ALU.mult, ALU.subtract), ["u1", "u2"] + wk, wk)
        else:
            vop(TT(outi, v3(u1), v3(u2), ALU.add), ["u1", "u2"] + wk, wk)

    cmul3(v3(BBR), v3(BBI), cr, ci, BRE, BIM, ["cr", "ci", "BRE", "BIM"], ["BB"])
    EN = ar.bf(16 * 256); EQ = tmp(16 * 256); G = tmp(16 * 2 * 9 * 16)
    EN5 = EN.rearrange("p (g r s c) -> p g r s c", g=16, r=2, s=8)
    EQ5 = EQ.rearrange("p (g r s c) -> p g r s c", g=16, r=2, s=8)
    G5 = G.rearrange("p (g r j c) -> p g r j c", g=16, r=2, j=9)
    for s_ in range(8):
        cmul3(EN5[:, :, 0, s_, :], EN5[:, :, 1, s_, :], PR3[:, 7 - s_, :], PI3[:, 7 - s_, :], BBR, BBI, ["PR", "PI", "BB"], ["EN"])
        cmul3(EQ5[:, :, 0, s_, :], EQ5[:, :, 1, s_, :], QR3[:, s_, :], QI3[:, s_, :], BBR, BBI, ["QR", "QI", "BB"], ["EQ"])
    for j in range(9):
        cmul3(G5[:, :, 0, j, :], G5[:, :, 1, j, :], PR3[:, j, :], PI3[:, j, :], CRE, CIM, ["PR", "PI", "CRE", "CIM"], ["G"], neg_im=True)
    for ri in range(2):
        vop(CP(Wcr4[:, ri, :, :].rearrange("p g (s c) -> p g s c", s=8), G5[:, :, ri, 1:9, :]), ["G"], ["Wcr"])
    Wst5 = Wst.rearrange("p (g a r q) -> p g a r q", g=16, a=2, r=2)
    ENb5 = EN5
    ptbs = [PSt[5].bitcast(BF16), PSt[4].bitcast(BF16)]
    for g2 in range(16):
        for ri in range(2):
            k = ri
            S.op(PE, TR(ptbs[k][:, 0:128], ENb5[:, g2, ri, :, :].rearrange("p s c -> p (s c)"), identb), reads=["EN", "identb"], writes=[f"psb{k}"])
            S.op(AC, ACT(Wst5[:, g2, :, ri, :], ptbs[k][:, 0:128].rearrange("p (a q) -> p a q", a=2), AF.Copy),
                 reads=[f"psb{k}"], writes=["Wst"])
    si_ = ar.i32(128); sf_ = tmp(128); ri_ = ar.i32(1); rf_ = tmp(1); BLK = tmp(128); wtmp = [tmp(128), tmp(128)]
    S.op(PL, lambda e: e.iota(si_, pattern=[[1, 128]], base=0, channel_multiplier=0), writes=["si_"])
    vop(CP(sf_, si_), ["si_"], ["sf_"])
    vop(TS(sf_, sf_, 1.0 / 16, -0.46875, ALU.mult, ALU.add), ["sf_"], ["sf_"])
    vop(CP(si_, sf_), ["sf_"], ["si_"])
    vop(CP(sf_, si_), ["si_"], ["sf_"])
    vop(TS(rf_, pidx, 1.0 / 16, -0.46875, ALU.mult, ALU.add), ["pidx"], ["rf_"])
    vop(CP(ri_, rf_), ["rf_"], ["ri_"])
    vop(CP(rf_, ri_), ["ri_"], ["rf_"])
    vop(TS(BLK, sf_, rf_[:, 0:1], None, ALU.is_ge), ["sf_", "rf_"], ["BLK"])
    for g in range(32):
        par, g2 = g % 2, g // 2
        ps_ = slice(64 * par, 64 * par + 64)
        k = 2 + g % 2
        for ri in range(2):
            S.op(PE, MM(PSt[k][:, 0:128], EQ5[ps_, g2, ri, :, :].rearrange("p s c -> p (s c)"),
                        G5[ps_, g2, ri, 0:8, :].rearrange("p s c -> p (s c)"), ri == 0, ri == 1),
                 reads=["EQ", "G"], writes=[f"ps{k}"])
        S.op(DV, TT(wtmp[g % 2], PSt[k][:, 0:128], BLK, ALU.mult), reads=[f"ps{k}", "BLK"], writes=[f"wtmp{g % 2}"])
        S.op(DV, STT(Wintra3[:, g, :], identf, Dcol[:, g:g + 1], wtmp[g % 2], ALU.mult, ALU.add), reads=[f"wtmp{g % 2}", "identf", "Dcol"], writes=["Wintra"])
    ur = tmp(16); ui = tmp(16); wr = tmp(16); wi = tmp(16); w2r = tmp(16); w2i = tmp(16); a8 = tmp(16)
    vop(TS(a8, a_, 8.0, None, ALU.mult), ["a_"], ["a8"])
    S.op(AC, ACT(R8, a8, AF.Exp), reads=["a8"], writes=["R8"])
    S.op(AC, ACT(a8, a8, AF.Exp, scale=-1.0), reads=["a8"], writes=["a8"])
    vop(TT(ur, PR3[:, 8, :], a8, ALU.mult), ["PR", "a8"], ["ur"]); vop(TT(ui, PI3[:, 8, :], a8, ALU.mult), ["PI", "a8"], ["ui"])
    vop(CP(COS3[:, :, 0], ur), ["ur"], ["COS"]); vop(CP(SIN3[:, :, 0], ui), ["ui"], ["SIN"])
    vop(CP(wr, ur), ["ur"], ["wr"]); vop(CP(wi, ui), ["ui"], ["wi"])
    e1 = tmp(16 * 32); e2 = tmp(16 * 32)
    for k in range(6):
        n = 1 << k
        e1v = e1[:, 0:16 * n].rearrange("p (g m) -> p g m", g=16); e2v = e2[:, 0:16 * n].rearrange("p (g m) -> p g m", g=16)
        wrb = bc(wr, 2, n); wib = bc(wi, 2, n)
        vop(TT(e1v, COS3[:, :, 0:n], wrb, ALU.mult), ["COS", "wr"], ["e1"]); vop(TT(e2v, SIN3[:, :, 0:n], wib, ALU.mult), ["SIN", "wi"], ["e2"])
        vop(TT(COS3[:, :, n:2 * n], e1v, e2v, ALU.subtract), ["e1", "e2", "COS"], ["COS"])
        vop(TT(e1v, COS3[:, :, 0:n], wib, ALU.mult), ["COS", "wi", "e1"], ["e1"]); vop(TT(e2v, SIN3[:, :, 0:n], wrb, ALU.mult), ["SIN", "wr", "e2"], ["e2"])
        vop(TT(SIN3[:, :, n:2 * n], e1v, e2v, ALU.add), ["e1", "e2", "SIN"], ["SIN"])
        if k < 5:
            cmul(w2r, w2i, wr, wi, wr, wi, ["wr", "wi"], ["w2"])
            vop(CP(wr, w2r), ["w2"], ["wr"]); vop(CP(wi, w2i), ["w2"], ["wi"])
    DBG.update(ur=ur, ui=ui, a8=a8, wr=wr, wi=wi, PR=PR, PI=PI, e1=e1, e2=e2)
    DBG.update(Wintra=Wintra, Wst=Wst, Wcr=Wcr, COS=COS, SIN=SIN, R8=R8, MASKH=MASKH, Win=Win, zcol=zcol, xi=xi, gpre=gpre, invf=invf, identb=identb, GPOST=GPOST)
    S.op(PL, MS(mhalf, -0.5), writes=["mhalf"])
    S.op(PL, MS(Rst, 0.0), writes=["R"]); S.op(PL, MS(Rb, 0.0), writes=["Rb0"]); S.op(PL, MS(Rb2, 0.0), writes=["Rb1"]); S.op(PL, MS(CAR, 0.0), writes=["CAR"])
    S.barrier()
    ar.top = P_BASE
    EPS = 1e-6
    pt_t = PSt[4]
    ptb = pt_t.bitcast(BF16)
    pss, pso, psu = PSt[5], PSt[6], PSt[7]
    mA = ar.top
    hT = ar.bf(8 * 1024); hT3 = hT.rearrange("p (k t) -> p k t", k=8)
    xs = [ar.f32(1024), ar.f32(1024), ar.f32(1024)]
    hb = [ar.bf(1024), ar.bf(1024)]
    junk = ar.bf(1024)
    ssq = ar.f32(8); rstd = ar.f32(8)
    rcos = ar.f32(512); rsin = ar.f32(512); rang = ar.f32(512); rk = ar.f32(512); rki = ar.i32(512); rm = ar.f32(512); posf = ar.f32(8)
    rcos3 = rcos.rearrange("p (c j) -> p c j", c=8); rsin3 = rsin.rearrange("p (c j) -> p c j", c=8)
    qkr = [ar.bf(1024), ar.bf(1024)]
    qkT = [ar.bf(1024), ar.bf(1024)]
    vb = [ar.bf(512), ar.bf(512)]; vz = [ar.bf(512), ar.bf(512)]; sTb = ar.bf(512)
    h4 = lambda a: a.rearrange("p (h d) -> p h d", h=4)
    a8v = lambda a: a.rearrange("p (a d) -> p a d", a=8)
    sTb3 = h4(sTb)
    rt = [ar.f32(512) for _ in range(4)]
    yn = ar.f32(512); sgg = [ar.f32(512), ar.f32(512)]; yr = ar.bf(512)
    bst = ar.f32(24); mv = ar.f32(8); rs = ar.f32(4); vtmp = ar.f32(4)
    mR_end = ar.top
    ar.top = mA
    U8 = ar.bf(4096); U83 = U8.rearrange("p (g n) -> p g n", g=32)
    Y8 = ar.bf(4096); Y83 = Y8.rearrange("p (g n) -> p g n", g=32)
    SPV = ar.bf(2 * 16 * 128); SPV4 = SPV.rearrange("p (r g n) -> p r g n", r=2, g=16)
    PRE = [[ar.f32(256), ar.f32(256)] for _ in range(2)]; SCN = [[ar.f32(256), ar.f32(256)] for _ in range(2)]
    st_ = [[ar.f32(256) for _ in range(4)] for _ in range(2)]
    FUL = [[ar.f32(256), ar.f32(256)] for _ in range(2)]
    ysT = [ar.bf(512), ar.bf(512)]; ys2 = [ar.bf(512), ar.bf(512)]; ys2T = [ar.bf(512), ar.bf(512)]
    sig = ar.f32(512); xs2 = [ar.f32(1024), ar.f32(1024)]; tm = [ar.f32(1024), ar.f32(1024)]
    junk2 = ar.bf(1024); ss2 = [ar.f32(2), ar.f32(2)]; rs2 = [ar.f32(2), ar.f32(2)]
    mS_end = ar.top
    assert max(mR_end, mS_end) <= AW, (mR_end, mS_end)
    v34 = lambda a: a.rearrange("p (g m) -> p g m", g=4)
    Tb4 = Tb.rearrange("p (g s c) -> p g s c", g=32, s=8)
    YT4 = Tb.rearrange("p (s g c) -> p s g c", s=8, g=32)
    YT3 = Tb.rearrange("p (s f) -> p s f", s=8)
    Rbs = [Rb, Rb2]

    mh = [mhalf]

    def rsqrt_cols(dst, src, scale, nm_src, nm_dst):
        n = dst.shape[1]
        S.op(DV, TS(dst, src, scale, EPS, ALU.mult, ALU.add), reads=[nm_src], writes=[nm_dst])
        S.op(PL, TT(dst, dst, mh[0][:, 0:n], ALU.pow), reads=[nm_dst, "mhalf"], writes=[nm_dst])

    rcnt = [0]
    for kt in range(4):
        S.dma(DMA(Wglu3[:, kt, :], wg_v[:, kt, :], max_dma_last_dim=4096), q="pool", writes=["Wglu"])
    for kt in range(8):
        S.dma(DMA(Wout3[:, kt, :], wo_v[:, kt, :], max_dma_last_dim=4096), q="pool", writes=["Wout"])
    for sbi in range((NPRE + NMAIN) if STOP is None else int(STOP)):
        pre = sbi < NPRE
        xsrc = x_pre if pre else x_own
        row0 = 1024 * (sbi if pre else sbi - NPRE)
        pcol = 0 if pre else 1
        xv = xsrc[row0:row0 + 1024, :].rearrange("(n s) d -> n s d", s=8)
        S.op(DV, TS(posf, iopc, posc[:, pcol:pcol + 1], float(row0), ALU.add, ALU.add), reads=["iopc", "posc", "rang"], writes=["posf"])
        rang3 = rang.rearrange("p (c j) -> p c j", c=8)
        S.op(DV, TT(rang3, bc(posf, 2, 64), bc(invf, 1, 8), ALU.mult), reads=["posf", "invf", "rcos"], writes=["rang"])
        S.op(DV, TS(rk, rang, 1.0 / TWO_PI, None, ALU.mult), reads=["rang"], writes=["rk"])
        S.op(DV, CP(rki, rk), reads=["rk"], writes=["rki"])
        S.op(DV, CP(rk, rki), reads=["rki"], writes=["rk"])
        S.op(DV, STT(rm, rk, -C1, rang, ALU.mult, ALU.add), reads=["rk", "rang"], writes=["rm"])
        S.op(DV, STT(rm, rk, -C2, rm, ALU.mult, ALU.add), reads=["rk", "rm"], writes=["rm"])
        S.op(DV, TS(rm, rm, math.pi, -math.pi, ALU.min, ALU.max), reads=["rm"], writes=["rm"])
        S.op(AC, ACT(rsin, rm, AF.Sin), reads=["rm"], writes=["rsin"])
        S.op(DV, TS(rk, rm, -1.0, None, ALU.mult), reads=["rm", "rk"], writes=["rk"])
        S.op(DV, TT(rk, rk, rm, ALU.max), reads=["rm", "rk"], writes=["rk"])
        S.op(DV, TS(rk, rk, -1.0, math.pi / 2, ALU.mult, ALU.add), reads=["rk"], writes=["rk"])
        S.op(AC, ACT(rcos, rk, AF.Sin), reads=["rk"], writes=["rcos"])

        def ht1(s):
            b = s % 3
            S.dma(DMA(xs[b], xsrc[row0 + 128 * s:row0 + 128 * s + 128, :]), writes=[f"xs{b}"])
            S.op(AC, ACT(junk, xs[b], AF.Square, accum_out=ssq[:, s:s + 1]), reads=[f"xs{b}"], writes=["junk", f"ssq{s}"])

        def ht2(s):
            b = s % 2; bx = s % 3
            rsqrt_cols(rstd[:, s:s + 1], ssq[:, s:s + 1], 1.0 / 1024, f"ssq{s}", f"rstd{s}")
            S.op(AC, ACT(hb[b], xs[bx], AF.Copy, scale=rstd[:, s:s + 1]), reads=[f"xs{bx}", f"rstd{s}"], writes=[f"hb{b}"])
            tbk, tnm = (ptb, "pt") if s % 2 == 0 else (PSt[5].bitcast(BF16), "pss")
            for kt in range(8):
                S.op(PE, TR(tbk[:, kt * 128:(kt + 1) * 128], hb[b][:, kt * 128:(kt + 1) * 128], identb), reads=[f"hb{b}", "identb"], writes=[tnm])
            S.op(DV, TT(hT3[:, :, 128 * s:128 * s + 128], tbk.rearrange("p (k n) -> p k n", k=8), bc(gpre, 2, 128), ALU.mult), reads=[tnm, "gpre"], writes=["hT"])

        if os.environ.get("KSEQ_H"):
            for s in range(8):
                ht1(s); ht2(s)
        else:
            ht1(0); ht1(1)
            for s in range(8):
                if s + 2 < 8:
                    ht1(s + 2)
                ht2(s)

        tkc = lambda c: slice(128 * c, 128 * c + 128)
        if pre:
            def A(c):
                bk, bv = (PSt[0], PSt[1]) if c % 2 == 0 else (PSt[2], PSt[3])
                nk, nv = ("ps0", "ps1") if c % 2 == 0 else ("ps2", "ps3")
                for (bank, col0, nm) in ((bk, 512, nk), (bv, 1024, nv)):
                    for kt in range(8):
                        S.op(PE, MM(bank[:, :], hT3[:, kt, tkc(c)], Win3[:, kt, col0:col0 + 512], kt == 0, kt == 7), reads=["hT", "Win"], writes=[nm])

            def B(c):
                b = c % 2
                bk, bv = (PSt[0], PSt[1]) if c % 2 == 0 else (PSt[2], PSt[3])
                nk, nv = ("ps0", "ps1") if c % 2 == 0 else ("ps2", "ps3")
                cosb = bc(rcos3[:, c, :], 1, 4); sinb = bc(rsin3[:, c, :], 1, 4)
                xq = bk[:, :].rearrange("p (h a j) -> p h a j", h=4, a=2)
                x1 = xq[:, :, 0, :]; x2 = xq[:, :, 1, :]
                r3 = [t_[:, 0:256].rearrange("p (h j) -> p h j", h=4) for t_ in rt]
                S.op(DV, TT(r3[0], x1, cosb, ALU.mult), reads=[nk, "rcos"], writes=["rt0"])
                S.op(DV, TT(r3[1], x2, sinb, ALU.mult), reads=[nk, "rsin"], writes=["rt1"])
                S.op(DV, TT(r3[2], x1, sinb, ALU.mult), reads=[nk, "rsin"], writes=["rt2"])
                S.op(DV, TT(r3[3], x2, cosb, ALU.mult), reads=[nk, "rcos"], writes=["rt3"])
                q4 = qkr[b].rearrange("p (a h j) -> p a h j", a=8, h=2)
                S.op(PL, TT(q4[:, 4:8, 0, :], r3[0], r3[1], ALU.subtract), reads=["rt0", "rt1"], writes=[f"qkr{b}"])
                S.op(PL, TT(q4[:, 4:8, 1, :], r3[2], r3[3], ALU.add), reads=["rt2", "rt3"], writes=[f"qkr{b}"])
                S.op(DV, TT(h4(vz[b]), h4(bv[:, :]), bc(zcol, 2, 128), ALU.mult), reads=[nv, "zcol"], writes=[f"vz{b}"])

            def ST(c):
                b = c % 2
                psu3 = h4(psu[:, :])
                for h in range(4):
                    S.op(PE, MM(psu3[:, h, :], a8v(qkr[b])[:, 4 + h, :], h4(vz[b])[:, h, :], True, True), reads=[f"qkr{b}", f"vz{b}"], writes=["psu"])
                for h in range(4):
                    S.op(DV, STT(Rst3[:, h, :], Rst3[:, h, :], GHEAD[h], psu3[:, h, :], ALU.mult, ALU.add), reads=["psu", "R"], writes=["R"])

            A(0)
            for c in range(8):
                if c + 1 < 8:
                    A(c + 1)
                B(c)
                ST(c)
            nb = rcnt[0] % 2
            S.op(AC, ACT(Rbs[nb], Rst, AF.Copy), reads=["R"], writes=[f"Rb{nb}"])
        else:

            def A1(c):
                for (col0, off) in ((0, 0), (512, 512)):
                    for kt in range(8):
                        S.op(PE, MM(PSt[off // 512][:, :], hT3[:, kt, tkc(c)], Win3[:, kt, col0:col0 + 512], kt == 0, kt == 7), reads=["hT", "Win"], writes=["ps0" if off == 0 else "ps1"])

            def A2(c):
                for (col0, off) in ((1024, 0), (1536, 512)):
                    for kt in range(8):
                        S.op(PE, MM(PSt[2 + off // 512][:, :], hT3[:, kt, tkc(c)], Win3[:, kt, col0:col0 + 512], kt == 0, kt == 7), reads=["hT", "Win"], writes=["ps2" if off == 0 else "ps3"])

            def B1(c):
                b = c % 2
                cosb = bc(rcos3[:, c, :], 1, 4); sinb = bc(rsin3[:, c, :], 1, 4)
                q4 = qkr[b].rearrange("p (a t j) -> p a t j", a=8, t=2)
                for half in range(2):
                    nm = "ps0" if half == 0 else "ps1"
                    xq = PSt[half][:, :].rearrange("p (a t j) -> p a t j", a=4, t=2)
                    x1 = xq[:, :, 0, :]; x2 = xq[:, :, 1, :]
                    r3 = [t_[:, 256 * half:256 * half + 256].rearrange("p (a j) -> p a j", a=4) for t_ in rt]
                    S.op(DV, TT(r3[0], x1, cosb, ALU.mult), reads=[nm, "rcos"], writes=[f"rt0{half}"])
                    S.op(DV, TT(r3[1], x2, sinb, ALU.mult), reads=[nm, "rsin"], writes=[f"rt1{half}"])
                    S.op(DV, TT(r3[2], x1, sinb, ALU.mult), reads=[nm, "rsin"], writes=[f"rt2{half}"])
                    S.op(DV, TT(r3[3], x2, cosb, ALU.mult), reads=[nm, "rcos"], writes=[f"rt3{half}"])
                    S.op(PL, TT(q4[:, 4 * half:4 * half + 4, 0, :], r3[0], r3[1], ALU.subtract), reads=[f"rt0{half}", f"rt1{half}"], writes=[f"qkr{b}"])
                    S.op(PL, TT(q4[:, 4 * half:4 * half + 4, 1, :], r3[2], r3[3], ALU.add), reads=[f"rt2{half}", f"rt3{half}"], writes=[f"qkr{b}"])

            def B2(c):
                b = c % 2
                for h in range(4):
                    S.op(AC, ACT(h4(vz[b])[:, h, :], h4(PSt[2][:, :])[:, h, :], AF.Copy, scale=zcol[:, h:h + 1]), reads=["ps2", "zcol"], writes=[f"vz{b}"])
                S.op(AC, ACT(vb[b], PSt[2][:, :], AF.Copy), reads=["ps2"], writes=[f"vb{b}"])
                S.op(AC, ACT(sgg[b], PSt[3][:, :], AF.Silu), reads=["ps3"], writes=[f"sgg{b}"])
                S.op(PL, TT(sgg[b], sgg[b], GGN, ALU.mult), reads=[f"sgg{b}", "GGN"], writes=[f"sgg{b}"])

            def Tqk(c):
                b = c % 2
                for a in range(8):
                    S.op(PE, TR(ptb[:, a * 128:(a + 1) * 128], a8v(qkr[b])[:, a, :], identb), reads=[f"qkr{b}", "identb"], writes=["pt"])
                S.op(AC, ACT(qkT[b], ptb[:, :], AF.Copy), reads=["pt"], writes=[f"qkT{b}"])

            def SC(c):
                b = c % 2
                pss3 = h4(pss[:, :]); qT = a8v(qkT[b])
                for h in range(4):
                    S.op(PE, MM(pss3[:, h, :], qT[:, 4 + h, :], qT[:, h, :], True, True), reads=[f"qkT{b}"], writes=["pss"])
                S.op(DV, TT(sTb3, pss3, MASKH3, ALU.mult), reads=["pss", "MASKH"], writes=["sTb"])

            def ST(c):
                b = c % 2
                psu3 = h4(psu[:, :])
                for h in range(4):
                    S.op(PE, MM(psu3[:, h, :], a8v(qkr[b])[:, 4 + h, :], h4(vz[b])[:, h, :], True, True), reads=[f"qkr{b}", f"vz{b}"], writes=["psu"])
                for h in range(4):
                    S.op(DV, STT(Rst3[:, h, :], Rst3[:, h, :], GHEAD[h], psu3[:, h, :], ALU.mult, ALU.add), reads=["psu", "R"], writes=["R"])
                nb = (rcnt[0] + c + 1) % 2
                S.op(AC, ACT(Rbs[nb], Rst, AF.Copy), reads=["R"], writes=[f"Rb{nb}"])

            def OUT(c):
                b = c % 2
                cb = (rcnt[0] + c) % 2
                pso3 = h4(pso[:, :]); qT = a8v(qkT[b])
                for h in range(4):
                    S.op(PE, MM(pso3[:, h, :], sTb3[:, h, :], h4(vb[b])[:, h, :], True, False), reads=["sTb", f"vb{b}"], writes=["pso"])
                    S.op(PE, MM(pso3[:, h, :], qT[:, h, :], h4(Rbs[cb])[:, h, :], False, True), reads=[f"qkT{b}", f"Rb{cb}"], writes=["pso"])

            def GN(c):
                b = c % 2
                pso3 = h4(pso[:, :])
                bst3 = bst.rearrange("p (h k) -> p h k", h=4); mv3 = mv.rearrange("p (h k) -> p h k", h=4)
                for h in range(4):
                    S.op(DV, lambda e, h=h: e.bn_stats(out=bst3[:, h, :], in_=pso3[:, h, :]), reads=["pso"], writes=["bst"])
                    S.op(DV, lambda e, h=h: e.bn_aggr(out=mv3[:, h, :], in_=bst3[:, h, :]), reads=["bst"], writes=["mv"])
                S.op(DV, TT(vtmp, mv3[:, :, 1], xi2, ALU.mult), reads=["mv", "xi2"], writes=["vtmp"])
                rsqrt_cols(rs, vtmp, 1.0, "vtmp", "rs")
                S.op(DV, TT(rs, rs, xi, ALU.mult), reads=["rs", "xi"], writes=["rs"])
                yn3 = h4(yn)
                for h in range(4):
                    S.op(DV, TS(yn3[:, h, :], pso3[:, h, :], mv3[:, h, 0:1], rs[:, h:h + 1], ALU.subtract, ALU.mult), reads=["pso", "mv", "rs"], writes=["yn"])
                S.op(PL, TT(yr, yn, sgg[b], ALU.mult), reads=["yn", f"sgg{b}"], writes=["yr"])

            def Tyr(c):
                for h in range(4):
                    S.op(PE, TR(ptb[:, h * 128:(h + 1) * 128], yr[:, h * 128:(h + 1) * 128], identb), reads=["yr", "identb"], writes=["pt"])
                S.op(AC, ACT(yrT3[:, :, tkc(c)], ptb[:, 0:512].rearrange("p (h i) -> p h i", h=4), AF.Copy), reads=["pt"], writes=["yrT"])

            if os.environ.get("KSEQ_R"):
                for c in range(8):
                    A1(c); A2(c); B1(c); B2(c); Tqk(c); SC(c); ST(c); OUT(c); GN(c); Tyr(c)
            else:
                A1(0); A2(0); B1(0); B2(0)
                for c in range(8):
                    Tqk(c)
                    if c + 1 < 8:
                        A1(c + 1)
                    if c > 0:
                        Tyr(c - 1)
                    SC(c); ST(c)
                    if c + 1 < 8:
                        B1(c + 1)
                        A2(c + 1)
                    OUT(c)
                    if c + 1 < 8:
                        B2(c + 1)
                    GN(c)
                Tyr(7)
        rcnt[0] += (0 if pre else 8)
        for s in range(8):
            bank, nm = (pss, "pss") if s % 2 == 0 else (pso, "pso")
            for kt in range(8):
                S.op(PE, MM(bank[:, :], hT3[:, kt, s::8], Win3[:, kt, 2048:2560], kt == 0, kt == 7), reads=["hT", "Win"], writes=[nm])
            S.op(AC, ACT(Tb4[:, :, s, :], bank[:, :].rearrange("p (g c) -> p g c", g=32), AF.Copy), reads=[nm], writes=["T"])
        S.barrier()
        Tflat = Tb.rearrange("p (g f) -> p g f", g=32)
        ptb2 = PSt[5].bitcast(BF16)
        for gq in range(4):
            tb_, nm = (ptb, "pt") if gq % 2 == 0 else (ptb2, "pss")
            for j in range(8):
                S.op(PE, TR(tb_[:, j * 128:(j + 1) * 128], Tflat[:, 8 * gq + j, :], identb), reads=["T", "identb"], writes=[nm])
            S.op(AC if gq % 2 else DV, (ACT(U83[:, 8 * gq:8 * gq + 8, :], tb_.rearrange("p (g n) -> p g n", g=8), AF.Copy) if gq % 2 else
                                        CP(U83[:, 8 * gq:8 * gq + 8, :], tb_.rearrange("p (g n) -> p g n", g=8))), reads=[nm], writes=["U8"])
        its = [(hf, gb) for hf in range(2) for gb in range(4)]

        def Mm(it):
            hf, gb = its[it]; p = it % 2
            nsl = slice(64 * hf, 64 * hf + 64)
            psr3 = PSt[2 * p][:, 0:256].rearrange("p (g m) -> p g m", g=4); psi3 = PSt[2 * p + 1][:, 0:256].rearrange("p (g m) -> p g m", g=4)
            for j in range(8):
                g = 8 * gb + j; par = j % 2; slot = j // 2
                ps_ = slice(64 * par, 64 * par + 64)
                S.op(PE, MM(psr3[ps_, slot, :], Wst4[:, g, 0, :], U83[:, g, nsl], True, True), reads=["U8", "Wst"], writes=[f"ps{2 * p}"])
                S.op(PE, MM(psi3[ps_, slot, :], Wst4[:, g, 1, :], U83[:, g, nsl], True, True), reads=["U8", "Wst"], writes=[f"ps{2 * p + 1}"])

        def R1(it):
            hf, gb = its[it]; p = it % 2
            g2s = slice(4 * gb, 4 * gb + 4)
            psr3 = PSt[2 * p][:, 0:256].rearrange("p (g m) -> p g m", g=4); psi3 = PSt[2 * p + 1][:, 0:256].rearrange("p (g m) -> p g m", g=4)
            cosv = COS3[:, g2s, :]; sinv = SIN3[:, g2s, :]
            nr_, ni_ = f"ps{2 * p}", f"ps{2 * p + 1}"
            S.op(DV, TT(v34(st_[p][0]), psr3, cosv, ALU.mult), reads=[nr_, "COS"], writes=[f"st{p}0"])
            S.op(DV, TT(v34(st_[p][1]), psi3, sinv, ALU.mult), reads=[ni_, "SIN"], writes=[f"st{p}1"])
            S.op(DV, TT(v34(st_[p][2]), psi3, cosv, ALU.mult), reads=[ni_, "COS"], writes=[f"st{p}2"])
            S.op(DV, TT(v34(st_[p][3]), psr3, sinv, ALU.mult), reads=[nr_, "SIN"], writes=[f"st{p}3"])
            S.op(PL, TT(PRE[p][0], st_[p][0], st_[p][1], ALU.add), reads=[f"st{p}0", f"st{p}1"], writes=[f"PRE{p}0"])
            S.op(PL, TT(PRE[p][1], st_[p][2], st_[p][3], ALU.subtract), reads=[f"st{p}2", f"st{p}3"], writes=[f"PRE{p}1"])

        def SCAN(it):
            hf, gb = its[it]; p = it % 2
            g2s = slice(4 * gb, 4 * gb + 4)
            if not pre:
                for ri in range(2):
                    S.op(AC, ACT(SPV4[:, ri, g2s, 64 * hf:64 * hf + 1], CAR3[:, ri, g2s].unsqueeze(2), AF.Copy), reads=[f"CAR{gb}"], writes=["SPV"])
            for ri in range(2):
                for slot in range(4):
                    g2 = 4 * gb + slot
                    S.op(DV, lambda e, ri=ri, slot=slot, g2=g2, p=p: e.tensor_tensor_scan(
                        out=v34(SCN[p][ri])[:, slot, :], data0=R8[:, g2:g2 + 1].broadcast_to([128, 64]), data1=v34(PRE[p][ri])[:, slot, :],
                        initial=CAR3[:, ri, g2:g2 + 1], op0=ALU.mult, op1=ALU.add), reads=[f"PRE{p}{ri}", f"CAR{gb}", "R8"], writes=[f"SCN{p}{ri}"])

        def R2(it):
            hf, gb = its[it]; p = it % 2
            g2s = slice(4 * gb, 4 * gb + 4)
            cl = slice(63, 64) if pre else slice(0, 64)
            cosv = COS3[:, g2s, cl]; sinv = SIN3[:, g2s, cl]
            w = lambda a: v34(a)[:, :, cl]
            S.op(DV, TT(w(st_[p][0]), w(SCN[p][0]), cosv, ALU.mult), reads=[f"SCN{p}0", "COS"], writes=[f"st{p}0"])
            S.op(DV, TT(w(st_[p][1]), w(SCN[p][1]), sinv, ALU.mult), reads=[f"SCN{p}1", "SIN"], writes=[f"st{p}1"])
            S.op(DV, TT(w(st_[p][2]), w(SCN[p][1]), cosv, ALU.mult), reads=[f"SCN{p}1", "COS"], writes=[f"st{p}2"])
            S.op(DV, TT(w(st_[p][3]), w(SCN[p][0]), sinv, ALU.mult), reads=[f"SCN{p}0", "SIN"], writes=[f"st{p}3"])
            S.op(PL, TT(w(FUL[p][0]), w(st_[p][0]), w(st_[p][1]), ALU.subtract), reads=[f"st{p}0", f"st{p}1"], writes=[f"FUL{p}0"])
            S.op(PL, TT(w(FUL[p][1]), w(st_[p][2]), w(st_[p][3]), ALU.add), reads=[f"st{p}2", f"st{p}3"], writes=[f"FUL{p}1"])
            for ri in range(2):
                S.op(PL, CP(CAR3[:, ri, g2s], v34(FUL[p][ri])[:, :, 63]), reads=[f"FUL{p}{ri}", f"SCN{p}0", f"SCN{p}1", "SPV"], writes=[f"CAR{gb}"])
                if not pre:
                    S.op(AC, ACT(SPV4[:, ri, g2s, 64 * hf + 1:64 * hf + 64], v34(FUL[p][ri])[:, :, 0:63], AF.Copy), reads=[f"FUL{p}{ri}"], writes=["SPV"])

        if os.environ.get("KSEQ_S"):
            for it in range(8):
                Mm(it); R1(it); SCAN(it); R2(it)
        else:
            Mm(0); Mm(1); R1(0)
            for it in range(8):
                if it + 1 < 8:
                    R1(it + 1)
                SCAN(it)
                R2(it)
                if it + 2 < 8:
                    Mm(it + 2)
        if not pre:
            for gq in range(8):
                py, nm = (PSt[0], "ps0") if gq % 2 == 0 else (PSt[1], "ps1")
                py3 = py[:, :].rearrange("p (g n) -> p g n", g=4)
                for j in range(4):
                    g = 4 * gq + j; par = g % 2; g2 = g // 2
                    ps_ = slice(64 * par, 64 * par + 64)
                    S.op(PE, MM(py3[:, j, :], Wintra3[:, g, :], U83[:, g, :], True, False), reads=["U8", "Wintra"], writes=[nm])
                    S.op(PE, MM(py3[:, j, :], Wcr4[ps_, 0, g2, :], SPV4[ps_, 0, g2, :], False, False), reads=["SPV", "Wcr"], writes=[nm])
                    S.op(PE, MM(py3[:, j, :], Wcr4[ps_, 1, g2, :], SPV4[ps_, 1, g2, :], False, True), reads=["SPV", "Wcr"], writes=[nm])
                S.op(AC, ACT(Y83[:, 4 * gq:4 * gq + 4, :], py3, AF.Gelu_apprx_tanh), reads=[nm], writes=["Y8"])
            for gq in range(4):
                tb_, nm = (ptb, "pt") if gq % 2 == 0 else (ptb2, "pss")
                for j in range(8):
                    S.op(PE, TR(tb_[:, j * 128:(j + 1) * 128], Y83[:, 8 * gq + j, :], identb), reads=["Y8", "identb"], writes=[nm])
                S.op(DV, CP(YT4[:, :, 8 * gq:8 * gq + 8, :].rearrange("p s g c -> p g s c"), tb_.rearrange("p (g s c) -> p g s c", g=8, s=8)), reads=[nm, "U8"], writes=["T"])
            pg0, pg1 = PSt[2], PSt[3]
            pms = [(PSt[0], PSt[1], "ps0", "ps1"), (PSt[6], PSt[7], "pso", "psu")]

            def T1(s):
                b = s % 2
                for kt in range(4):
                    S.op(PE, TR(ptb[:, kt * 128:(kt + 1) * 128], YT3[:, s, kt * 128:(kt + 1) * 128], identb), reads=["T", "identb"], writes=["pt"])
                S.op(AC, ACT(ysT[b], ptb[:, 0:512], AF.Copy), reads=["pt"], writes=[f"ysT{b}"])

            def GLU(s):
                b = s % 2
                ysT3 = ysT[b].rearrange("p (k n) -> p k n", k=4)
                for hfc, bank, nm in ((0, pg0, "ps2"), (1, pg1, "ps3")):
                    for kt in range(4):
                        S.op(PE, MM(bank[:, :], ysT3[:, kt, :], Wglu3[:, kt, hfc * 512:(hfc + 1) * 512], kt == 0, kt == 3), reads=[f"ysT{b}", "Wglu"], writes=[nm])
                S.op(AC, ACT(sig, pg1[:, :], AF.Sigmoid), reads=["ps3"], writes=["sig"])
                S.op(DV, TT(ys2[b], pg0[:, :], sig, ALU.mult), reads=["ps2", "sig"], writes=[f"ys2{b}"])

            def T2(s):
                b = s % 2
                for kt in range(4):
                    S.op(PE, TR(ptb2[:, kt * 128:(kt + 1) * 128], ys2[b][:, kt * 128:(kt + 1) * 128], identb), reads=[f"ys2{b}", "identb"], writes=["pss"])
                S.op(AC, ACT(ys2T[b], ptb2[:, 0:512], AF.Copy), reads=["pss"], writes=[f"ys2T{b}"])

            def WO(s):
                b = s % 2
                pm0, pm1, n0, n1 = pms[b]
                y2T3 = ys2T[b].rearrange("p (k n) -> p k n", k=4)
                for hfc, bank, nm in ((0, pm0, n0), (1, pm1, n1)):
                    for kt in range(8):
                        lhs = yrT3[:, kt, s::8] if kt < 4 else y2T3[:, kt - 4, :]
                        S.op(PE, MM(bank[:, :], lhs, Wout3[:, kt, hfc * 512:(hfc + 1) * 512], kt == 0, kt == 7), reads=["yrT", f"ys2T{b}", "Wout"], writes=[nm])
                S.dma(DMA(xs2[b], xv[:, s, :]), writes=[f"xs2{b}"])
                for hfc, bank, nm in ((0, pm0, n0), (1, pm1, n1)):
                    S.op(AC, ACT(junk2[:, 0:512], bank[:, :], AF.Square, accum_out=ss2[b][:, hfc:hfc + 1]), reads=[nm], writes=["junk2", f"ss2{b}{hfc}"])
                S.op(DV, TT(ss2[b][:, 0:1], ss2[b][:, 0:1], ss2[b][:, 1:2], ALU.add), reads=[f"ss2{b}0", f"ss2{b}1"], writes=[f"ss2{b}0"])
                rsqrt_cols(rs2[b][:, 0:1], ss2[b][:, 0:1], 1.0 / 1024, f"ss2{b}0", f"rs2{b}")
                for hfc, bank, nm in ((0, pm0, n0), (1, pm1, n1)):
                    cs_ = slice(hfc * 512, hfc * 512 + 512)
                    S.op(DV, STT(tm[b][:, cs_], bank[:, :], rs2[b][:, 0:1], GPOST[:, cs_], ALU.mult, ALU.mult), reads=[nm, f"rs2{b}", "GPOST"], writes=[f"tm{b}"])
                S.op(PL, TT(xs2[b], xs2[b], tm[b], ALU.add), reads=[f"tm{b}", f"xs2{b}"], writes=[f"xs2{b}"])
                r0 = 1024 * (sbi - NPRE)
                S.dma(DMA(x1s[r0:r0 + 1024, :].rearrange("(n s) d -> n s d", s=8)[:, s, :], xs2[b]), reads=[f"xs2{b}"], writes=["x1s"])

            if os.environ.get("KSEQ_W"):
                for s in range(8):
                    T1(s); GLU(s); T2(s); WO(s)
            else:
                T1(0); GLU(0); T1(1); T2(0)
                for s in range(1, 8):
                    GLU(s)
                    WO(s - 1)
                    if s + 1 < 8:
                        T1(s + 1)
                    T2(s)
                WO(7)
        S.barrier()

    ar.top = 0
    W1b = ar.bf(8 * 4096); W1b3 = W1b.rearrange("p (k c) -> p k c", k=8)
    W2b = ar.bf(32 * 1024); W2b3 = W2b.rearrange("p (k c) -> p k c", k=32)
    identb2 = ar.bf(128); GPOST2 = ar.f32(1024); g2c = ar.f32(8)
    X1g = ar.f32(4096); X1g3 = X1g.rearrange("p (j d) -> p j d", j=4)
    h2T = ar.bf(8 * 512); h2T3 = h2T.rearrange("p (k t) -> p k t", k=8)
    aT = ar.bf(32 * 512); aT3 = aT.rearrange("p (k t) -> p k t", k=32)
    hb2 = ar.bf(1024); jk = ar.bf(1024); rl = [ar.f32(512), ar.f32(512)]; tmb = ar.f32(1024); sq = ar.f32(4); rq = ar.f32(4); so = ar.f32(2); ro = ar.f32(2)
    mhalf2 = ar.f32(8)
    assert ar.top <= AW, ar.top
    stgB = aT.bitcast(F32) if False else None
    cst = X1g
    S.op(DV, CP(cst[:, 0:8], g2pre), reads=[], writes=["cst"])
    S.op(DV, CP(jk[:, 0:128], identb), reads=[], writes=["jk"])
    S.barrier()
    S.op(DV, CP(g2c, cst[:, 0:8]), reads=["cst"], writes=["g2c"])
    S.op(DV, CP(identb2, jk[:, 0:128]), reads=["jk"], writes=["identb2"])
    S.dma(DMA(GPOST2, dbc(g2post_d)), writes=["GPOST2"])
    S.op(PL, MS(mhalf2, -0.5), writes=["mhalf"])
    mh[0] = mhalf2
    S.barrier()
    w1_v = w1_d.rearrange("(k p) c -> p k c", p=128)
    w2_v = w2_d.rearrange("(k p) c -> p k c", p=128)
    for blk in range(8):
        S.dma(DMA(W1b3[:, :, blk * 512:(blk + 1) * 512], w1_v[:, :, blk * 512:(blk + 1) * 512], max_dma_last_dim=4096), q="pool", writes=[f"W1b{blk}"])
    for k4 in range(8):
        S.dma(DMA(W2b3[:, 4 * k4:4 * k4 + 4, :], w2_v[:, 4 * k4:4 * k4 + 4, :], max_dma_last_dim=4096), q="pool", writes=[f"W2b{k4}"])
    pf = [PSt[0], PSt[1], PSt[2], PSt[3]]
    pmo_pairs = [((PSt[5], PSt[6]), ("pmo0", "pmo1")), ((PSt[7], PSt[4]), ("pm7", "pt"))]
    NG = NT // 512 if STOP is None else 0
    x1v = lambda gi: x1s[512 * gi:512 * gi + 512, :].rearrange("(p j) d -> p j d", j=4)
    outv = lambda gi: out_d[512 * gi:512 * gi + 512, :].rearrange("(p j) d -> p j d", j=4)

    def ld_sq(gi, j):
        S.dma(DMA(X1g3[:, j, :], x1v(gi)[:, j, :]), writes=[f"X1g{j}"])
        S.op(AC, ACT(jk, X1g3[:, j, :], AF.Square, accum_out=sq[:, j:j + 1]), reads=[f"X1g{j}"], writes=["jk", "sq"])

    for j in range(4 if NG > 0 else 0):
        ld_sq(0, j)
    for gi in range(NG):
        rsqrt_cols(rq, sq, 1.0 / 1024, "sq", "rq")
        for j in range(4):
            S.op(AC, ACT(hb2, X1g3[:, j, :], AF.Copy, scale=rq[:, j:j + 1]), reads=[f"X1g{j}", "rq"], writes=["hb2"])
            for kt in range(8):
                S.op(PE, TR(ptb[:, kt * 128:(kt + 1) * 128], hb2[:, kt * 128:(kt + 1) * 128], identb2), reads=["hb2", "identb2"], writes=["pt"])
            S.op(DV, TT(h2T3[:, :, j * 128:(j + 1) * 128], ptb.rearrange("p (k n) -> p k n", k=8), bc(g2c, 2, 128), ALU.mult), reads=["pt", "g2c"], writes=["h2T"])
        for ft in range(32):
            bk = ft % 4
            for kt in range(8):
                S.op(PE, MM(pf[bk][:, :], W1b3[:, kt, ft * 128:(ft + 1) * 128], h2T3[:, kt, :], kt == 0, kt == 7), reads=["h2T", f"W1b{ft // 4}"], writes=[f"pf{bk}"])
            S.op(AC, ACT(rl[ft % 2], pf[bk][:, :], AF.Relu), reads=[f"pf{bk}"], writes=[f"rl{ft % 2}"])
            S.op(PL if ft % 2 else DV, TT(aT3[:, ft, :], rl[ft % 2], rl[ft % 2], ALU.mult), reads=[f"rl{ft % 2}"], writes=["aT"])
        for j in range(4):
            pmo, pmn = pmo_pairs[j % 2]
            for hfc in range(2):
                for kt in range(32):
                    S.op(PE, MM(pmo[hfc][:, :], aT3[:, kt, j * 128:(j + 1) * 128], W2b3[:, kt, hfc * 512:(hfc + 1) * 512], kt == 0, kt == 31), reads=["aT", f"W2b{kt // 4}"], writes=[pmn[hfc]])
            for hfc in range(2):
                S.op(AC, ACT(jk[:, 0:512], pmo[hfc][:, :], AF.Square, accum_out=so[:, hfc:hfc + 1]), reads=[pmn[hfc]], writes=["jk", f"so{hfc}"])
            S.op(DV, TT(so[:, 0:1], so[:, 0:1], so[:, 1:2], ALU.add), reads=["so0", "so1"], writes=["so0"])
            rsqrt_cols(ro[:, 0:1], so[:, 0:1], 1.0 / 1024, "so0", "ro")
            for hfc in range(2):
                cs_ = slice(hfc * 512, hfc * 512 + 512)
                S.op(DV, STT(tmb[:, cs_], pmo[hfc][:, :], ro[:, 0:1], GPOST2[:, cs_], ALU.mult, ALU.mult), reads=[pmn[hfc], "ro", "GPOST2"], writes=["tmb"])
            S.op(PL, TT(X1g3[:, j, :], X1g3[:, j, :], tmb, ALU.add), reads=["tmb", f"X1g{j}"], writes=[f"X1g{j}"])
            S.dma(DMA(outv(gi)[:, j, :], X1g3[:, j, :]), reads=[f"X1g{j}"], writes=["out"])
            if gi + 1 < NG:
                ld_sq(gi + 1, j)
    S.barrier()
    sems = {k: es.enter_context(nc.semaphore(f"s_{k[0]}_{k[1]}")) for k in sorted(S.semkeys)}
    with nc.Block() as block:
        @block.tensor
        def _(e):
            S.replay("pe", e, sems)

        @block.scalar
        def _(e):
            S.replay("act", e, sems)

        @block.vector
        def _(e):
            S.replay("dve", e, sems)

        @block.gpsimd
        def _(e):
            S.replay("pool", e, sems)

        @block.sync
        def _(e):
            S.replay("sp", e, sems)
    es.close()
    return nc


def _run(x, params, NPRE, NMAIN, n_cores, core_plan):
    nc = build(NPRE, NMAIN)
    NT = NMAIN * 1024
    f = lambda a: np.ascontiguousarray(np.asarray(a, dtype=np.float32))
    base = {
        "norm_mix_pre": f(params["norm_mix_pre"]).reshape(8, 128), "norm_mix_post": f(params["norm_mix_post"]).reshape(1, 1024),
        "w_in": f(params["w_in"]).reshape(1024, 2560), "ret_gn_gain": f(params["ret_gn_gain"]).reshape(1, 512),
        "ssm_lambda_re": f(params["ssm_lambda_re"]).reshape(32, 64), "ssm_lambda_im": f(params["ssm_lambda_im"]).reshape(32, 64),
        "ssm_log_dt": f(params["ssm_log_dt"]).reshape(1, 32),
        "ssm_b_re": f(params["ssm_b_re"]).reshape(32, 64, 16), "ssm_b_im": f(params["ssm_b_im"]).reshape(32, 64, 16),
        "ssm_c_re": f(params["ssm_c_re"]).reshape(32, 16, 64), "ssm_c_im": f(params["ssm_c_im"]).reshape(32, 16, 64),
        "ssm_d": f(params["ssm_d"]).reshape(32, 16),
        "w_glu": f(params["w_glu"]).reshape(512, 1024), "w_out": f(params["w_out"]).reshape(1024, 1024),
        "norm_mlp_pre": f(params["norm_mlp_pre"]).reshape(8, 128), "norm_mlp_post": f(params["norm_mlp_post"]).reshape(1, 1024),
        "w_ff1": f(params["w_ff1"]).reshape(1024, 4096), "w_ff2": f(params["w_ff2"]).reshape(4096, 1024),
    }
    in_maps = []
    for (b, st) in core_plan:
        m = dict(base)
        m["x_own"] = f(x[b, st:st + NT])
        if st > 0:
            m["x_pre"] = f(x[b, st - NPRE * 1024:st])
            pb = np.array([st - NPRE * 1024, st], np.float32)
        else:
            m["x_pre"] = np.zeros((max(NPRE, 1) * 1024, 1024), np.float32)
            pb = np.array([0.0, 0.0], np.float32)
        m["posb"] = np.ascontiguousarray(np.broadcast_to(pb[None, :], (128, 2)))
        in_maps.append(m)
    res = run_bass_kernel_spmd(nc, in_maps, core_ids=list(range(n_cores)))
    out = np.zeros(x.shape, np.float32)
    for i, (b, st) in enumerate(core_plan):
        out[b, st:st + NT] = res.results[i]["out"]
    return out


def kernel(x, **params):
    x = np.asarray(x, dtype=np.float32)
    plan = [(b, h * 4096) for b in range(4) for h in range(2)]
    return _run(x, params, 4, 4, 8, plan)
```

```python
import math
import os
STOP = os.environ.get('KSTOP')
from contextlib import ExitStack

import numpy as np
import concourse.bass as bass
import concourse.mybir as mybir
from concourse.bass_utils import run_bass_kernel_spmd

F32 = mybir.dt.float32
BF16 = mybir.dt.bfloat16
I32 = mybir.dt.int32
AF = mybir.ActivationFunctionType
ALU = mybir.AluOpType

ENGS = ("pe", "act", "dve", "pool", "sp")
SAME_ENG_WAITS = os.environ.get('KSAME', '1') == '1'
EPOCH = 30000
NDMA = 8


class Sched:
    def __init__(self):
        self.prog = {e: [] for e in ENGS}
        self.cnt = {e: 0 for e in ENGS}
        self.seen = {e: {} for e in ENGS}
        self.lw = {}
        self.rd = {}
        self.dma_i = {}
        self.dma_val = {}
        self.semkeys = set()

    def _deps(self, reads, writes):
        d = {}

        def add(x):
            if x is None:
                return
            s, v = x
            if d.get(s, 0) < v:
                d[s] = v

        for k in reads:
            add(self.lw.get(k))
        for k in writes:
            add(self.lw.get(k))
            for r in self.rd.get(k, ()):
                add(r)
        return d

    def _emit(self, eng, d, fn, my, inc):
        waits = []
        for s, v in d.items():
            if self.seen[eng].get(s, 0) < v:
                self.seen[eng][s] = v
                if s[0] == eng and (eng == "pe" or not SAME_ENG_WAITS):
                    continue
                waits.append((s, v))
        self.prog[eng].append((waits, fn, my, inc))
        if my is not None:
            self.semkeys.add(my[0])

    def _update(self, reads, writes, my):
        for k in writes:
            self.lw[k] = my
            self.rd[k] = []
        for k in reads:
            self.rd.setdefault(k, []).append(my)

    def op(self, eng, fn, reads=(), writes=()):
        self.nrec = getattr(self, 'nrec', 0) + 1
        if self.nrec > int(os.environ.get('KMAX', '100000000')):
            return
        d = self._deps(reads, writes)
        c = self.cnt[eng]
        self.cnt[eng] = c + 1
        my = ((eng, c // EPOCH), c % EPOCH + 1)
        self._emit(eng, d, fn, my, 1)
        self._update(reads, writes, my)

    def dma(self, fn, reads=(), writes=(), q="sp", slow=False):
        if slow and os.environ.get('KNOSLOW'):
            return
        self.nrec = getattr(self, 'nrec', 0) + 1
        if self.nrec > int(os.environ.get('KMAX', '100000000')):
            return
        d = self._deps(reads, writes)
        i = self.dma_i.get(q, 0)
        self.dma_i[q] = (i + 1) % NDMA
        sk = ("dma_" + q, i)
        pv = self.dma_val.get(sk, 0)
        if pv > 0:
            d[sk] = max(d.get(sk, 0), pv)
        self.dma_val[sk] = pv + 16
        my = (sk, pv + 16)
        self._emit(q, d, fn, my, 16)
        self._update(reads, writes, my)

    def barrier(self):
        allv = {}
        for e in ENGS:
            c = self.cnt[e]
            if c > 0:
                allv[(e, (c - 1) // EPOCH)] = (c - 1) % EPOCH + 1
        for sk, v in self.dma_val.items():
            if v > 0:
                allv[sk] = v
        for e in ENGS:
            waits = []
            for s, v in allv.items():
                if self.seen[e].get(s, 0) < v:
                    self.seen[e][s] = v
                    waits.append((s, v))
            self.prog[e].append((waits, None, None, 0))
        self.lw = {}
        self.rd = {}

    def replay(self, eng, e, sems):
        for waits, fn, my, inc in self.prog[eng]:
            for s, v in waits:
                e.wait_ge(sems[s], v)
            if fn is not None:
                ins = fn(e)
                ins.then_inc(sems[my[0]], inc)


def bc(ap, axis, n):
    a = ap.unsqueeze(axis)
    shp = list(a.shape)
    shp[axis] = n
    return a.broadcast_to(shp)


class Arena:
    def __init__(self, A):
        self.A = A
        self.Ab = A.bitcast(BF16)
        self.Ai = A.bitcast(I32)
        self.top = 0

    def f32(self, n):
        o = self.top
        self.top += n
        return self.A[:, o:o + n]

    def i32(self, n):
        o = self.top
        self.top += n
        return self.Ai[:, o:o + n]

    def bf(self, n):
        o = self.top
        self.top += (n + 1) // 2
        return self.Ab[:, 2 * o:2 * o + n]


LNG = [math.log(1.0 - math.exp(v)) for v in np.linspace(math.log(1.0 / 32), math.log(1.0 / 512), 4)]
GHEAD = [math.exp(128 * v) for v in LNG]
INVF = (np.float32(10000.0) ** (-(np.arange(64, dtype=np.float32) / np.float32(64)))).astype(np.float32)
TWO_PI = 2.0 * math.pi
C1 = 6.28125
C2 = TWO_PI - C1
AW = 52224


DBG = {}


def build(NPRE, NMAIN):
    nc = bass.Bass("TRN2", target_bir_lowering=False)
    NT = NMAIN * 1024
    dr = lambda n, s, dt=F32, kind="ExternalInput": nc.dram_tensor(n, s, dt, kind=kind).ap()
    x_own = dr("x_own", [NT, 1024])
    x_pre = dr("x_pre", [max(NPRE, 1) * 1024, 1024])
    posb = dr("posb", [128, 2])
    g_pre_d = dr("norm_mix_pre", [8, 128]); g_post_d = dr("norm_mix_post", [1, 1024])
    w_in_d = dr("w_in", [1024, 2560]); ggn_d = dr("ret_gn_gain", [1, 512])
    lre_d = dr("ssm_lambda_re", [32, 64]); lim_d = dr("ssm_lambda_im", [32, 64]); ldt_d = dr("ssm_log_dt", [1, 32])
    bre_d = dr("ssm_b_re", [32, 64, 16]); bim_d = dr("ssm_b_im", [32, 64, 16])
    cre_d = dr("ssm_c_re", [32, 16, 64]); cim_d = dr("ssm_c_im", [32, 16, 64]); sd_d = dr("ssm_d", [32, 16])
    w_glu_d = dr("w_glu", [512, 1024]); w_out_d = dr("w_out", [1024, 1024])
    g2pre_d = dr("norm_mlp_pre", [8, 128]); g2post_d = dr("norm_mlp_post", [1, 1024])
    w1_d = dr("w_ff1", [1024, 4096]); w2_d = dr("w_ff2", [4096, 1024])
    out_d = dr("out", [NT, 1024], kind="ExternalOutput")
    x1s = dr("x1s", [NT, 1024], kind="Internal")

    S = Sched()
    es = ExitStack()
    A_t = es.enter_context(nc.sbuf_tensor("arena", [128, AW], F32))
    PSt = [es.enter_context(nc.psum_tensor(f"ps{i}", [128, 512], F32)) for i in range(8)]
    ar = Arena(A_t)

    def psv(i, n=1):
        assert n == 1
        return PSt[i][:, :]

    def dbc(ap1, n=128):
        return bass.AP(ap1.tensor, ap1.offset, [[0, n]] + [list(d) for d in ap1.ap[1:]])

    DV, AC, PL, PE = "dve", "act", "pool", "pe"
    TT = lambda o, a, b, op: (lambda e: e.tensor_tensor(out=o, in0=a, in1=b, op=op))
    TS = lambda o, a, s1, s2, op0, op1=None: (lambda e: e.tensor_scalar(out=o, in0=a, scalar1=s1, scalar2=s2, op0=op0, op1=op1) if op1 is not None
                                              else e.tensor_scalar(out=o, in0=a, scalar1=s1, scalar2=None, op0=op0))
    STT = lambda o, a, s, b, op0, op1: (lambda e: e.scalar_tensor_tensor(out=o, in0=a, scalar=s, in1=b, op0=op0, op1=op1))
    CP = lambda o, a: (lambda e: e.tensor_copy(out=o, in_=a))
    ACT = lambda o, a, f, **kw: (lambda e: e.activation(out=o, in_=a, func=f, **kw))
    MM = lambda o, l, r, st, sp: (lambda e: e.matmul(o, lhsT=l, rhs=r, start=st, stop=sp))
    TR = lambda o, a, idn: (lambda e: e.transpose(o, a, idn))
    DMA = lambda o, a, **kw: (lambda e: e.dma_start(out=o, in_=a, **kw))
    MS = lambda o, v: (lambda e: e.memset(o, v))

    Win = ar.bf(8 * 2560); Win3 = Win.rearrange("p (k c) -> p k c", k=8)
    Wintra = ar.bf(32 * 128); Wintra3 = Wintra.rearrange("p (g c) -> p g c", g=32)
    Wst = ar.bf(32 * 128); Wst4 = Wst.rearrange("p (g r q) -> p g r q", g=32, r=2)
    Wcr = ar.bf(2 * 16 * 128); Wcr4 = Wcr.rearrange("p (r g c) -> p r g c", r=2, g=16)
    COS = ar.f32(16 * 64); COS3 = COS.rearrange("p (g m) -> p g m", g=16)
    SIN = ar.f32(16 * 64); SIN3 = SIN.rearrange("p (g m) -> p g m", g=16)
    R8 = ar.f32(16)
    CAR = ar.f32(32); CAR3 = CAR.rearrange("p (r g) -> p r g", r=2)
    identb = ar.bf(128); identf = ar.f32(128)
    mask01 = ar.f32(128)
    MASKH = ar.f32(512); MASKH3 = MASKH.rearrange("p (h i) -> p h i", h=4)
    zcol = ar.f32(4); xi = ar.f32(4); xi2 = ar.f32(4); pidx = ar.f32(1); pidx1 = ar.f32(1)
    GPOST = ar.f32(1024); GGN = ar.f32(512)
    gpre = ar.f32(8); g2pre = ar.f32(8)
    invf = ar.f32(64)
    iopc = ar.f32(8)
    posc = ar.f32(2)
    Rst = ar.f32(512); Rst3 = Rst.rearrange("p (h d) -> p h d", h=4)
    Rb = ar.bf(512); Rb3 = Rb.rearrange("p (h d) -> p h d", h=4)
    yrT = ar.bf(4 * 1024); yrT3 = yrT.rearrange("p (k t) -> p k t", k=4)
    Tb = ar.bf(4096)
    Rb2 = ar.bf(512)
    Wglu = ar.bf(4096); Wglu3 = Wglu.rearrange("p (k c) -> p k c", k=4)
    Wout = ar.bf(8192); Wout3 = Wout.rearrange("p (k c) -> p k c", k=8)
    mhalf = ar.f32(8)
    P_BASE = ar.top

    m0 = ar.top
    ioi = ar.i32(128); iof = ar.f32(128)
    S.op(PL, lambda e: e.iota(ioi, pattern=[[1, 128]], base=0, channel_multiplier=-1), writes=["ioi"])
    S.op(DV, CP(iof, ioi), reads=["ioi"], writes=["iof"])
    S.op(DV, lambda e: e.tensor_single_scalar(identf, iof, 0.0, op=ALU.is_equal), reads=["iof"], writes=["identf"])
    S.op(DV, CP(identb, identf), reads=["identf"], writes=["identb"])
    S.op(DV, lambda e: e.tensor_single_scalar(mask01, iof, 0.0, op=ALU.is_ge), reads=["iof"], writes=["mask01"])
    pii = ar.i32(1)
    S.op(PL, lambda e: e.iota(pii, pattern=[[0, 1]], base=0, channel_multiplier=1), writes=["pii"])
    S.op(DV, CP(pidx, pii), reads=["pii"], writes=["pidx"])
    S.op(DV, TS(pidx1, pidx, 1.0, None, ALU.add), reads=["pidx"], writes=["pidx1"])
    pm = ar.f32(1)
    S.op(DV, TS(pm, pidx, -1.0, 127.0, ALU.mult, ALU.add), reads=["pidx"], writes=["pm"])
    gcol = ar.f32(4)
    DH = 128.0 ** -0.5
    for h in range(4):
        S.op(AC, ACT(gcol[:, h:h + 1], pidx1, AF.Exp, scale=-LNG[h]), reads=["pidx1"], writes=["gcol"])
        S.op(AC, ACT(zcol[:, h:h + 1], pm, AF.Exp, scale=LNG[h]), reads=["pm"], writes=["zcol"])
        S.op(AC, ACT(xi[:, h:h + 1], pidx1, AF.Exp, scale=LNG[h]), reads=["pidx1"], writes=["xi"])
    S.op(DV, TS(gcol, gcol, DH, None, ALU.mult), reads=["gcol"], writes=["gcol"])
    S.op(DV, TS(zcol, zcol, DH, None, ALU.mult), reads=["zcol"], writes=["zcol"])
    S.op(DV, TT(xi2, xi, xi, ALU.mult), reads=["xi"], writes=["xi2"])
    for h in range(4):
        S.op(DV, TS(MASKH3[:, h, :], mask01, gcol[:, h:h + 1], None, ALU.mult), reads=["mask01", "gcol"], writes=["MASKH"])
    for j in range(64):
        S.op(PL, MS(invf[:, j:j + 1], float(INVF[j])), writes=["invf"])
    ioci = ar.i32(8)
    S.op(PL, lambda e: e.iota(ioci, pattern=[[128, 8]], base=0, channel_multiplier=1), writes=["ioci"])
    S.op(DV, CP(iopc, ioci), reads=["ioci"], writes=["iopc"])
    S.dma(DMA(posc, posb), writes=["posc"])
    S.dma(DMA(GPOST, dbc(g_post_d)), writes=["GPOST"])
    S.dma(DMA(GGN, dbc(ggn_d)), writes=["GGN"])
    g8 = ar.f32(128)
    for (src, dst, nm) in ((g_pre_d, gpre, "gpre"), (g2pre_d, g2pre, "g2pre")):
        S.dma(DMA(g8[0:8, :], src), writes=["g8"])
        S.op(PE, TR(PSt[0][:, 0:8], g8[0:8, :], identf[0:8, 0:8]), reads=["g8", "identf"], writes=["ps0"])
        S.op(DV, CP(dst, PSt[0][:, 0:8]), reads=["ps0"], writes=[nm])
    w_in_v = w_in_d.rearrange("(k p) c -> p k c", p=128)
    wg_v = w_glu_d.rearrange("(k p) c -> p k c", p=128)
    wo_v = w_out_d.rearrange("(k p) c -> p k c", p=128)
    for kt in range(8):
        S.dma(DMA(Win3[:, kt, :], w_in_v[:, kt, :], max_dma_last_dim=4096), q="pool", writes=["Win"])
    LRE = ar.f32(16); LIM = ar.f32(16); DT = ar.f32(16)
    BRE = ar.f32(256); BIM = ar.f32(256); CRE = ar.f32(256); CIM = ar.f32(256)
    Dcol = ar.f32(32)
    for par in range(2):
        ps_ = slice(64 * par, 64 * par + 64)
        for (dst, src, nm) in ((LRE, lre_d, "LRE"), (LIM, lim_d, "LIM")):
            S.dma(DMA(dst[ps_, :], bass.AP(src.tensor, par * 64, [[1, 64], [128, 16]]), allow_slow_non_contiguous=True), slow=True, writes=[nm])
        S.dma(DMA(DT[ps_, :], bass.AP(ldt_d.tensor, par, [[0, 64], [2, 16]]), allow_slow_non_contiguous=True), slow=True, writes=["DT"])
        for (dst, src, nm) in ((BRE, bre_d, "BRE"), (BIM, bim_d, "BIM")):
            S.dma(DMA(dst[ps_, :].rearrange("p (g c) -> p g c", g=16), bass.AP(src.tensor, par * 1024, [[16, 64], [2048, 16], [1, 16]])), writes=[nm])
    for par in range(2):
        ps_ = slice(64 * par, 64 * par + 64)
        for (dst, src, nm) in ((CRE, cre_d, "CRE"), (CIM, cim_d, "CIM")):
            for gg in range(16):
                S.dma(DMA(dst[ps_, gg * 16:(gg + 1) * 16], bass.AP(src.tensor, par * 1024 + gg * 2048, [[1, 64], [64, 16]]), allow_slow_non_contiguous=True),
                      slow=True, writes=[nm], q=("sp" if nm == "CRE" else "act"))
    for sp_ in range(8):
        S.dma(DMA(Dcol[16 * sp_:16 * sp_ + 16, :], bass.AP(sd_d.tensor, 0, [[1, 16], [16, 32]]), allow_slow_non_contiguous=True), slow=True, writes=["Dcol"])
    cnt = [0]

    def tmp(n):
        return ar.f32(n)

    def vop(fn, reads, writes):
        cnt[0] += 1
        S.op(DV, fn, reads=reads, writes=writes)

    a_ = tmp(16); th = tmp(16); rr = tmp(16); kf = tmp(16); ki = ar.i32(16); red = tmp(16); sn = tmp(16); cs = tmp(16); ab = tmp(16)
    vop(TS(LRE, LRE, -1e-4, None, ALU.min), ["LRE"], ["LRE"])
    S.op(AC, ACT(DT, DT, AF.Exp), reads=["DT"], writes=["DT"])
    vop(TT(a_, LRE, DT, ALU.mult), ["LRE", "DT"], ["a_"])
    vop(TT(th, LIM, DT, ALU.mult), ["LIM", "DT"], ["th"])
    S.op(AC, ACT(rr, a_, AF.Exp), reads=["a_"], writes=["rr"])
    vop(TS(kf, th, 1.0 / TWO_PI, None, ALU.mult), ["th"], ["kf"])
    vop(CP(ki, kf), ["kf"], ["ki"])
    vop(CP(kf, ki), ["ki"], ["kf"])
    vop(STT(red, kf, -C1, th, ALU.mult, ALU.add), ["kf", "th"], ["red"])
    vop(STT(red, kf, -C2, red, ALU.mult, ALU.add), ["kf", "red"], ["red"])
    vop(TS(red, red, math.pi, -math.pi, ALU.min, ALU.max), ["red"], ["red"])
    S.op(AC, ACT(sn, red, AF.Sin), reads=["red"], writes=["sn"])
    vop(TS(ab, red, -1.0, None, ALU.mult), ["red"], ["ab"])
    vop(TT(ab, ab, red, ALU.max), ["ab", "red"], ["ab"])
    vop(TS(ab, ab, -1.0, math.pi / 2, ALU.mult, ALU.add), ["ab"], ["ab"])
    S.op(AC, ACT(cs, ab, AF.Sin), reads=["ab"], writes=["cs"])
    PR = tmp(9 * 16); PI = tmp(9 * 16); QR = tmp(8 * 16); QI = tmp(8 * 16)
    PR3 = PR.rearrange("p (j g) -> p j g", g=16); PI3 = PI.rearrange("p (j g) -> p j g", g=16)
    QR3 = QR.rearrange("p (j g) -> p j g", g=16); QI3 = QI.rearrange("p (j g) -> p j g", g=16)
    t1 = tmp(16); t2 = tmp(16); ivr = tmp(16); ivi = tmp(16); r2 = tmp(16)
    vop(MS(PR3[:, 0, :], 1.0), [], ["PR"]); vop(MS(PI3[:, 0, :], 0.0), [], ["PI"])
    vop(MS(QR3[:, 0, :], 1.0), [], ["QR"]); vop(MS(QI3[:, 0, :], 0.0), [], ["QI"])
    vop(TT(PR3[:, 1, :], rr, cs, ALU.mult), ["rr", "cs", "PR"], ["PR"])
    vop(TT(PI3[:, 1, :], rr, sn, ALU.mult), ["rr", "sn", "PI"], ["PI"])
    vop(TT(r2, rr, rr, ALU.mult), ["rr"], ["r2"])
    vop(lambda e: e.reciprocal(out=r2, in_=r2), ["r2"], ["r2"])
    vop(TT(ivr, PR3[:, 1, :], r2, ALU.mult), ["PR", "r2"], ["ivr"])
    vop(TT(ivi, PI3[:, 1, :], r2, ALU.mult), ["PI", "r2"], ["ivi"])
    vop(TS(ivi, ivi, -1.0, None, ALU.mult), ["ivi"], ["ivi"])

    def cmul(outr, outi, ar_, ai_, br_, bi_, rk, wk):
        vop(TT(t1, ar_, br_, ALU.mult), rk, ["t1"]); vop(TT(t2, ai_, bi_, ALU.mult), rk, ["t2"])
        vop(TT(outr, t1, t2, ALU.subtract), ["t1", "t2"] + wk, wk)
        vop(TT(t1, ar_, bi_, ALU.mult), rk + ["t1"], ["t1"]); vop(TT(t2, ai_, br_, ALU.mult), rk + ["t2"], ["t2"])
        vop(TT(outi, t1, t2, ALU.add), ["t1", "t2"] + wk, wk)

    for j in range(1, 8):
        cmul(PR3[:, j + 1, :], PI3[:, j + 1, :], PR3[:, j, :], PI3[:, j, :], PR3[:, 1, :], PI3[:, 1, :], ["PR", "PI"], ["PR", "PI"])
    vop(CP(QR3[:, 1, :], ivr), ["ivr", "QR"], ["QR"]); vop(CP(QI3[:, 1, :], ivi), ["ivi", "QI"], ["QI"])
    for j in range(1, 7):
        cmul(QR3[:, j + 1, :], QI3[:, j + 1, :], QR3[:, j, :], QI3[:, j, :], ivr, ivi, ["QR", "QI", "ivr", "ivi"], ["QR", "QI"])
    nr = tmp(16); ni = tmp(16); den = tmp(16); lb1 = tmp(16); cr = tmp(16); ci = tmp(16)
    vop(TS(lb1, PR3[:, 1, :], -1.0, None, ALU.add), ["PR"], ["lb1"])
    vop(TT(t1, lb1, LRE, ALU.mult), ["lb1", "LRE"], ["t1"]); vop(TT(t2, PI3[:, 1, :], LIM, ALU.mult), ["PI", "LIM"], ["t2"])
    vop(TT(nr, t1, t2, ALU.add), ["t1", "t2"], ["nr"])
    vop(TT(t1, PI3[:, 1, :], LRE, ALU.mult), ["PI", "LRE", "t1"], ["t1"]); vop(TT(t2, lb1, LIM, ALU.mult), ["lb1", "LIM", "t2"], ["t2"])
    vop(TT(ni, t1, t2, ALU.subtract), ["t1", "t2"], ["ni"])
    vop(TT(t1, LRE, LRE, ALU.mult), ["LRE", "t1"], ["t1"]); vop(TT(t2, LIM, LIM, ALU.mult), ["LIM", "t2"], ["t2"])
    vop(TT(den, t1, t2, ALU.add), ["t1", "t2"], ["den"])
    vop(lambda e: e.reciprocal(out=den, in_=den), ["den"], ["den"])
    vop(TT(cr, nr, den, ALU.mult), ["nr", "den"], ["cr"]); vop(TT(ci, ni, den, ALU.mult), ["ni", "den"], ["ci"])
    BBR = tmp(256); BBI = tmp(256); u1 = tmp(256); u2 = tmp(256)
    v3 = lambda a: a.rearrange("p (g c) -> p g c", g=16)
    b16 = lambda a: bc(a, 2, 16)

    def cmul3(outr, outi, sr, si, xr, xi_, rk, wk, neg_im=False):
        vop(TT(v3(u1), v3(xr), b16(sr), ALU.mult), rk, ["u1"]); vop(TT(v3(u2), v3(xi_), b16(si), ALU.mult), rk, ["u2"])
        vop(TT(outr, v3(u1), v3(u2), ALU.subtract), ["u1", "u2"] + wk, wk)
        vop(TT(v3(u1), v3(xi_), b16(sr), ALU.mult), rk + ["u1"], ["u1"]); vop(TT(v3(u2), v3(xr), b16(si), ALU.mult), rk + ["u2"], ["u2"])
        if neg_im:
            vop(STT(outi, v3(u1), -1.0, v3(u2), ALU.mult, ALU.subtract), ["u1", "u2"] + wk, wk)
        else:
            vop(TT(outi, v3(u1), v3(u2), ALU.add), ["u1", "u2"] + wk, wk)

    cmul3(v3(BBR), v3(BBI), cr, ci, BRE, BIM, ["cr", "ci", "BRE", "BIM"], ["BB"])
    EN = ar.bf(16 * 256); EQ = tmp(16 * 256); G = tmp(16 * 2 * 9 * 16)
    EN5 = EN.rearrange("p (g r s c) -> p g r s c", g=16, r=2, s=8)
    EQ5 = EQ.rearrange("p (g r s c) -> p g r s c", g=16, r=2, s=8)
    G5 = G.rearrange("p (g r j c) -> p g r j c", g=16, r=2, j=9)
    for s_ in range(8):
        cmul3(EN5[:, :, 0, s_, :], EN5[:, :, 1, s_, :], PR3[:, 7 - s_, :], PI3[:, 7 - s_, :], BBR, BBI, ["PR", "PI", "BB"], ["EN"])
        cmul3(EQ5[:, :, 0, s_, :], EQ5[:, :, 1, s_, :], QR3[:, s_, :], QI3[:, s_, :], BBR, BBI, ["QR", "QI", "BB"], ["EQ"])
    for j in range(9):
        cmul3(G5[:, :, 0, j, :], G5[:, :, 1, j, :], PR3[:, j, :], PI3[:, j, :], CRE, CIM, ["PR", "PI", "CRE", "CIM"], ["G"], neg_im=True)
    for ri in range(2):
        vop(CP(Wcr4[:, ri, :, :].rearrange("p g (s c) -> p g s c", s=8), G5[:, :, ri, 1:9, :]), ["G"], ["Wcr"])
    Wst5 = Wst.rearrange("p (g a r q) -> p g a r q", g=16, a=2, r=2)
    ENb5 = EN5
    ptbs = [PSt[5].bitcast(BF16), PSt[4].bitcast(BF16)]
    for g2 in range(16):
        for ri in range(2):
            k = ri
            S.op(PE, TR(ptbs[k][:, 0:128], ENb5[:, g2, ri, :, :].rearrange("p s c -> p (s c)"), identb), reads=["EN", "identb"], writes=[f"psb{k}"])
            S.op(AC, ACT(Wst5[:, g2, :, ri, :], ptbs[k][:, 0:128].rearrange("p (a q) -> p a q", a=2), AF.Copy),
                 reads=[f"psb{k}"], writes=["Wst"])
    si_ = ar.i32(128); sf_ = tmp(128); ri_ = ar.i32(1); rf_ = tmp(1); BLK = tmp(128); wtmp = [tmp(128), tmp(128)]
    S.op(PL, lambda e: e.iota(si_, pattern=[[1, 128]], base=0, channel_multiplier=0), writes=["si_"])
    vop(CP(sf_, si_), ["si_"], ["sf_"])
    vop(TS(sf_, sf_, 1.0 / 16, -0.46875, ALU.mult, ALU.add), ["sf_"], ["sf_"])
    vop(CP(si_, sf_), ["sf_"], ["si_"])
    vop(CP(sf_, si_), ["si_"], ["sf_"])
    vop(TS(rf_, pidx, 1.0 / 16, -0.46875, ALU.mult, ALU.add), ["pidx"], ["rf_"])
    vop(CP(ri_, rf_), ["rf_"], ["ri_"])
    vop(CP(rf_, ri_), ["ri_"], ["rf_"])
    vop(TS(BLK, sf_, rf_[:, 0:1], None, ALU.is_ge), ["sf_", "rf_"], ["BLK"])
    for g in range(32):
        par, g2 = g % 2, g // 2
        ps_ = slice(64 * par, 64 * par + 64)
        k = 2 + g % 2
        for ri in range(2):
            S.op(PE, MM(PSt[k][:, 0:128], EQ5[ps_, g2, ri, :, :].rearrange("p s c -> p (s c)"),
                        G5[ps_, g2, ri, 0:8, :].rearrange("p s c -> p (s c)"), ri == 0, ri == 1),
                 reads=["EQ", "G"], writes=[f"ps{k}"])
        S.op(DV, TT(wtmp[g % 2], PSt[k][:, 0:128], BLK, ALU.mult), reads=[f"ps{k}", "BLK"], writes=[f"wtmp{g % 2}"])
        S.op(DV, STT(Wintra3[:, g, :], identf, Dcol[:, g:g + 1], wtmp[g % 2], ALU.mult, ALU.add), reads=[f"wtmp{g % 2}", "identf", "Dcol"], writes=["Wintra"])
    ur = tmp(16); ui = tmp(16); wr = tmp(16); wi = tmp(16); w2r = tmp(16); w2i = tmp(16); a8 = tmp(16)
    vop(TS(a8, a_, 8.0, None, ALU.mult), ["a_"], ["a8"])
    S.op(AC, ACT(R8, a8, AF.Exp), reads=["a8"], writes=["R8"])
    S.op(AC, ACT(a8, a8, AF.Exp, scale=-1.0), reads=["a8"], writes=["a8"])
    vop(TT(ur, PR3[:, 8, :], a8, ALU.mult), ["PR", "a8"], ["ur"]); vop(TT(ui, PI3[:, 8, :], a8, ALU.mult), ["PI", "a8"], ["ui"])
    vop(CP(COS3[:, :, 0], ur), ["ur"], ["COS"]); vop(CP(SIN3[:, :, 0], ui), ["ui"], ["SIN"])
    vop(CP(wr, ur), ["ur"], ["wr"]); vop(CP(wi, ui), ["ui"], ["wi"])
    e1 = tmp(16 * 32); e2 = tmp(16 * 32)
    for k in range(6):
        n = 1 << k
        e1v = e1[:, 0:16 * n].rearrange("p (g m) -> p g m", g=16); e2v = e2[:, 0:16 * n].rearrange("p (g m) -> p g m", g=16)
        wrb = bc(wr, 2, n); wib = bc(wi, 2, n)
        vop(TT(e1v, COS3[:, :, 0:n], wrb, ALU.mult), ["COS", "wr"], ["e1"]); vop(TT(e2v, SIN3[:, :, 0:n], wib, ALU.mult), ["SIN", "wi"], ["e2"])
        vop(TT(COS3[:, :, n:2 * n], e1v, e2v, ALU.subtract), ["e1", "e2", "COS"], ["COS"])
        vop(TT(e1v, COS3[:, :, 0:n], wib, ALU.mult), ["COS", "wi", "e1"], ["e1"]); vop(TT(e2v, SIN3[:, :, 0:n], wrb, ALU.mult), ["SIN", "wr", "e2"], ["e2"])
        vop(TT(SIN3[:, :, n:2 * n], e1v, e2v, ALU.add), ["e1", "e2", "SIN"], ["SIN"])
        if k < 5:
            cmul(w2r, w2i, wr, wi, wr, wi, ["wr", "wi"], ["w2"])
            vop(CP(wr, w2r), ["w2"], ["wr"]); vop(CP(wi, w2i), ["w2"], ["wi"])
    DBG.update(ur=ur, ui=ui, a8=a8, wr=wr, wi=wi, PR=PR, PI=PI, e1=e1, e2=e2)
    DBG.update(Wintra=Wintra, Wst=Wst, Wcr=Wcr, COS=COS, SIN=SIN, R8=R8, MASKH=MASKH, Win=Win, zcol=zcol, xi=xi, gpre=gpre, invf=invf, identb=identb, GPOST=GPOST)
    S.op(PL, MS(mhalf, -0.5), writes=["mhalf"])
    S.op(PL, MS(Rst, 0.0), writes=["R"]); S.op(PL, MS(Rb, 0.0), writes=["Rb0"]); S.op(PL, MS(Rb2, 0.0), writes=["Rb1"]); S.op(PL, MS(CAR, 0.0), writes=["CAR"])
    S.barrier()
    ar.top = P_BASE
    EPS = 1e-6
    pt_t = PSt[4]
    ptb = pt_t.bitcast(BF16)
    pss, pso, psu = PSt[5], PSt[6], PSt[7]
    mA = ar.top
    hT = ar.bf(8 * 1024); hT3 = hT.rearrange("p (k t) -> p k t", k=8)
    xs = [ar.f32(1024), ar.f32(1024), ar.f32(1024)]
    hb = [ar.bf(1024), ar.bf(1024)]
    junk = ar.bf(1024)
    ssq = ar.f32(8); rstd = ar.f32(8)
    rcos = ar.f32(512); rsin = ar.f32(512); rang = ar.f32(512); rk = ar.f32(512); rki = ar.i32(512); rm = ar.f32(512); posf = ar.f32(8)
    rcos3 = rcos.rearrange("p (c j) -> p c j", c=8); rsin3 = rsin.rearrange("p (c j) -> p c j", c=8)
    qkr = [ar.bf(1024), ar.bf(1024)]
    qkT = [ar.bf(1024), ar.bf(1024)]
    vb = [ar.bf(512), ar.bf(512)]; vz = [ar.bf(512), ar.bf(512)]; sTb = ar.bf(512)
    h4 = lambda a: a.rearrange("p (h d) -> p h d", h=4)
    a8v = lambda a: a.rearrange("p (a d) -> p a d", a=8)
    sTb3 = h4(sTb)
    rt = [ar.f32(512) for _ in range(4)]
    yn = ar.f32(512); sgg = [ar.f32(512), ar.f32(512)]; yr = ar.bf(512)
    bst = ar.f32(24); mv = ar.f32(8); rs = ar.f32(4); vtmp = ar.f32(4)
    mR_end = ar.top
    ar.top = mA
    U8 = ar.bf(4096); U83 = U8.rearrange("p (g n) -> p g n", g=32)
    Y8 = ar.bf(4096); Y83 = Y8.rearrange("p (g n) -> p g n", g=32)
    SPV = ar.bf(2 * 16 * 128); SPV4 = SPV.rearrange("p (r g n) -> p r g n", r=2, g=16)
    PRE = [[ar.f32(256), ar.f32(256)] for _ in range(2)]; SCN = [[ar.f32(256), ar.f32(256)] for _ in range(2)]
    st_ = [[ar.f32(256) for _ in range(4)] for _ in range(2)]
    FUL = [[ar.f32(256), ar.f32(256)] for _ in range(2)]
    ysT = [ar.bf(512), ar.bf(512)]; ys2 = [ar.bf(512), ar.bf(512)]; ys2T = [ar.bf(512), ar.bf(512)]
    sig = ar.f32(512); xs2 = [ar.f32(1024), ar.f32(1024)]; tm = [ar.f32(1024), ar.f32(1024)]
    junk2 = ar.bf(1024); ss2 = [ar.f32(2), ar.f32(2)]; rs2 = [ar.f32(2), ar.f32(2)]
    mS_end = ar.top
    assert max(mR_end, mS_end) <= AW, (mR_end, mS_end)
    v34 = lambda a: a.rearrange("p (g m) -> p g m", g=4)
    Tb4 = Tb.rearrange("p (g s c) -> p g s c", g=32, s=8)
    YT4 = Tb.rearrange("p (s g c) -> p s g c", s=8, g=32)
    YT3 = Tb.rearrange("p (s f) -> p s f", s=8)
    Rbs = [Rb, Rb2]

    mh = [mhalf]

    def rsqrt_cols(dst, src, scale, nm_src, nm_dst):
        n = dst.shape[1]
        S.op(DV, TS(dst, src, scale, EPS, ALU.mult, ALU.add), reads=[nm_src], writes=[nm_dst])
        S.op(PL, TT(dst, dst, mh[0][:, 0:n], ALU.pow), reads=[nm_dst, "mhalf"], writes=[nm_dst])

    rcnt = [0]
    for kt in range(4):
        S.dma(DMA(Wglu3[:, kt, :], wg_v[:, kt, :], max_dma_last_dim=4096), q="pool", writes=["Wglu"])
    for kt in range(8):
        S.dma(DMA(Wout3[:, kt, :], wo_v[:, kt, :], max_dma_last_dim=4096), q="pool", writes=["Wout"])
    for sbi in range((NPRE + NMAIN) if STOP is None else int(STOP)):
        pre = sbi < NPRE
        xsrc = x_pre if pre else x_own
        row0 = 1024 * (sbi if pre else sbi - NPRE)
        pcol = 0 if pre else 1
        xv = xsrc[row0:row0 + 1024, :].rearrange("(n s) d -> n s d", s=8)
        S.op(DV, TS(posf, iopc, posc[:, pcol:pcol + 1], float(row0), ALU.add, ALU.add), reads=["iopc", "posc", "rang"], writes=["posf"])
        rang3 = rang.rearrange("p (c j) -> p c j", c=8)
        S.op(DV, TT(rang3, bc(posf, 2, 64), bc(invf, 1, 8), ALU.mult), reads=["posf", "invf", "rcos"], writes=["rang"])
        S.op(DV, TS(rk, rang, 1.0 / TWO_PI, None, ALU.mult), reads=["rang"], writes=["rk"])
        S.op(DV, CP(rki, rk), reads=["rk"], writes=["rki"])
        S.op(DV, CP(rk, rki), reads=["rki"], writes=["rk"])
        S.op(DV, STT(rm, rk, -C1, rang, ALU.mult, ALU.add), reads=["rk", "rang"], writes=["rm"])
        S.op(DV, STT(rm, rk, -C2, rm, ALU.mult, ALU.add), reads=["rk", "rm"], writes=["rm"])
        S.op(DV, TS(rm, rm, math.pi, -math.pi, ALU.min, ALU.max), reads=["rm"], writes=["rm"])
        S.op(AC, ACT(rsin, rm, AF.Sin), reads=["rm"], writes=["rsin"])
        S.op(DV, TS(rk, rm, -1.0, None, ALU.mult), reads=["rm", "rk"], writes=["rk"])
        S.op(DV, TT(rk, rk, rm, ALU.max), reads=["rm", "rk"], writes=["rk"])
        S.op(DV, TS(rk, rk, -1.0, math.pi / 2, ALU.mult, ALU.add), reads=["rk"], writes=["rk"])
        S.op(AC, ACT(rcos, rk, AF.Sin), reads=["rk"], writes=["rcos"])

        def ht1(s):
            b = s % 3
            S.dma(DMA(xs[b], xsrc[row0 + 128 * s:row0 + 128 * s + 128, :]), writes=[f"xs{b}"])
            S.op(AC, ACT(junk, xs[b], AF.Square, accum_out=ssq[:, s:s + 1]), reads=[f"xs{b}"], writes=["junk", f"ssq{s}"])

        def ht2(s):
            b = s % 2; bx = s % 3
            rsqrt_cols(rstd[:, s:s + 1], ssq[:, s:s + 1], 1.0 / 1024, f"ssq{s}", f"rstd{s}")
            S.op(AC, ACT(hb[b], xs[bx], AF.Copy, scale=rstd[:, s:s + 1]), reads=[f"xs{bx}", f"rstd{s}"], writes=[f"hb{b}"])
            tbk, tnm = (ptb, "pt") if s % 2 == 0 else (PSt[5].bitcast(BF16), "pss")
            for kt in range(8):
                S.op(PE, TR(tbk[:, kt * 128:(kt + 1) * 128], hb[b][:, kt * 128:(kt + 1) * 128], identb), reads=[f"hb{b}", "identb"], writes=[tnm])
            S.op(DV, TT(hT3[:, :, 128 * s:128 * s + 128], tbk.rearrange("p (k n) -> p k n", k=8), bc(gpre, 2, 128), ALU.mult), reads=[tnm, "gpre"], writes=["hT"])

        if os.environ.get("KSEQ_H"):
            for s in range(8):
                ht1(s); ht2(s)
        else:
            ht1(0); ht1(1)
            for s in range(8):
                if s + 2 < 8:
                    ht1(s + 2)
                ht2(s)

        tkc = lambda c: slice(128 * c, 128 * c + 128)
        if pre:
            def A(c):
                bk, bv = (PSt[0], PSt[1]) if c % 2 == 0 else (PSt[2], PSt[3])
                nk, nv = ("ps0", "ps1") if c % 2 == 0 else ("ps2", "ps3")
                for (bank, col0, nm) in ((bk, 512, nk), (bv, 1024, nv)):
                    for kt in range(8):
                        S.op(PE, MM(bank[:, :], hT3[:, kt, tkc(c)], Win3[:, kt, col0:col0 + 512], kt == 0, kt == 7), reads=["hT", "Win"], writes=[nm])

            def B(c):
                b = c % 2
                bk, bv = (PSt[0], PSt[1]) if c % 2 == 0 else (PSt[2], PSt[3])
                nk, nv = ("ps0", "ps1") if c % 2 == 0 else ("ps2", "ps3")
                cosb = bc(rcos3[:, c, :], 1, 4); sinb = bc(rsin3[:, c, :], 1, 4)
                xq = bk[:, :].rearrange("p (h a j) -> p h a j", h=4, a=2)
                x1 = xq[:, :, 0, :]; x2 = xq[:, :, 1, :]
                r3 = [t_[:, 0:256].rearrange("p (h j) -> p h j", h=4) for t_ in rt]
                S.op(DV, TT(r3[0], x1, cosb, ALU.mult), reads=[nk, "rcos"], writes=["rt0"])
                S.op(DV, TT(r3[1], x2, sinb, ALU.mult), reads=[nk, "rsin"], writes=["rt1"])
                S.op(DV, TT(r3[2], x1, sinb, ALU.mult), reads=[nk, "rsin"], writes=["rt2"])
                S.op(DV, TT(r3[3], x2, cosb, ALU.mult), reads=[nk, "rcos"], writes=["rt3"])
                q4 = qkr[b].rearrange("p (a h j) -> p a h j", a=8, h=2)
                S.op(PL, TT(q4[:, 4:8, 0, :], r3[0], r3[1], ALU.subtract), reads=["rt0", "rt1"], writes=[f"qkr{b}"])
                S.op(PL, TT(q4[:, 4:8, 1, :], r3[2], r3[3], ALU.add), reads=["rt2", "rt3"], writes=[f"qkr{b}"])
                S.op(DV, TT(h4(vz[b]), h4(bv[:, :]), bc(zcol, 2, 128), ALU.mult), reads=[nv, "zcol"], writes=[f"vz{b}"])

            def ST(c):
                b = c % 2
                psu3 = h4(psu[:, :])
                for h in range(4):
                    S.op(PE, MM(psu3[:, h, :], a8v(qkr[b])[:, 4 + h, :], h4(vz[b])[:, h, :], True, True), reads=[f"qkr{b}", f"vz{b}"], writes=["psu"])
                for h in range(4):
                    S.op(DV, STT(Rst3[:, h, :], Rst3[:, h, :], GHEAD[h], psu3[:, h, :], ALU.mult, ALU.add), reads=["psu", "R"], writes=["R"])

            A(0)
            for c in range(8):
                if c + 1 < 8:
                    A(c + 1)
                B(c)
                ST(c)
            nb = rcnt[0] % 2
            S.op(AC, ACT(Rbs[nb], Rst, AF.Copy), reads=["R"], writes=[f"Rb{nb}"])
        else:

            def A1(c):
                for (col0, off) in ((0, 0), (512, 512)):
                    for kt in range(8):
                        S.op(PE, MM(PSt[off // 512][:, :], hT3[:, kt, tkc(c)], Win3[:, kt, col0:col0 + 512], kt == 0, kt == 7), reads=["hT", "Win"], writes=["ps0" if off == 0 else "ps1"])

            def A2(c):
                for (col0, off) in ((1024, 0), (1536, 512)):
                    for kt in range(8):
                        S.op(PE, MM(PSt[2 + off // 512][:, :], hT3[:, kt, tkc(c)], Win3[:, kt, col0:col0 + 512], kt == 0, kt == 7), reads=["hT", "Win"], writes=["ps2" if off == 0 else "ps3"])

            def B1(c):
                b = c % 2
                cosb = bc(rcos3[:, c, :], 1, 4); sinb = bc(rsin3[:, c, :], 1, 4)
                q4 = qkr[b].rearrange("p (a t j) -> p a t j", a=8, t=2)
                for half in range(2):
                    nm = "ps0" if half == 0 else "ps1"
                    xq = PSt[half][:, :].rearrange("p (a t j) -> p a t j", a=4, t=2)
                    x1 = xq[:, :, 0, :]; x2 = xq[:, :, 1, :]
                    r3 = [t_[:, 256 * half:256 * half + 256].rearrange("p (a j) -> p a j", a=4) for t_ in rt]
                    S.op(DV, TT(r3[0], x1, cosb, ALU.mult), reads=[nm, "rcos"], writes=[f"rt0{half}"])
                    S.op(DV, TT(r3[1], x2, sinb, ALU.mult), reads=[nm, "rsin"], writes=[f"rt1{half}"])
                    S.op(DV, TT(r3[2], x1, sinb, ALU.mult), reads=[nm, "rsin"], writes=[f"rt2{half}"])
                    S.op(DV, TT(r3[3], x2, cosb, ALU.mult), reads=[nm, "rcos"], writes=[f"rt3{half}"])
                    S.op(PL, TT(q4[:, 4 * half:4 * half + 4, 0, :], r3[0], r3[1], ALU.subtract), reads=[f"rt0{half}", f"rt1{half}"], writes=[f"qkr{b}"])
                    S.op(PL, TT(q4[:, 4 * half:4 * half + 4, 1, :], r3[2], r3[3], ALU.add), reads=[f"rt2{half}", f"rt3{half}"], writes=[f"qkr{b}"])

            def B2(c):
                b = c % 2
                for h in range(4):
                    S.op(AC, ACT(h4(vz[b])[:, h, :], h4(PSt[2][:, :])[:, h, :], AF.Copy, scale=zcol[:, h:h + 1]), reads=["ps2", "zcol"], writes=[f"vz{b}"])
                S.op(AC, ACT(vb[b], PSt[2][:, :], AF.Copy), reads=["ps2"], writes=[f"vb{b}"])
                S.op(AC, ACT(sgg[b], PSt[3][:, :], AF.Silu), reads=["ps3"], writes=[f"sgg{b}"])
                S.op(PL, TT(sgg[b], sgg[b], GGN, ALU.mult), reads=[f"sgg{b}", "GGN"], writes=[f"sgg{b}"])

            def Tqk(c):
                b = c % 2
                for a in range(8):
                    S.op(PE, TR(ptb[:, a * 128:(a + 1) * 128], a8v(qkr[b])[:, a, :], identb), reads=[f"qkr{b}", "identb"], writes=["pt"])
                S.op(AC, ACT(qkT[b], ptb[:, :], AF.Copy), reads=["pt"], writes=[f"qkT{b}"])

            def SC(c):
                b = c % 2
                pss3 = h4(pss[:, :]); qT = a8v(qkT[b])
                for h in range(4):
                    S.op(PE, MM(pss3[:, h, :], qT[:, 4 + h, :], qT[:, h, :], True, True), reads=[f"qkT{b}"], writes=["pss"])
                S.op(DV, TT(sTb3, pss3, MASKH3, ALU.mult), reads=["pss", "MASKH"], writes=["sTb"])

            def ST(c):
                b = c % 2
                psu3 = h4(psu[:, :])
                for h in range(4):
                    S.op(PE, MM(psu3[:, h, :], a8v(qkr[b])[:, 4 + h, :], h4(vz[b])[:, h, :], True, True), reads=[f"qkr{b}", f"vz{b}"], writes=["psu"])
                for h in range(4):
                    S.op(DV, STT(Rst3[:, h, :], Rst3[:, h, :], GHEAD[h], psu3[:, h, :], ALU.mult, ALU.add), reads=["psu", "R"], writes=["R"])
                nb = (rcnt[0] + c + 1) % 2
                S.op(AC, ACT(Rbs[nb], Rst, AF.Copy), reads=["R"], writes=[f"Rb{nb}"])

            def OUT(c):
                b = c % 2
                cb = (rcnt[0] + c) % 2
                pso3 = h4(pso[:, :]); qT = a8v(qkT[b])
                for h in range(4):
                    S.op(PE, MM(pso3[:, h, :], sTb3[:, h, :], h4(vb[b])[:, h, :], True, False), reads=["sTb", f"vb{b}"], writes=["pso"])
                    S.op(PE, MM(pso3[:, h, :], qT[:, h, :], h4(Rbs[cb])[:, h, :], False, True), reads=[f"qkT{b}", f"Rb{cb}"], writes=["pso"])

            def GN(c):
                b = c % 2
                pso3 = h4(pso[:, :])
                bst3 = bst.rearrange("p (h k) -> p h k", h=4); mv3 = mv.rearrange("p (h k) -> p h k", h=4)
                for h in range(4):
                    S.op(DV, lambda e, h=h: e.bn_stats(out=bst3[:, h, :], in_=pso3[:, h, :]), reads=["pso"], writes=["bst"])
                    S.op(DV, lambda e, h=h: e.bn_aggr(out=mv3[:, h, :], in_=bst3[:, h, :]), reads=["bst"], writes=["mv"])
                S.op(DV, TT(vtmp, mv3[:, :, 1], xi2, ALU.mult), reads=["mv", "xi2"], writes=["vtmp"])
                rsqrt_cols(rs, vtmp, 1.0, "vtmp", "rs")
                S.op(DV, TT(rs, rs, xi, ALU.mult), reads=["rs", "xi"], writes=["rs"])
                yn3 = h4(yn)
                for h in range(4):
                    S.op(DV, TS(yn3[:, h, :], pso3[:, h, :], mv3[:, h, 0:1], rs[:, h:h + 1], ALU.subtract, ALU.mult), reads=["pso", "mv", "rs"], writes=["yn"])
                S.op(PL, TT(yr, yn, sgg[b], ALU.mult), reads=["yn", f"sgg{b}"], writes=["yr"])

            def Tyr(c):
                for h in range(4):
                    S.op(PE, TR(ptb[:, h * 128:(h + 1) * 128], yr[:, h * 128:(h + 1) * 128], identb), reads=["yr", "identb"], writes=["pt"])
                S.op(AC, ACT(yrT3[:, :, tkc(c)], ptb[:, 0:512].rearrange("p (h i) -> p h i", h=4), AF.Copy), reads=["pt"], writes=["yrT"])

            if os.environ.get("KSEQ_R"):
                for c in range(8):
                    A1(c); A2(c); B1(c); B2(c); Tqk(c); SC(c); ST(c); OUT(c); GN(c); Tyr(c)
            else:
                A1(0); A2(0); B1(0); B2(0)
                for c in range(8):
                    Tqk(c)
                    if c + 1 < 8:
                        A1(c + 1)
                    if c > 0:
                        Tyr(c - 1)
                    SC(c); ST(c)
                    if c + 1 < 8:
                        B1(c + 1)
                        A2(c + 1)
                    OUT(c)
                    if c + 1 < 8:
                        B2(c + 1)
                    GN(c)
                Tyr(7)
        rcnt[0] += (0 if pre else 8)
        for s in range(8):
            bank, nm = (pss, "pss") if s % 2 == 0 else (pso, "pso")
            for kt in range(8):
                S.op(PE, MM(bank[:, :], hT3[:, kt, s::8], Win3[:, kt, 2048:2560], kt == 0, kt == 7), reads=["hT", "Win"], writes=[nm])
            S.op(AC, ACT(Tb4[:, :, s, :], bank[:, :].rearrange("p (g c) -> p g c", g=32), AF.Copy), reads=[nm], writes=["T"])
        S.barrier()
        Tflat = Tb.rearrange("p (g f) -> p g f", g=32)
        ptb2 = PSt[5].bitcast(BF16)
        for gq in range(4):
            tb_, nm = (ptb, "pt") if gq % 2 == 0 else (ptb2, "pss")
            for j in range(8):
                S.op(PE, TR(tb_[:, j * 128:(j + 1) * 128], Tflat[:, 8 * gq + j, :], identb), reads=["T", "identb"], writes=[nm])
            S.op(AC if gq % 2 else DV, (ACT(U83[:, 8 * gq:8 * gq + 8, :], tb_.rearrange("p (g n) -> p g n", g=8), AF.Copy) if gq % 2 else
                                        CP(U83[:, 8 * gq:8 * gq + 8, :], tb_.rearrange("p (g n) -> p g n", g=8))), reads=[nm], writes=["U8"])
        its = [(hf, gb) for hf in range(2) for gb in range(4)]

        def Mm(it):
            hf, gb = its[it]; p = it % 2
            nsl = slice(64 * hf, 64 * hf + 64)
            psr3 = PSt[2 * p][:, 0:256].rearrange("p (g m) -> p g m", g=4); psi3 = PSt[2 * p + 1][:, 0:256].rearrange("p (g m) -> p g m", g=4)
            for j in range(8):
                g = 8 * gb + j; par = j % 2; slot = j // 2
                ps_ = slice(64 * par, 64 * par + 64)
                S.op(PE, MM(psr3[ps_, slot, :], Wst4[:, g, 0, :], U83[:, g, nsl], True, True), reads=["U8", "Wst"], writes=[f"ps{2 * p}"])
                S.op(PE, MM(psi3[ps_, slot, :], Wst4[:, g, 1, :], U83[:, g, nsl], True, True), reads=["U8", "Wst"], writes=[f"ps{2 * p + 1}"])

        def R1(it):
            hf, gb = its[it]; p = it % 2
            g2s = slice(4 * gb, 4 * gb + 4)
            psr3 = PSt[2 * p][:, 0:256].rearrange("p (g m) -> p g m", g=4); psi3 = PSt[2 * p + 1][:, 0:256].rearrange("p (g m) -> p g m", g=4)
            cosv = COS3[:, g2s, :]; sinv = SIN3[:, g2s, :]
            nr_, ni_ = f"ps{2 * p}", f"ps{2 * p + 1}"
            S.op(DV, TT(v34(st_[p][0]), psr3, cosv, ALU.mult), reads=[nr_, "COS"], writes=[f"st{p}0"])
            S.op(DV, TT(v34(st_[p][1]), psi3, sinv, ALU.mult), reads=[ni_, "SIN"], writes=[f"st{p}1"])
            S.op(DV, TT(v34(st_[p][2]), psi3, cosv, ALU.mult), reads=[ni_, "COS"], writes=[f"st{p}2"])
            S.op(DV, TT(v34(st_[p][3]), psr3, sinv, ALU.mult), reads=[nr_, "SIN"], writes=[f"st{p}3"])
            S.op(PL, TT(PRE[p][0], st_[p][0], st_[p][1], ALU.add), reads=[f"st{p}0", f"st{p}1"], writes=[f"PRE{p}0"])
            S.op(PL, TT(PRE[p][1], st_[p][2], st_[p][3], ALU.subtract), reads=[f"st{p}2", f"st{p}3"], writes=[f"PRE{p}1"])

        def SCAN(it):
            hf, gb = its[it]; p = it % 2
            g2s = slice(4 * gb, 4 * gb + 4)
            if not pre:
                for ri in range(2):
                    S.op(AC, ACT(SPV4[:, ri, g2s, 64 * hf:64 * hf + 1], CAR3[:, ri, g2s].unsqueeze(2), AF.Copy), reads=[f"CAR{gb}"], writes=["SPV"])
            for ri in range(2):
                for slot in range(4):
                    g2 = 4 * gb + slot
                    S.op(DV, lambda e, ri=ri, slot=slot, g2=g2, p=p: e.tensor_tensor_scan(
                        out=v34(SCN[p][ri])[:, slot, :], data0=R8[:, g2:g2 + 1].broadcast_to([128, 64]), data1=v34(PRE[p][ri])[:, slot, :],
                        initial=CAR3[:, ri, g2:g2 + 1], op0=ALU.mult, op1=ALU.add), reads=[f"PRE{p}{ri}", f"CAR{gb}", "R8"], writes=[f"SCN{p}{ri}"])

        def R2(it):
            hf, gb = its[it]; p = it % 2
            g2s = slice(4 * gb, 4 * gb + 4)
            cl = slice(63, 64) if pre else slice(0, 64)
            cosv = COS3[:, g2s, cl]; sinv = SIN3[:, g2s, cl]
            w = lambda a: v34(a)[:, :, cl]
            S.op(DV, TT(w(st_[p][0]), w(SCN[p][0]), cosv, ALU.mult), reads=[f"SCN{p}0", "COS"], writes=[f"st{p}0"])
            S.op(DV, TT(w(st_[p][1]), w(SCN[p][1]), sinv, ALU.mult), reads=[f"SCN{p}1", "SIN"], writes=[f"st{p}1"])
            S.op(DV, TT(w(st_[p][2]), w(SCN[p][1]), cosv, ALU.mult), reads=[f"SCN{p}1", "COS"], writes=[f"st{p}2"])
            S.op(DV, TT(w(st_[p][3]), w(SCN[p][0]), sinv, ALU.mult), reads=[f"SCN{p}0", "SIN"], writes=[f"st{p}3"])
            S.op(PL, TT(w(FUL[p][0]), w(st_[p][0]), w(st_[p][1]), ALU.subtract), reads=[f"st{p}0", f"st{p}1"], writes=[f"FUL{p}0"])
            S.op(PL, TT(w(FUL[p][1]), w(st_[p][2]), w(st_[p][3]), ALU.add), reads=[f"st{p}2", f"st{p}3"], writes=[f"FUL{p}1"])
            for ri in range(2):
                S.op(PL, CP(CAR3[:, ri, g2s], v34(FUL[p][ri])[:, :, 63]), reads=[f"FUL{p}{ri}", f"SCN{p}0", f"SCN{p}1", "SPV"], writes=[f"CAR{gb}"])
                if not pre:
                    S.op(AC, ACT(SPV4[:, ri, g2s, 64 * hf + 1:64 * hf + 64], v34(FUL[p][ri])[:, :, 0:63], AF.Copy), reads=[f"FUL{p}{ri}"], writes=["SPV"])

        if os.environ.get("KSEQ_S"):
            for it in range(8):
                Mm(it); R1(it); SCAN(it); R2(it)
        else:
            Mm(0); Mm(1); R1(0)
            for it in range(8):
                if it + 1 < 8:
                    R1(it + 1)
                SCAN(it)
                R2(it)
                if it + 2 < 8:
                    Mm(it + 2)
        if not pre:
            for gq in range(8):
                py, nm = (PSt[0], "ps0") if gq % 2 == 0 else (PSt[1], "ps1")
                py3 = py[:, :].rearrange("p (g n) -> p g n", g=4)
                for j in range(4):
                    g = 4 * gq + j; par = g % 2; g2 = g // 2
                    ps_ = slice(64 * par, 64 * par + 64)
                    S.op(PE, MM(py3[:, j, :], Wintra3[:, g, :], U83[:, g, :], True, False), reads=["U8", "Wintra"], writes=[nm])
                    S.op(PE, MM(py3[:, j, :], Wcr4[ps_, 0, g2, :], SPV4[ps_, 0, g2, :], False, False), reads=["SPV", "Wcr"], writes=[nm])
                    S.op(PE, MM(py3[:, j, :], Wcr4[ps_, 1, g2, :], SPV4[ps_, 1, g2, :], False, True), reads=["SPV", "Wcr"], writes=[nm])
                S.op(AC, ACT(Y83[:, 4 * gq:4 * gq + 4, :], py3, AF.Gelu_apprx_tanh), reads=[nm], writes=["Y8"])
            for gq in range(4):
                tb_, nm = (ptb, "pt") if gq % 2 == 0 else (ptb2, "pss")
                for j in range(8):
                    S.op(PE, TR(tb_[:, j * 128:(j + 1) * 128], Y83[:, 8 * gq + j, :], identb), reads=["Y8", "identb"], writes=[nm])
                S.op(DV, CP(YT4[:, :, 8 * gq:8 * gq + 8, :].rearrange("p s g c -> p g s c"), tb_.rearrange("p (g s c) -> p g s c", g=8, s=8)), reads=[nm, "U8"], writes=["T"])
            pg0, pg1 = PSt[2], PSt[3]
            pms = [(PSt[0], PSt[1], "ps0", "ps1"), (PSt[6], PSt[7], "pso", "psu")]

            def T1(s):
                b = s % 2
                for kt in range(4):
                    S.op(PE, TR(ptb[:, kt * 128:(kt + 1) * 128], YT3[:, s, kt * 128:(kt + 1) * 128], identb), reads=["T", "identb"], writes=["pt"])
                S.op(AC, ACT(ysT[b], ptb[:, 0:512], AF.Copy), reads=["pt"], writes=[f"ysT{b}"])

            def GLU(s):
                b = s % 2
                ysT3 = ysT[b].rearrange("p (k n) -> p k n", k=4)
                for hfc, bank, nm in ((0, pg0, "ps2"), (1, pg1, "ps3")):
                    for kt in range(4):
                        S.op(PE, MM(bank[:, :], ysT3[:, kt, :], Wglu3[:, kt, hfc * 512:(hfc + 1) * 512], kt == 0, kt == 3), reads=[f"ysT{b}", "Wglu"], writes=[nm])
                S.op(AC, ACT(sig, pg1[:, :], AF.Sigmoid), reads=["ps3"], writes=["sig"])
                S.op(DV, TT(ys2[b], pg0[:, :], sig, ALU.mult), reads=["ps2", "sig"], writes=[f"ys2{b}"])

            def T2(s):
                b = s % 2
                for kt in range(4):
                    S.op(PE, TR(ptb2[:, kt * 128:(kt + 1) * 128], ys2[b][:, kt * 128:(kt + 1) * 128], identb), reads=[f"ys2{b}", "identb"], writes=["pss"])
                S.op(AC, ACT(ys2T[b], ptb2[:, 0:512], AF.Copy), reads=["pss"], writes=[f"ys2T{b}"])

            def WO(s):
                b = s % 2
                pm0, pm1, n0, n1 = pms[b]
                y2T3 = ys2T[b].rearrange("p (k n) -> p k n", k=4)
                for hfc, bank, nm in ((0, pm0, n0), (1, pm1, n1)):
                    for kt in range(8):
                        lhs = yrT3[:, kt, s::8] if kt < 4 else y2T3[:, kt - 4, :]
                        S.op(PE, MM(bank[:, :], lhs, Wout3[:, kt, hfc * 512:(hfc + 1) * 512], kt == 0, kt == 7), reads=["yrT", f"ys2T{b}", "Wout"], writes=[nm])
                S.dma(DMA(xs2[b], xv[:, s, :]), writes=[f"xs2{b}"])
                for hfc, bank, nm in ((0, pm0, n0), (1, pm1, n1)):
                    S.op(AC, ACT(junk2[:, 0:512], bank[:, :], AF.Square, accum_out=ss2[b][:, hfc:hfc + 1]), reads=[nm], writes=["junk2", f"ss2{b}{hfc}"])
                S.op(DV, TT(ss2[b][:, 0:1], ss2[b][:, 0:1], ss2[b][:, 1:2], ALU.add), reads=[f"ss2{b}0", f"ss2{b}1"], writes=[f"ss2{b}0"])
                rsqrt_cols(rs2[b][:, 0:1], ss2[b][:, 0:1], 1.0 / 1024, f"ss2{b}0", f"rs2{b}")
                for hfc, bank, nm in ((0, pm0, n0), (1, pm1, n1)):
                    cs_ = slice(hfc * 512, hfc * 512 + 512)
                    S.op(DV, STT(tm[b][:, cs_], bank[:, :], rs2[b][:, 0:1], GPOST[:, cs_], ALU.mult, ALU.mult), reads=[nm, f"rs2{b}", "GPOST"], writes=[f"tm{b}"])
                S.op(PL, TT(xs2[b], xs2[b], tm[b], ALU.add), reads=[f"tm{b}", f"xs2{b}"], writes=[f"xs2{b}"])
                r0 = 1024 * (sbi - NPRE)
                S.dma(DMA(x1s[r0:r0 + 1024, :].rearrange("(n s) d -> n s d", s=8)[:, s, :], xs2[b]), reads=[f"xs2{b}"], writes=["x1s"])

            if os.environ.get("KSEQ_W"):
                for s in range(8):
                    T1(s); GLU(s); T2(s); WO(s)
            else:
                T1(0); GLU(0); T1(1); T2(0)
                for s in range(1, 8):
                    GLU(s)
                    WO(s - 1)
                    if s + 1 < 8:
                        T1(s + 1)
                    T2(s)
                WO(7)
        S.barrier()

    ar.top = 0
    W1b = ar.bf(8 * 4096); W1b3 = W1b.rearrange("p (k c) -> p k c", k=8)
    W2b = ar.bf(32 * 1024); W2b3 = W2b.rearrange("p (k c) -> p k c", k=32)
    identb2 = ar.bf(128); GPOST2 = ar.f32(1024); g2c = ar.f32(8)
    X1g = ar.f32(4096); X1g3 = X1g.rearrange("p (j d) -> p j d", j=4)
    h2T = ar.bf(8 * 512); h2T3 = h2T.rearrange("p (k t) -> p k t", k=8)
    aT = ar.bf(32 * 512); aT3 = aT.rearrange("p (k t) -> p k t", k=32)
    hb2s = [ar.bf(1024), ar.bf(1024)]; jk = ar.bf(1024); rl = [ar.f32(512), ar.f32(512)]; tmb = ar.f32(1024); sq = ar.f32(4); rq = ar.f32(4); so = ar.f32(2); ro = ar.f32(2)
    mhalf2 = ar.f32(8)
    assert ar.top <= AW, ar.top
    stgB = aT.bitcast(F32) if False else None
    cst = X1g
    S.op(DV, CP(cst[:, 0:8], g2pre), reads=[], writes=["cst"])
    S.op(DV, CP(jk[:, 0:128], identb), reads=[], writes=["jk"])
    S.barrier()
    S.op(DV, CP(g2c, cst[:, 0:8]), reads=["cst"], writes=["g2c"])
    S.op(DV, CP(identb2, jk[:, 0:128]), reads=["jk"], writes=["identb2"])
    S.dma(DMA(GPOST2, dbc(g2post_d)), writes=["GPOST2"])
    S.op(PL, MS(mhalf2, -0.5), writes=["mhalf"])
    mh[0] = mhalf2
    S.barrier()
    w1_v = w1_d.rearrange("(k p) c -> p k c", p=128)
    w2_v = w2_d.rearrange("(k p) c -> p k c", p=128)
    for blk in range(8):
        S.dma(DMA(W1b3[:, :, blk * 512:(blk + 1) * 512], w1_v[:, :, blk * 512:(blk + 1) * 512], max_dma_last_dim=4096), q="pool", writes=[f"W1b{blk}"])
    for k4 in range(8):
        S.dma(DMA(W2b3[:, 4 * k4:4 * k4 + 4, :], w2_v[:, 4 * k4:4 * k4 + 4, :], max_dma_last_dim=4096), q="pool", writes=[f"W2b{k4}"])
    pf = [PSt[0], PSt[1], PSt[2], PSt[3]]
    pmo_pairs = [((PSt[5], PSt[6]), ("pmo0", "pmo1")), ((PSt[7], PSt[4]), ("pm7", "pt"))]
    NG = NT // 512 if STOP is None else 0
    x1v = lambda gi: x1s[512 * gi:512 * gi + 512, :].rearrange("(p j) d -> p j d", j=4)
    outv = lambda gi: out_d[512 * gi:512 * gi + 512, :].rearrange("(p j) d -> p j d", j=4)

    def ld_sq(gi, j):
        S.dma(DMA(X1g3[:, j, :], x1v(gi)[:, j, :]), writes=[f"X1g{j}"])
        S.op(AC, ACT(jk, X1g3[:, j, :], AF.Square, accum_out=sq[:, j:j + 1]), reads=[f"X1g{j}"], writes=["jk", "sq"])

    for j in range(4 if NG > 0 else 0):
        ld_sq(0, j)
    for gi in range(NG):
        rsqrt_cols(rq, sq, 1.0 / 1024, "sq", "rq")
        for j in range(4):
            hb2 = hb2s[j % 2]
            S.op(AC, ACT(hb2, X1g3[:, j, :], AF.Copy, scale=rq[:, j:j + 1]), reads=[f"X1g{j}", "rq"], writes=[f"hb2{j % 2}"])
            for kt in range(8):
                S.op(PE, TR(ptb[:, kt * 128:(kt + 1) * 128], hb2[:, kt * 128:(kt + 1) * 128], identb2), reads=[f"hb2{j % 2}", "identb2"], writes=["pt"])
            S.op(DV, TT(h2T3[:, :, j * 128:(j + 1) * 128], ptb.rearrange("p (k n) -> p k n", k=8), bc(g2c, 2, 128), ALU.mult), reads=["pt", "g2c"], writes=["h2T"])
        for ft in range(32):
            bk = ft % 4
            for kt in range(8):
                S.op(PE, MM(pf[bk][:, :], W1b3[:, kt, ft * 128:(ft + 1) * 128], h2T3[:, kt, :], kt == 0, kt == 7), reads=["h2T", f"W1b{ft // 4}"], writes=[f"pf{bk}"])
            S.op(AC, ACT(rl[ft % 2], pf[bk][:, :], AF.Relu), reads=[f"pf{bk}"], writes=[f"rl{ft % 2}"])
            S.op(PL if ft % 2 else DV, TT(aT3[:, ft, :], rl[ft % 2], rl[ft % 2], ALU.mult), reads=[f"rl{ft % 2}"], writes=["aT"])
        for j in range(4):
            pmo, pmn = pmo_pairs[j % 2]
            for hfc in range(2):
                for kt in range(32):
                    S.op(PE, MM(pmo[hfc][:, :], aT3[:, kt, j * 128:(j + 1) * 128], W2b3[:, kt, hfc * 512:(hfc + 1) * 512], kt == 0, kt == 31), reads=["aT", f"W2b{kt // 4}"], writes=[pmn[hfc]])
            for hfc in range(2):
                S.op(AC, ACT(jk[:, 0:512], pmo[hfc][:, :], AF.Square, accum_out=so[:, hfc:hfc + 1]), reads=[pmn[hfc]], writes=["jk", f"so{hfc}"])
            S.op(DV, TT(so[:, 0:1], so[:, 0:1], so[:, 1:2], ALU.add), reads=["so0", "so1"], writes=["so0"])
            rsqrt_cols(ro[:, 0:1], so[:, 0:1], 1.0 / 1024, "so0", "ro")
            for hfc in range(2):
                cs_ = slice(hfc * 512, hfc * 512 + 512)
                S.op(DV, STT(tmb[:, cs_], pmo[hfc][:, :], ro[:, 0:1], GPOST2[:, cs_], ALU.mult, ALU.mult), reads=[pmn[hfc], "ro", "GPOST2"], writes=["tmb"])
            S.op(PL, TT(X1g3[:, j, :], X1g3[:, j, :], tmb, ALU.add), reads=["tmb", f"X1g{j}"], writes=[f"X1g{j}"])
            S.dma(DMA(outv(gi)[:, j, :], X1g3[:, j, :]), reads=[f"X1g{j}"], writes=["out"])
            if gi + 1 < NG:
                ld_sq(gi + 1, j)
    S.barrier()
    sems = {k: es.enter_context(nc.semaphore(f"s_{k[0]}_{k[1]}")) for k in sorted(S.semkeys)}
    with nc.Block() as block:
        @block.tensor
        def _(e):
            S.replay("pe", e, sems)

        @block.scalar
        def _(e):
            S.replay("act", e, sems)

        @block.vector
        def _(e):
            S.replay("dve", e, sems)

        @block.gpsimd
        def _(e):
            S.replay("pool", e, sems)

        @block.sync
        def _(e):
            S.replay("sp", e, sems)
    es.close()
    return nc


def _run(x, params, NPRE, NMAIN, n_cores, core_plan):
    nc = build(NPRE, NMAIN)
    NT = NMAIN * 1024
    f = lambda a: np.ascontiguousarray(np.asarray(a, dtype=np.float32))
    base = {
        "norm_mix_pre": f(params["norm_mix_pre"]).reshape(8, 128), "norm_mix_post": f(params["norm_mix_post"]).reshape(1, 1024),
        "w_in": f(params["w_in"]).reshape(1024, 2560), "ret_gn_gain": f(params["ret_gn_gain"]).reshape(1, 512),
        "ssm_lambda_re": f(params["ssm_lambda_re"]).reshape(32, 64), "ssm_lambda_im": f(params["ssm_lambda_im"]).reshape(32, 64),
        "ssm_log_dt": f(params["ssm_log_dt"]).reshape(1, 32),
        "ssm_b_re": f(params["ssm_b_re"]).reshape(32, 64, 16), "ssm_b_im": f(params["ssm_b_im"]).reshape(32, 64, 16),
        "ssm_c_re": f(params["ssm_c_re"]).reshape(32, 16, 64), "ssm_c_im": f(params["ssm_c_im"]).reshape(32, 16, 64),
        "ssm_d": f(params["ssm_d"]).reshape(32, 16),
        "w_glu": f(params["w_glu"]).reshape(512, 1024), "w_out": f(params["w_out"]).reshape(1024, 1024),
        "norm_mlp_pre": f(params["norm_mlp_pre"]).reshape(8, 128), "norm_mlp_post": f(params["norm_mlp_post"]).reshape(1, 1024),
        "w_ff1": f(params["w_ff1"]).reshape(1024, 4096), "w_ff2": f(params["w_ff2"]).reshape(4096, 1024),
    }
    in_maps = []
    for (b, st) in core_plan:
        m = dict(base)
        m["x_own"] = f(x[b, st:st + NT])
        if st > 0:
            m["x_pre"] = f(x[b, st - NPRE * 1024:st])
            pb = np.array([st - NPRE * 1024, st], np.float32)
        else:
            m["x_pre"] = np.zeros((max(NPRE, 1) * 1024, 1024), np.float32)
            pb = np.array([0.0, 0.0], np.float32)
        m["posb"] = np.ascontiguousarray(np.broadcast_to(pb[None, :], (128, 2)))
        in_maps.append(m)
    res = run_bass_kernel_spmd(nc, in_maps, core_ids=list(range(n_cores)))
    out = np.zeros(x.shape, np.float32)
    for i, (b, st) in enumerate(core_plan):
        out[b, st:st + NT] = res.results[i]["out"]
    return out


def kernel(x, **params):
    x = np.asarray(x, dtype=np.float32)
    plan = [(b, h * 4096) for b in range(4) for h in range(2)]
    return _run(x, params, 4, 4, 8, plan)
```

```python
import math
import os
STOP = os.environ.get('KSTOP')
from contextlib import ExitStack

import numpy as np
import concourse.bass as bass
import concourse.mybir as mybir
from concourse.bass_utils import run_bass_kernel_spmd

F32 = mybir.dt.float32
BF16 = mybir.dt.bfloat16
I32 = mybir.dt.int32
AF = mybir.ActivationFunctionType
ALU = mybir.AluOpType

ENGS = ("pe", "act", "dve", "pool", "sp")
SAME_ENG_WAITS = os.environ.get('KSAME', '1') == '1'
EPOCH = 30000
NDMA = 8


class Sched:
    def __init__(self):
        self.prog = {e: [] for e in ENGS}
        self.cnt = {e: 0 for e in ENGS}
        self.seen = {e: {} for e in ENGS}
        self.lw = {}
        self.rd = {}
        self.dma_i = {}
        self.dma_val = {}
        self.semkeys = set()

    def _deps(self, reads, writes):
        d = {}

        def add(x):
            if x is None:
                return
            s, v = x
            if d.get(s, 0) < v:
                d[s] = v

        for k in reads:
            add(self.lw.get(k))
        for k in writes:
            add(self.lw.get(k))
            for r in self.rd.get(k, ()):
                add(r)
        return d

    def _emit(self, eng, d, fn, my, inc):
        waits = []
        for s, v in d.items():
            if self.seen[eng].get(s, 0) < v:
                self.seen[eng][s] = v
                if s[0] == eng and (eng == "pe" or not SAME_ENG_WAITS):
                    continue
                waits.append((s, v))
        self.prog[eng].append((waits, fn, my, inc))
        if my is not None:
            self.semkeys.add(my[0])

    def _update(self, reads, writes, my):
        for k in writes:
            self.lw[k] = my
            self.rd[k] = []
        for k in reads:
            self.rd.setdefault(k, []).append(my)

    def op(self, eng, fn, reads=(), writes=()):
        self.nrec = getattr(self, 'nrec', 0) + 1
        if self.nrec > int(os.environ.get('KMAX', '100000000')):
            return
        d = self._deps(reads, writes)
        c = self.cnt[eng]
        self.cnt[eng] = c + 1
        my = ((eng, c // EPOCH), c % EPOCH + 1)
        self._emit(eng, d, fn, my, 1)
        self._update(reads, writes, my)

    def dma(self, fn, reads=(), writes=(), q="sp", slow=False):
        if slow and os.environ.get('KNOSLOW'):
            return
        self.nrec = getattr(self, 'nrec', 0) + 1
        if self.nrec > int(os.environ.get('KMAX', '100000000')):
            return
        d = self._deps(reads, writes)
        i = self.dma_i.get(q, 0)
        self.dma_i[q] = (i + 1) % NDMA
        sk = ("dma_" + q, i)
        pv = self.dma_val.get(sk, 0)
        if pv > 0:
            d[sk] = max(d.get(sk, 0), pv)
        self.dma_val[sk] = pv + 16
        my = (sk, pv + 16)
        self._emit(q, d, fn, my, 16)
        self._update(reads, writes, my)

    def barrier(self):
        allv = {}
        for e in ENGS:
            c = self.cnt[e]
            if c > 0:
                allv[(e, (c - 1) // EPOCH)] = (c - 1) % EPOCH + 1
        for sk, v in self.dma_val.items():
            if v > 0:
                allv[sk] = v
        for e in ENGS:
            waits = []
            for s, v in allv.items():
                if self.seen[e].get(s, 0) < v:
                    self.seen[e][s] = v
                    waits.append((s, v))
            self.prog[e].append((waits, None, None, 0))
        self.lw = {}
        self.rd = {}

    def replay(self, eng, e, sems):
        for waits, fn, my, inc in self.prog[eng]:
            for s, v in waits:
                e.wait_ge(sems[s], v)
            if fn is not None:
                ins = fn(e)
                ins.then_inc(sems[my[0]], inc)


def bc(ap, axis, n):
    a = ap.unsqueeze(axis)
    shp = list(a.shape)
    shp[axis] = n
    return a.broadcast_to(shp)


class Arena:
    def __init__(self, A):
        self.A = A
        self.Ab = A.bitcast(BF16)
        self.Ai = A.bitcast(I32)
        self.top = 0

    def f32(self, n):
        o = self.top
        self.top += n
        return self.A[:, o:o + n]

    def i32(self, n):
        o = self.top
        self.top += n
        return self.Ai[:, o:o + n]

    def bf(self, n):
        o = self.top
        self.top += (n + 1) // 2
        return self.Ab[:, 2 * o:2 * o + n]


LNG = [math.log(1.0 - math.exp(v)) for v in np.linspace(math.log(1.0 / 32), math.log(1.0 / 512), 4)]
GHEAD = [math.exp(128 * v) for v in LNG]
INVF = (np.float32(10000.0) ** (-(np.arange(64, dtype=np.float32) / np.float32(64)))).astype(np.float32)
TWO_PI = 2.0 * math.pi
C1 = 6.28125
C2 = TWO_PI - C1
AW = 52224


DBG = {}


def build(NPRE, NMAIN):
    nc = bass.Bass("TRN2", target_bir_lowering=False)
    NT = NMAIN * 1024
    dr = lambda n, s, dt=F32, kind="ExternalInput": nc.dram_tensor(n, s, dt, kind=kind).ap()
    x_own = dr("x_own", [NT, 1024])
    x_pre = dr("x_pre", [max(NPRE, 1) * 1024, 1024])
    posb = dr("posb", [128, 2])
    g_pre_d = dr("norm_mix_pre", [8, 128]); g_post_d = dr("norm_mix_post", [1, 1024])
    w_in_d = dr("w_in", [1024, 2560]); ggn_d = dr("ret_gn_gain", [1, 512])
    lre_d = dr("ssm_lambda_re", [32, 64]); lim_d = dr("ssm_lambda_im", [32, 64]); ldt_d = dr("ssm_log_dt", [1, 32])
    bre_d = dr("ssm_b_re", [32, 64, 16]); bim_d = dr("ssm_b_im", [32, 64, 16])
    cre_d = dr("ssm_c_re", [32, 16, 64]); cim_d = dr("ssm_c_im", [32, 16, 64]); sd_d = dr("ssm_d", [32, 16])
    w_glu_d = dr("w_glu", [512, 1024]); w_out_d = dr("w_out", [1024, 1024])
    g2pre_d = dr("norm_mlp_pre", [8, 128]); g2post_d = dr("norm_mlp_post", [1, 1024])
    w1_d = dr("w_ff1", [1024, 4096]); w2_d = dr("w_ff2", [4096, 1024])
    out_d = dr("out", [NT, 1024], kind="ExternalOutput")
    x1s = dr("x1s", [NT, 1024], kind="Internal")

    S = Sched()
    es = ExitStack()
    A_t = es.enter_context(nc.sbuf_tensor("arena", [128, AW], F32))
    PSt = [es.enter_context(nc.psum_tensor(f"ps{i}", [128, 512], F32)) for i in range(8)]
    ar = Arena(A_t)

    def psv(i, n=1):
        assert n == 1
        return PSt[i][:, :]

    def dbc(ap1, n=128):
        return bass.AP(ap1.tensor, ap1.offset, [[0, n]] + [list(d) for d in ap1.ap[1:]])

    DV, AC, PL, PE = "dve", "act", "pool", "pe"
    TT = lambda o, a, b, op: (lambda e: e.tensor_tensor(out=o, in0=a, in1=b, op=op))
    TS = lambda o, a, s1, s2, op0, op1=None: (lambda e: e.tensor_scalar(out=o, in0=a, scalar1=s1, scalar2=s2, op0=op0, op1=op1) if op1 is not None
                                              else e.tensor_scalar(out=o, in0=a, scalar1=s1, scalar2=None, op0=op0))
    STT = lambda o, a, s, b, op0, op1: (lambda e: e.scalar_tensor_tensor(out=o, in0=a, scalar=s, in1=b, op0=op0, op1=op1))
    CP = lambda o, a: (lambda e: e.tensor_copy(out=o, in_=a))
    ACT = lambda o, a, f, **kw: (lambda e: e.activation(out=o, in_=a, func=f, **kw))
    MM = lambda o, l, r, st, sp: (lambda e: e.matmul(o, lhsT=l, rhs=r, start=st, stop=sp))
    TR = lambda o, a, idn: (lambda e: e.transpose(o, a, idn))
    DMA = lambda o, a, **kw: (lambda e: e.dma_start(out=o, in_=a, **kw))
    MS = lambda o, v: (lambda e: e.memset(o, v))

    Win = ar.bf(8 * 2560); Win3 = Win.rearrange("p (k c) -> p k c", k=8)
    Wintra = ar.bf(32 * 128); Wintra3 = Wintra.rearrange("p (g c) -> p g c", g=32)
    Wst = ar.bf(32 * 128); Wst4 = Wst.rearrange("p (g r q) -> p g r q", g=32, r=2)
    Wcr = ar.bf(2 * 16 * 128); Wcr4 = Wcr.rearrange("p (r g c) -> p r g c", r=2, g=16)
    COS = ar.f32(16 * 64); COS3 = COS.rearrange("p (g m) -> p g m", g=16)
    SIN = ar.f32(16 * 64); SIN3 = SIN.rearrange("p (g m) -> p g m", g=16)
    R8 = ar.f32(16)
    CAR = ar.f32(32); CAR3 = CAR.rearrange("p (r g) -> p r g", r=2)
    identb = ar.bf(128); identf = ar.f32(128)
    mask01 = ar.f32(128)
    MASKH = ar.f32(512); MASKH3 = MASKH.rearrange("p (h i) -> p h i", h=4)
    zcol = ar.f32(4); xi = ar.f32(4); xi2 = ar.f32(4); pidx = ar.f32(1); pidx1 = ar.f32(1)
    GPOST = ar.f32(1024); GGN = ar.f32(512)
    gpre = ar.f32(8); g2pre = ar.f32(8)
    invf = ar.f32(64)
    iopc = ar.f32(8)
    posc = ar.f32(2)
    Rst = ar.f32(512); Rst3 = Rst.rearrange("p (h d) -> p h d", h=4)
    Rb = ar.bf(512); Rb3 = Rb.rearrange("p (h d) -> p h d", h=4)
    yrT = ar.bf(4 * 1024); yrT3 = yrT.rearrange("p (k t) -> p k t", k=4)
    Tb = ar.bf(4096)
    Rb2 = ar.bf(512)
    Wglu = ar.bf(4096); Wglu3 = Wglu.rearrange("p (k c) -> p k c", k=4)
    Wout = ar.bf(8192); Wout3 = Wout.rearrange("p (k c) -> p k c", k=8)
    mhalf = ar.f32(8)
    P_BASE = ar.top

    m0 = ar.top
    ioi = ar.i32(128); iof = ar.f32(128)
    S.op(PL, lambda e: e.iota(ioi, pattern=[[1, 128]], base=0, channel_multiplier=-1), writes=["ioi"])
    S.op(DV, CP(iof, ioi), reads=["ioi"], writes=["iof"])
    S.op(DV, lambda e: e.tensor_single_scalar(identf, iof, 0.0, op=ALU.is_equal), reads=["iof"], writes=["identf"])
    S.op(DV, CP(identb, identf), reads=["identf"], writes=["identb"])
    S.op(DV, lambda e: e.tensor_single_scalar(mask01, iof, 0.0, op=ALU.is_ge), reads=["iof"], writes=["mask01"])
    pii = ar.i32(1)
    S.op(PL, lambda e: e.iota(pii, pattern=[[0, 1]], base=0, channel_multiplier=1), writes=["pii"])
    S.op(DV, CP(pidx, pii), reads=["pii"], writes=["pidx"])
    S.op(DV, TS(pidx1, pidx, 1.0, None, ALU.add), reads=["pidx"], writes=["pidx1"])
    pm = ar.f32(1)
    S.op(DV, TS(pm, pidx, -1.0, 127.0, ALU.mult, ALU.add), reads=["pidx"], writes=["pm"])
    gcol = ar.f32(4)
    DH = 128.0 ** -0.5
    for h in range(4):
        S.op(AC, ACT(gcol[:, h:h + 1], pidx1, AF.Exp, scale=-LNG[h]), reads=["pidx1"], writes=["gcol"])
        S.op(AC, ACT(zcol[:, h:h + 1], pm, AF.Exp, scale=LNG[h]), reads=["pm"], writes=["zcol"])
        S.op(AC, ACT(xi[:, h:h + 1], pidx1, AF.Exp, scale=LNG[h]), reads=["pidx1"], writes=["xi"])
    S.op(DV, TS(gcol, gcol, DH, None, ALU.mult), reads=["gcol"], writes=["gcol"])
    S.op(DV, TS(zcol, zcol, DH, None, ALU.mult), reads=["zcol"], writes=["zcol"])
    S.op(DV, TT(xi2, xi, xi, ALU.mult), reads=["xi"], writes=["xi2"])
    for h in range(4):
        S.op(DV, TS(MASKH3[:, h, :], mask01, gcol[:, h:h + 1], None, ALU.mult), reads=["mask01", "gcol"], writes=["MASKH"])
    for j in range(64):
        S.op(PL, MS(invf[:, j:j + 1], float(INVF[j])), writes=["invf"])
    ioci = ar.i32(8)
    S.op(PL, lambda e: e.iota(ioci, pattern=[[128, 8]], base=0, channel_multiplier=1), writes=["ioci"])
    S.op(DV, CP(iopc, ioci), reads=["ioci"], writes=["iopc"])
    S.dma(DMA(posc, posb), writes=["posc"])
    S.dma(DMA(GPOST, dbc(g_post_d)), writes=["GPOST"])
    S.dma(DMA(GGN, dbc(ggn_d)), writes=["GGN"])
    g8 = ar.f32(128)
    for (src, dst, nm) in ((g_pre_d, gpre, "gpre"), (g2pre_d, g2pre, "g2pre")):
        S.dma(DMA(g8[0:8, :], src), writes=["g8"])
        S.op(PE, TR(PSt[0][:, 0:8], g8[0:8, :], identf[0:8, 0:8]), reads=["g8", "identf"], writes=["ps0"])
        S.op(DV, CP(dst, PSt[0][:, 0:8]), reads=["ps0"], writes=[nm])
    w_in_v = w_in_d.rearrange("(k p) c -> p k c", p=128)
    wg_v = w_glu_d.rearrange("(k p) c -> p k c", p=128)
    wo_v = w_out_d.rearrange("(k p) c -> p k c", p=128)
    for kt in range(8):
        S.dma(DMA(Win3[:, kt, :], w_in_v[:, kt, :], max_dma_last_dim=4096), q="pool", writes=["Win"])
    LRE = ar.f32(16); LIM = ar.f32(16); DT = ar.f32(16)
    BRE = ar.f32(256); BIM = ar.f32(256); CRE = ar.f32(256); CIM = ar.f32(256)
    Dcol = ar.f32(32)
    for par in range(2):
        ps_ = slice(64 * par, 64 * par + 64)
        for (dst, src, nm) in ((LRE, lre_d, "LRE"), (LIM, lim_d, "LIM")):
            S.dma(DMA(dst[ps_, :], bass.AP(src.tensor, par * 64, [[1, 64], [128, 16]]), allow_slow_non_contiguous=True), slow=True, writes=[nm])
        S.dma(DMA(DT[ps_, :], bass.AP(ldt_d.tensor, par, [[0, 64], [2, 16]]), allow_slow_non_contiguous=True), slow=True, writes=["DT"])
        for (dst, src, nm) in ((BRE, bre_d, "BRE"), (BIM, bim_d, "BIM")):
            S.dma(DMA(dst[ps_, :].rearrange("p (g c) -> p g c", g=16), bass.AP(src.tensor, par * 1024, [[16, 64], [2048, 16], [1, 16]])), writes=[nm])
    for par in range(2):
        ps_ = slice(64 * par, 64 * par + 64)
        for (dst, src, nm) in ((CRE, cre_d, "CRE"), (CIM, cim_d, "CIM")):
            for gg in range(16):
                S.dma(DMA(dst[ps_, gg * 16:(gg + 1) * 16], bass.AP(src.tensor, par * 1024 + gg * 2048, [[1, 64], [64, 16]]), allow_slow_non_contiguous=True),
                      slow=True, writes=[nm], q=("sp" if nm == "CRE" else "act"))
    for sp_ in range(8):
        S.dma(DMA(Dcol[16 * sp_:16 * sp_ + 16, :], bass.AP(sd_d.tensor, 0, [[1, 16], [16, 32]]), allow_slow_non_contiguous=True), slow=True, writes=["Dcol"])
    cnt = [0]

    def tmp(n):
        return ar.f32(n)

    def vop(fn, reads, writes):
        cnt[0] += 1
        S.op(DV, fn, reads=reads, writes=writes)

    a_ = tmp(16); th = tmp(16); rr = tmp(16); kf = tmp(16); ki = ar.i32(16); red = tmp(16); sn = tmp(16); cs = tmp(16); ab = tmp(16)
    vop(TS(LRE, LRE, -1e-4, None, ALU.min), ["LRE"], ["LRE"])
    S.op(AC, ACT(DT, DT, AF.Exp), reads=["DT"], writes=["DT"])
    vop(TT(a_, LRE, DT, ALU.mult), ["LRE", "DT"], ["a_"])
    vop(TT(th, LIM, DT, ALU.mult), ["LIM", "DT"], ["th"])
    S.op(AC, ACT(rr, a_, AF.Exp), reads=["a_"], writes=["rr"])
    vop(TS(kf, th, 1.0 / TWO_PI, None, ALU.mult), ["th"], ["kf"])
    vop(CP(ki, kf), ["kf"], ["ki"])
    vop(CP(kf, ki), ["ki"], ["kf"])
    vop(STT(red, kf, -C1, th, ALU.mult, ALU.add), ["kf", "th"], ["red"])
    vop(STT(red, kf, -C2, red, ALU.mult, ALU.add), ["kf", "red"], ["red"])
    vop(TS(red, red, math.pi, -math.pi, ALU.min, ALU.max), ["red"], ["red"])
    S.op(AC, ACT(sn, red, AF.Sin), reads=["red"], writes=["sn"])
    vop(TS(ab, red, -1.0, None, ALU.mult), ["red"], ["ab"])
    vop(TT(ab, ab, red, ALU.max), ["ab", "red"], ["ab"])
    vop(TS(ab, ab, -1.0, math.pi / 2, ALU.mult, ALU.add), ["ab"], ["ab"])
    S.op(AC, ACT(cs, ab, AF.Sin), reads=["ab"], writes=["cs"])
    PR = tmp(9 * 16); PI = tmp(9 * 16); QR = tmp(8 * 16); QI = tmp(8 * 16)
    PR3 = PR.rearrange("p (j g) -> p j g", g=16); PI3 = PI.rearrange("p (j g) -> p j g", g=16)
    QR3 = QR.rearrange("p (j g) -> p j g", g=16); QI3 = QI.rearrange("p (j g) -> p j g", g=16)
    t1 = tmp(16); t2 = tmp(16); ivr = tmp(16); ivi = tmp(16); r2 = tmp(16)
    vop(MS(PR3[:, 0, :], 1.0), [], ["PR"]); vop(MS(PI3[:, 0, :], 0.0), [], ["PI"])
    vop(MS(QR3[:, 0, :], 1.0), [], ["QR"]); vop(MS(QI3[:, 0, :], 0.0), [], ["QI"])
    vop(TT(PR3[:, 1, :], rr, cs, ALU.mult), ["rr", "cs", "PR"], ["PR"])
    vop(TT(PI3[:, 1, :], rr, sn, ALU.mult), ["rr", "sn", "PI"], ["PI"])
    vop(TT(r2, rr, rr, ALU.mult), ["rr"], ["r2"])
    vop(lambda e: e.reciprocal(out=r2, in_=r2), ["r2"], ["r2"])
    vop(TT(ivr, PR3[:, 1, :], r2, ALU.mult), ["PR", "r2"], ["ivr"])
    vop(TT(ivi, PI3[:, 1, :], r2, ALU.mult), ["PI", "r2"], ["ivi"])
    vop(TS(ivi, ivi, -1.0, None, ALU.mult), ["ivi"], ["ivi"])

    def cmul(outr, outi, ar_, ai_, br_, bi_, rk, wk):
        vop(TT(t1, ar_, br_, ALU.mult), rk, ["t1"]); vop(TT(t2, ai_, bi_, ALU.mult), rk, ["t2"])
        vop(TT(outr, t1, t2, ALU.subtract), ["t1", "t2"] + wk, wk)
        vop(TT(t1, ar_, bi_, ALU.mult), rk + ["t1"], ["t1"]); vop(TT(t2, ai_, br_, ALU.mult), rk + ["t2"], ["t2"])
        vop(TT(outi, t1, t2, ALU.add), ["t1", "t2"] + wk, wk)

    for j in range(1, 8):
        cmul(PR3[:, j + 1, :], PI3[:, j + 1, :], PR3[:, j, :], PI3[:, j, :], PR3[:, 1, :], PI3[:, 1, :], ["PR", "PI"], ["PR", "PI"])
    vop(CP(QR3[:, 1, :], ivr), ["ivr", "QR"], ["QR"]); vop(CP(QI3[:, 1, :], ivi), ["ivi", "QI"], ["QI"])
    for j in range(1, 7):
        cmul(QR3[:, j + 1, :], QI3[:, j + 1, :], QR3[:, j, :], QI3[:, j, :], ivr, ivi, ["QR", "QI", "ivr", "ivi"], ["QR", "QI"])
    nr = tmp(16); ni = tmp(16); den = tmp(16); lb1 = tmp(16); cr = tmp(16); ci = tmp(16)
    vop(TS(lb1, PR3[:, 1, :], -1.0, None, ALU.add), ["PR"], ["lb1"])
    vop(TT(t1, lb1, LRE, ALU.mult), ["lb1", "LRE"], ["t1"]); vop(TT(t2, PI3[:, 1, :], LIM, ALU.mult), ["PI", "LIM"], ["t2"])
    vop(TT(nr, t1, t2, ALU.add), ["t1", "t2"], ["nr"])
    vop(TT(t1, PI3[:, 1, :], LRE, ALU.mult), ["PI", "LRE", "t1"], ["t1"]); vop(TT(t2, lb1, LIM, ALU.mult), ["lb1", "LIM", "t2"], ["t2"])
    vop(TT(ni, t1, t2, ALU.subtract), ["t1", "t2"], ["ni"])
    vop(TT(t1, LRE, LRE, ALU.mult), ["LRE", "t1"], ["t1"]); vop(TT(t2, LIM, LIM, ALU.mult), ["LIM", "t2"], ["t2"])
    vop(TT(den, t1, t2, ALU.add), ["t1", "t2"], ["den"])
    vop(lambda e: e.reciprocal(out=den, in_=den), ["den"], ["den"])
    vop(TT(cr, nr, den, ALU.mult), ["nr", "den"], ["cr"]); vop(TT(ci, ni, den, ALU.mult), ["ni", "den"], ["ci"])
    BBR = tmp(256); BBI = tmp(256); u1 = tmp(256); u2 = tmp(256)
    v3 = lambda a: a.rearrange("p (g c) -> p g c", g=16)
    b16 = lambda a: bc(a, 2, 16)

    def cmul3(outr, outi, sr, si, xr, xi_, rk, wk, neg_im=False):
        vop(TT(v3(u1), v3(xr), b16(sr), ALU.mult), rk, ["u1"]); vop(TT(v3(u2), v3(xi_), b16(si), ALU.mult), rk, ["u2"])
        vop(TT(outr, v3(u1), v3(u2), ALU.subtract), ["u1", "u2"] + wk, wk)
        vop(TT(v3(u1), v3(xi_), b16(sr), ALU.mult), rk + ["u1"], ["u1"]); vop(TT(v3(u2), v3(xr), b16(si), ALU.mult), rk + ["u2"], ["u2"])
        if neg_im:
            vop(STT(outi, v3(u1), -1.0, v3(u2), ALU.mult, ALU.subtract), ["u1", "u2"] + wk, wk)
        else:
            vop(TT(outi, v3(u1), v3(u2), ALU.add), ["u1", "u2"] + wk, wk)

    cmul3(v3(BBR), v3(BBI), cr, ci, BRE, BIM, ["cr", "ci", "BRE", "BIM"], ["BB"])
    EN = ar.bf(16 * 256); EQ = tmp(16 * 256); G = tmp(16 * 2 * 9 * 16)
    EN5 = EN.rearrange("p (g r s c) -> p g r s c", g=16, r=2, s=8)
    EQ5 = EQ.rearrange("p (g r s c) -> p g r s c", g=16, r=2, s=8)
    G5 = G.rearrange("p (g r j c) -> p g r j c", g=16, r=2, j=9)
    for s_ in range(8):
        cmul3(EN5[:, :, 0, s_, :], EN5[:, :, 1, s_, :], PR3[:, 7 - s_, :], PI3[:, 7 - s_, :], BBR, BBI, ["PR", "PI", "BB"], ["EN"])
        cmul3(EQ5[:, :, 0, s_, :], EQ5[:, :, 1, s_, :], QR3[:, s_, :], QI3[:, s_, :], BBR, BBI, ["QR", "QI", "BB"], ["EQ"])
    for j in range(9):
        cmul3(G5[:, :, 0, j, :], G5[:, :, 1, j, :], PR3[:, j, :], PI3[:, j, :], CRE, CIM, ["PR", "PI", "CRE", "CIM"], ["G"], neg_im=True)
    for ri in range(2):
        vop(CP(Wcr4[:, ri, :, :].rearrange("p g (s c) -> p g s c", s=8), G5[:, :, ri, 1:9, :]), ["G"], ["Wcr"])
    Wst5 = Wst.rearrange("p (g a r q) -> p g a r q", g=16, a=2, r=2)
    ENb5 = EN5
    ptbs = [PSt[5].bitcast(BF16), PSt[4].bitcast(BF16)]
    for g2 in range(16):
        for ri in range(2):
            k = ri
            S.op(PE, TR(ptbs[k][:, 0:128], ENb5[:, g2, ri, :, :].rearrange("p s c -> p (s c)"), identb), reads=["EN", "identb"], writes=[f"psb{k}"])
            S.op(AC, ACT(Wst5[:, g2, :, ri, :], ptbs[k][:, 0:128].rearrange("p (a q) -> p a q", a=2), AF.Copy),
                 reads=[f"psb{k}"], writes=["Wst"])
    si_ = ar.i32(128); sf_ = tmp(128); ri_ = ar.i32(1); rf_ = tmp(1); BLK = tmp(128); wtmp = [tmp(128), tmp(128)]
    S.op(PL, lambda e: e.iota(si_, pattern=[[1, 128]], base=0, channel_multiplier=0), writes=["si_"])
    vop(CP(sf_, si_), ["si_"], ["sf_"])
    vop(TS(sf_, sf_, 1.0 / 16, -0.46875, ALU.mult, ALU.add), ["sf_"], ["sf_"])
    vop(CP(si_, sf_), ["sf_"], ["si_"])
    vop(CP(sf_, si_), ["si_"], ["sf_"])
    vop(TS(rf_, pidx, 1.0 / 16, -0.46875, ALU.mult, ALU.add), ["pidx"], ["rf_"])
    vop(CP(ri_, rf_), ["rf_"], ["ri_"])
    vop(CP(rf_, ri_), ["ri_"], ["rf_"])
    vop(TS(BLK, sf_, rf_[:, 0:1], None, ALU.is_ge), ["sf_", "rf_"], ["BLK"])
    for g in range(32):
        par, g2 = g % 2, g // 2
        ps_ = slice(64 * par, 64 * par + 64)
        k = 2 + g % 2
        for ri in range(2):
            S.op(PE, MM(PSt[k][:, 0:128], EQ5[ps_, g2, ri, :, :].rearrange("p s c -> p (s c)"),
                        G5[ps_, g2, ri, 0:8, :].rearrange("p s c -> p (s c)"), ri == 0, ri == 1),
                 reads=["EQ", "G"], writes=[f"ps{k}"])
        S.op(DV, TT(wtmp[g % 2], PSt[k][:, 0:128], BLK, ALU.mult), reads=[f"ps{k}", "BLK"], writes=[f"wtmp{g % 2}"])
        S.op(DV, STT(Wintra3[:, g, :], identf, Dcol[:, g:g + 1], wtmp[g % 2], ALU.mult, ALU.add), reads=[f"wtmp{g % 2}", "identf", "Dcol"], writes=["Wintra"])
    ur = tmp(16); ui = tmp(16); wr = tmp(16); wi = tmp(16); w2r = tmp(16); w2i = tmp(16); a8 = tmp(16)
    vop(TS(a8, a_, 8.0, None, ALU.mult), ["a_"], ["a8"])
    S.op(AC, ACT(R8, a8, AF.Exp), reads=["a8"], writes=["R8"])
    S.op(AC, ACT(a8, a8, AF.Exp, scale=-1.0), reads=["a8"], writes=["a8"])
    vop(TT(ur, PR3[:, 8, :], a8, ALU.mult), ["PR", "a8"], ["ur"]); vop(TT(ui, PI3[:, 8, :], a8, ALU.mult), ["PI", "a8"], ["ui"])
    vop(CP(COS3[:, :, 0], ur), ["ur"], ["COS"]); vop(CP(SIN3[:, :, 0], ui), ["ui"], ["SIN"])
    vop(CP(wr, ur), ["ur"], ["wr"]); vop(CP(wi, ui), ["ui"], ["wi"])
    e1 = tmp(16 * 32); e2 = tmp(16 * 32)
    for k in range(6):
        n = 1 << k
        e1v = e1[:, 0:16 * n].rearrange("p (g m) -> p g m", g=16); e2v = e2[:, 0:16 * n].rearrange("p (g m) -> p g m", g=16)
        wrb = bc(wr, 2, n); wib = bc(wi, 2, n)
        vop(TT(e1v, COS3[:, :, 0:n], wrb, ALU.mult), ["COS", "wr"], ["e1"]); vop(TT(e2v, SIN3[:, :, 0:n], wib, ALU.mult), ["SIN", "wi"], ["e2"])
        vop(TT(COS3[:, :, n:2 * n], e1v, e2v, ALU.subtract), ["e1", "e2", "COS"], ["COS"])
        vop(TT(e1v, COS3[:, :, 0:n], wib, ALU.mult), ["COS", "wi", "e1"], ["e1"]); vop(TT(e2v, SIN3[:, :, 0:n], wrb, ALU.mult), ["SIN", "wr", "e2"], ["e2"])
        vop(TT(SIN3[:, :, n:2 * n], e1v, e2v, ALU.add), ["e1", "e2", "SIN"], ["SIN"])
        if k < 5:
            cmul(w2r, w2i, wr, wi, wr, wi, ["wr", "wi"], ["w2"])
            vop(CP(wr, w2r), ["w2"], ["wr"]); vop(CP(wi, w2i), ["w2"], ["wi"])
    DBG.update(ur=ur, ui=ui, a8=a8, wr=wr, wi=wi, PR=PR, PI=PI, e1=e1, e2=e2)
    DBG.update(Wintra=Wintra, Wst=Wst, Wcr=Wcr, COS=COS, SIN=SIN, R8=R8, MASKH=MASKH, Win=Win, zcol=zcol, xi=xi, gpre=gpre, invf=invf, identb=identb, GPOST=GPOST)
    S.op(PL, MS(mhalf, -0.5), writes=["mhalf"])
    S.op(PL, MS(Rst, 0.0), writes=["R"]); S.op(PL, MS(Rb, 0.0), writes=["Rb0"]); S.op(PL, MS(Rb2, 0.0), writes=["Rb1"]); S.op(PL, MS(CAR, 0.0), writes=["CAR"])
    S.barrier()
    ar.top = P_BASE
    EPS = 1e-6
    pt_t = PSt[4]
    ptb = pt_t.bitcast(BF16)
    pss, pso, psu = PSt[5], PSt[6], PSt[7]
    mA = ar.top
    hT = ar.bf(8 * 1024); hT3 = hT.rearrange("p (k t) -> p k t", k=8)
    xs = [ar.f32(1024), ar.f32(1024), ar.f32(1024)]
    hb = [ar.bf(1024), ar.bf(1024)]
    junk = ar.bf(1024)
    ssq = ar.f32(8); rstd = ar.f32(8)
    rcos = ar.f32(512); rsin = ar.f32(512); rang = ar.f32(512); rk = ar.f32(512); rki = ar.i32(512); rm = ar.f32(512); posf = ar.f32(8)
    rcos3 = rcos.rearrange("p (c j) -> p c j", c=8); rsin3 = rsin.rearrange("p (c j) -> p c j", c=8)
    qkr = [ar.bf(1024), ar.bf(1024)]
    qkT = [ar.bf(1024), ar.bf(1024)]
    vb = [ar.bf(512), ar.bf(512)]; vz = [ar.bf(512), ar.bf(512)]; sTb = ar.bf(512)
    h4 = lambda a: a.rearrange("p (h d) -> p h d", h=4)
    a8v = lambda a: a.rearrange("p (a d) -> p a d", a=8)
    sTb3 = h4(sTb)
    rt = [ar.f32(512) for _ in range(4)]
    yn = ar.f32(512); sgg = [ar.f32(512), ar.f32(512)]; yr = ar.bf(512)
    bst = ar.f32(24); mv = ar.f32(8); rs = ar.f32(4); vtmp = ar.f32(4)
    mR_end = ar.top
    ar.top = mA
    U8 = ar.bf(4096); U83 = U8.rearrange("p (g n) -> p g n", g=32)
    Y8 = ar.bf(4096); Y83 = Y8.rearrange("p (g n) -> p g n", g=32)
    SPV = ar.bf(2 * 16 * 128); SPV4 = SPV.rearrange("p (r g n) -> p r g n", r=2, g=16)
    PRE = [[ar.f32(256), ar.f32(256)] for _ in range(2)]; SCN = [[ar.f32(256), ar.f32(256)] for _ in range(2)]
    st_ = [[ar.f32(256) for _ in range(4)] for _ in range(2)]
    FUL = [[ar.f32(256), ar.f32(256)] for _ in range(2)]
    ysT = [ar.bf(512), ar.bf(512)]; ys2 = [ar.bf(512), ar.bf(512)]; ys2T = [ar.bf(512), ar.bf(512)]
    sig = ar.f32(512); xs2 = [ar.f32(1024), ar.f32(1024)]; tm = [ar.f32(1024), ar.f32(1024)]
    junk2 = ar.bf(1024); ss2 = [ar.f32(2), ar.f32(2)]; rs2 = [ar.f32(2), ar.f32(2)]
    mS_end = ar.top
    assert max(mR_end, mS_end) <= AW, (mR_end, mS_end)
    v34 = lambda a: a.rearrange("p (g m) -> p g m", g=4)
    Tb4 = Tb.rearrange("p (g s c) -> p g s c", g=32, s=8)
    YT4 = Tb.rearrange("p (s g c) -> p s g c", s=8, g=32)
    YT3 = Tb.rearrange("p (s f) -> p s f", s=8)
    Rbs = [Rb, Rb2]

    mh = [mhalf]

    def rsqrt_cols(dst, src, scale, nm_src, nm_dst):
        n = dst.shape[1]
        S.op(DV, TS(dst, src, scale, EPS, ALU.mult, ALU.add), reads=[nm_src], writes=[nm_dst])
        S.op(PL, TT(dst, dst, mh[0][:, 0:n], ALU.pow), reads=[nm_dst, "mhalf"], writes=[nm_dst])

    rcnt = [0]
    for kt in range(4):
        S.dma(DMA(Wglu3[:, kt, :], wg_v[:, kt, :], max_dma_last_dim=4096), q="pool", writes=["Wglu"])
    for kt in range(8):
        S.dma(DMA(Wout3[:, kt, :], wo_v[:, kt, :], max_dma_last_dim=4096), q="pool", writes=["Wout"])
    for sbi in range((NPRE + NMAIN) if STOP is None else int(STOP)):
        pre = sbi < NPRE
        xsrc = x_pre if pre else x_own
        row0 = 1024 * (sbi if pre else sbi - NPRE)
        pcol = 0 if pre else 1
        xv = xsrc[row0:row0 + 1024, :].rearrange("(n s) d -> n s d", s=8)
        S.op(DV, TS(posf, iopc, posc[:, pcol:pcol + 1], float(row0), ALU.add, ALU.add), reads=["iopc", "posc", "rang"], writes=["posf"])
        rang3 = rang.rearrange("p (c j) -> p c j", c=8)
        S.op(DV, TT(rang3, bc(posf, 2, 64), bc(invf, 1, 8), ALU.mult), reads=["posf", "invf", "rcos"], writes=["rang"])
        S.op(DV, TS(rk, rang, 1.0 / TWO_PI, None, ALU.mult), reads=["rang"], writes=["rk"])
        S.op(DV, CP(rki, rk), reads=["rk"], writes=["rki"])
        S.op(DV, CP(rk, rki), reads=["rki"], writes=["rk"])
        S.op(DV, STT(rm, rk, -C1, rang, ALU.mult, ALU.add), reads=["rk", "rang"], writes=["rm"])
        S.op(DV, STT(rm, rk, -C2, rm, ALU.mult, ALU.add), reads=["rk", "rm"], writes=["rm"])
        S.op(DV, TS(rm, rm, math.pi, -math.pi, ALU.min, ALU.max), reads=["rm"], writes=["rm"])
        S.op(AC, ACT(rsin, rm, AF.Sin), reads=["rm"], writes=["rsin"])
        S.op(DV, TS(rk, rm, -1.0, None, ALU.mult), reads=["rm", "rk"], writes=["rk"])
        S.op(DV, TT(rk, rk, rm, ALU.max), reads=["rm", "rk"], writes=["rk"])
        S.op(DV, TS(rk, rk, -1.0, math.pi / 2, ALU.mult, ALU.add), reads=["rk"], writes=["rk"])
        S.op(AC, ACT(rcos, rk, AF.Sin), reads=["rk"], writes=["rcos"])

        def ht1(s):
            b = s % 3
            S.dma(DMA(xs[b], xsrc[row0 + 128 * s:row0 + 128 * s + 128, :]), writes=[f"xs{b}"])
            S.op(AC, ACT(junk, xs[b], AF.Square, accum_out=ssq[:, s:s + 1]), reads=[f"xs{b}"], writes=["junk", f"ssq{s}"])

        def ht2(s):
            b = s % 2; bx = s % 3
            rsqrt_cols(rstd[:, s:s + 1], ssq[:, s:s + 1], 1.0 / 1024, f"ssq{s}", f"rstd{s}")
            S.op(AC, ACT(hb[b], xs[bx], AF.Copy, scale=rstd[:, s:s + 1]), reads=[f"xs{bx}", f"rstd{s}"], writes=[f"hb{b}"])
            tbk, tnm = (ptb, "pt") if s % 2 == 0 else (PSt[5].bitcast(BF16), "pss")
            for kt in range(8):
                S.op(PE, TR(tbk[:, kt * 128:(kt + 1) * 128], hb[b][:, kt * 128:(kt + 1) * 128], identb), reads=[f"hb{b}", "identb"], writes=[tnm])
            S.op(DV, TT(hT3[:, :, 128 * s:128 * s + 128], tbk.rearrange("p (k n) -> p k n", k=8), bc(gpre, 2, 128), ALU.mult), reads=[tnm, "gpre"], writes=["hT"])

        if os.environ.get("KSEQ_H"):
            for s in range(8):
                ht1(s); ht2(s)
        else:
            ht1(0); ht1(1)
            for s in range(8):
                if s + 2 < 8:
                    ht1(s + 2)
                ht2(s)

        tkc = lambda c: slice(128 * c, 128 * c + 128)
        if pre:
            def A(c):
                bk, bv = (PSt[0], PSt[1]) if c % 2 == 0 else (PSt[2], PSt[3])
                nk, nv = ("ps0", "ps1") if c % 2 == 0 else ("ps2", "ps3")
                for (bank, col0, nm) in ((bk, 512, nk), (bv, 1024, nv)):
                    for kt in range(8):
                        S.op(PE, MM(bank[:, :], hT3[:, kt, tkc(c)], Win3[:, kt, col0:col0 + 512], kt == 0, kt == 7), reads=["hT", "Win"], writes=[nm])

            def B(c):
                b = c % 2
                bk, bv = (PSt[0], PSt[1]) if c % 2 == 0 else (PSt[2], PSt[3])
                nk, nv = ("ps0", "ps1") if c % 2 == 0 else ("ps2", "ps3")
                cosb = bc(rcos3[:, c, :], 1, 4); sinb = bc(rsin3[:, c, :], 1, 4)
                xq = bk[:, :].rearrange("p (h a j) -> p h a j", h=4, a=2)
                x1 = xq[:, :, 0, :]; x2 = xq[:, :, 1, :]
                r3 = [t_[:, 0:256].rearrange("p (h j) -> p h j", h=4) for t_ in rt]
                S.op(DV, TT(r3[0], x1, cosb, ALU.mult), reads=[nk, "rcos"], writes=["rt0"])
                S.op(DV, TT(r3[1], x2, sinb, ALU.mult), reads=[nk, "rsin"], writes=["rt1"])
                S.op(DV, TT(r3[2], x1, sinb, ALU.mult), reads=[nk, "rsin"], writes=["rt2"])
                S.op(DV, TT(r3[3], x2, cosb, ALU.mult), reads=[nk, "rcos"], writes=["rt3"])
                q4 = qkr[b].rearrange("p (a h j) -> p a h j", a=8, h=2)
                S.op(PL, TT(q4[:, 4:8, 0, :], r3[0], r3[1], ALU.subtract), reads=["rt0", "rt1"], writes=[f"qkr{b}"])
                S.op(PL, TT(q4[:, 4:8, 1, :], r3[2], r3[3], ALU.add), reads=["rt2", "rt3"], writes=[f"qkr{b}"])
                S.op(DV, TT(h4(vz[b]), h4(bv[:, :]), bc(zcol, 2, 128), ALU.mult), reads=[nv, "zcol"], writes=[f"vz{b}"])

            def ST(c):
                b = c % 2
                psu3 = h4(psu[:, :])
                for h in range(4):
                    S.op(PE, MM(psu3[:, h, :], a8v(qkr[b])[:, 4 + h, :], h4(vz[b])[:, h, :], True, True), reads=[f"qkr{b}", f"vz{b}"], writes=["psu"])
                for h in range(4):
                    S.op(DV, STT(Rst3[:, h, :], Rst3[:, h, :], GHEAD[h], psu3[:, h, :], ALU.mult, ALU.add), reads=["psu", "R"], writes=["R"])

            A(0)
            for c in range(8):
                if c + 1 < 8:
                    A(c + 1)
                B(c)
                ST(c)
            nb = rcnt[0] % 2
            S.op(AC, ACT(Rbs[nb], Rst, AF.Copy), reads=["R"], writes=[f"Rb{nb}"])
        else:

            def A1(c):
                for (col0, off) in ((0, 0), (512, 512)):
                    for kt in range(8):
                        S.op(PE, MM(PSt[off // 512][:, :], hT3[:, kt, tkc(c)], Win3[:, kt, col0:col0 + 512], kt == 0, kt == 7), reads=["hT", "Win"], writes=["ps0" if off == 0 else "ps1"])

            def A2(c):
                for (col0, off) in ((1024, 0), (1536, 512)):
                    for kt in range(8):
                        S.op(PE, MM(PSt[2 + off // 512][:, :], hT3[:, kt, tkc(c)], Win3[:, kt, col0:col0 + 512], kt == 0, kt == 7), reads=["hT", "Win"], writes=["ps2" if off == 0 else "ps3"])

            def B1(c):
                b = c % 2
                cosb = bc(rcos3[:, c, :], 1, 4); sinb = bc(rsin3[:, c, :], 1, 4)
                q4 = qkr[b].rearrange("p (a t j) -> p a t j", a=8, t=2)
                for half in range(2):
                    nm = "ps0" if half == 0 else "ps1"
                    xq = PSt[half][:, :].rearrange("p (a t j) -> p a t j", a=4, t=2)
                    x1 = xq[:, :, 0, :]; x2 = xq[:, :, 1, :]
                    r3 = [t_[:, 256 * half:256 * half + 256].rearrange("p (a j) -> p a j", a=4) for t_ in rt]
                    S.op(DV, TT(r3[0], x1, cosb, ALU.mult), reads=[nm, "rcos"], writes=[f"rt0{half}"])
                    S.op(DV, TT(r3[1], x2, sinb, ALU.mult), reads=[nm, "rsin"], writes=[f"rt1{half}"])
                    S.op(DV, TT(r3[2], x1, sinb, ALU.mult), reads=[nm, "rsin"], writes=[f"rt2{half}"])
                    S.op(DV, TT(r3[3], x2, cosb, ALU.mult), reads=[nm, "rcos"], writes=[f"rt3{half}"])
                    S.op(PL, TT(q4[:, 4 * half:4 * half + 4, 0, :], r3[0], r3[1], ALU.subtract), reads=[f"rt0{half}", f"rt1{half}"], writes=[f"qkr{b}"])
                    S.op(PL, TT(q4[:, 4 * half:4 * half + 4, 1, :], r3[2], r3[3], ALU.add), reads=[f"rt2{half}", f"rt3{half}"], writes=[f"qkr{b}"])

            def B2(c):
                b = c % 2
                for h in range(4):
                    S.op(AC, ACT(h4(vz[b])[:, h, :], h4(PSt[2][:, :])[:, h, :], AF.Copy, scale=zcol[:, h:h + 1]), reads=["ps2", "zcol"], writes=[f"vz{b}"])
                S.op(AC, ACT(vb[b], PSt[2][:, :], AF.Copy), reads=["ps2"], writes=[f"vb{b}"])
                S.op(AC, ACT(sgg[b], PSt[3][:, :], AF.Silu), reads=["ps3"], writes=[f"sgg{b}"])
                S.op(PL, TT(sgg[b], sgg[b], GGN, ALU.mult), reads=[f"sgg{b}", "GGN"], writes=[f"sgg{b}"])

            def Tqk(c):
                b = c % 2
                for a in range(8):
                    S.op(PE, TR(ptb[:, a * 128:(a + 1) * 128], a8v(qkr[b])[:, a, :], identb), reads=[f"qkr{b}", "identb"], writes=["pt"])
                S.op(AC, ACT(qkT[b], ptb[:, :], AF.Copy), reads=["pt"], writes=[f"qkT{b}"])

            def SC(c):
                b = c % 2
                pss3 = h4(pss[:, :]); qT = a8v(qkT[b])
                for h in range(4):
                    S.op(PE, MM(pss3[:, h, :], qT[:, 4 + h, :], qT[:, h, :], True, True), reads=[f"qkT{b}"], writes=["pss"])
                S.op(DV, TT(sTb3, pss3, MASKH3, ALU.mult), reads=["pss", "MASKH"], writes=["sTb"])

            def ST(c):
                b = c % 2
                psu3 = h4(psu[:, :])
                for h in range(4):
                    S.op(PE, MM(psu3[:, h, :], a8v(qkr[b])[:, 4 + h, :], h4(vz[b])[:, h, :], True, True), reads=[f"qkr{b}", f"vz{b}"], writes=["psu"])
                for h in range(4):
                    S.op(DV, STT(Rst3[:, h, :], Rst3[:, h, :], GHEAD[h], psu3[:, h, :], ALU.mult, ALU.add), reads=["psu", "R"], writes=["R"])
                nb = (rcnt[0] + c + 1) % 2
                S.op(AC, ACT(Rbs[nb], Rst, AF.Copy), reads=["R"], writes=[f"Rb{nb}"])

            def OUT(c):
                b = c % 2
                cb = (rcnt[0] + c) % 2
                pso3 = h4(pso[:, :]); qT = a8v(qkT[b])
                for h in range(4):
                    S.op(PE, MM(pso3[:, h, :], sTb3[:, h, :], h4(vb[b])[:, h, :], True, False), reads=["sTb", f"vb{b}"], writes=["pso"])
                    S.op(PE, MM(pso3[:, h, :], qT[:, h, :], h4(Rbs[cb])[:, h, :], False, True), reads=[f"qkT{b}", f"Rb{cb}"], writes=["pso"])

            def GN(c):
                b = c % 2
                pso3 = h4(pso[:, :])
                bst3 = bst.rearrange("p (h k) -> p h k", h=4); mv3 = mv.rearrange("p (h k) -> p h k", h=4)
                for h in range(4):
                    S.op(DV, lambda e, h=h: e.bn_stats(out=bst3[:, h, :], in_=pso3[:, h, :]), reads=["pso"], writes=["bst"])
                    S.op(DV, lambda e, h=h: e.bn_aggr(out=mv3[:, h, :], in_=bst3[:, h, :]), reads=["bst"], writes=["mv"])
                S.op(DV, TT(vtmp, mv3[:, :, 1], xi2, ALU.mult), reads=["mv", "xi2"], writes=["vtmp"])
                rsqrt_cols(rs, vtmp, 1.0, "vtmp", "rs")
                S.op(DV, TT(rs, rs, xi, ALU.mult), reads=["rs", "xi"], writes=["rs"])
                yn3 = h4(yn)
                for h in range(4):
                    S.op(DV, TS(yn3[:, h, :], pso3[:, h, :], mv3[:, h, 0:1], rs[:, h:h + 1], ALU.subtract, ALU.mult), reads=["pso", "mv", "rs"], writes=["yn"])
                S.op(PL, TT(yr, yn, sgg[b], ALU.mult), reads=["yn", f"sgg{b}"], writes=["yr"])

            def Tyr(c):
                for h in range(4):
                    S.op(PE, TR(ptb[:, h * 128:(h + 1) * 128], yr[:, h * 128:(h + 1) * 128], identb), reads=["yr", "identb"], writes=["pt"])
                S.op(AC, ACT(yrT3[:, :, tkc(c)], ptb[:, 0:512].rearrange("p (h i) -> p h i", h=4), AF.Copy), reads=["pt"], writes=["yrT"])

            if os.environ.get("KSEQ_R"):
                for c in range(8):
                    A1(c); A2(c); B1(c); B2(c); Tqk(c); SC(c); ST(c); OUT(c); GN(c); Tyr(c)
            else:
                A1(0); A2(0); B1(0); B2(0)
                for c in range(8):
                    Tqk(c)
                    if c + 1 < 8:
                        A1(c + 1)
                    if c > 0:
                        Tyr(c - 1)
                    SC(c); ST(c)
                    if c + 1 < 8:
                        B1(c + 1)
                        A2(c + 1)
                    OUT(c)
                    if c + 1 < 8:
                        B2(c + 1)
                    GN(c)
                Tyr(7)
        rcnt[0] += (0 if pre else 8)
        for s in range(8):
            bank, nm = (pss, "pss") if s % 2 == 0 else (pso, "pso")
            for kt in range(8):
                S.op(PE, MM(bank[:, :], hT3[:, kt, s::8], Win3[:, kt, 2048:2560], kt == 0, kt == 7), reads=["hT", "Win"], writes=[nm])
            S.op(AC, ACT(Tb4[:, :, s, :], bank[:, :].rearrange("p (g c) -> p g c", g=32), AF.Copy), reads=[nm], writes=["T"])
        S.barrier()
        Tflat = Tb.rearrange("p (g f) -> p g f", g=32)
        ptb2 = PSt[5].bitcast(BF16)
        for gq in range(4):
            tb_, nm = (ptb, "pt") if gq % 2 == 0 else (ptb2, "pss")
            for j in range(8):
                S.op(PE, TR(tb_[:, j * 128:(j + 1) * 128], Tflat[:, 8 * gq + j, :], identb), reads=["T", "identb"], writes=[nm])
            S.op(AC if gq % 2 else DV, (ACT(U83[:, 8 * gq:8 * gq + 8, :], tb_.rearrange("p (g n) -> p g n", g=8), AF.Copy) if gq % 2 else
                                        CP(U83[:, 8 * gq:8 * gq + 8, :], tb_.rearrange("p (g n) -> p g n", g=8))), reads=[nm], writes=["U8"])
        its = [(hf, gb) for hf in range(2) for gb in range(4)]

        def Mm(it):
            hf, gb = its[it]; p = it % 2
            nsl = slice(64 * hf, 64 * hf + 64)
            psr3 = PSt[2 * p][:, 0:256].rearrange("p (g m) -> p g m", g=4); psi3 = PSt[2 * p + 1][:, 0:256].rearrange("p (g m) -> p g m", g=4)
            for j in range(8):
                g = 8 * gb + j; par = j % 2; slot = j // 2
                ps_ = slice(64 * par, 64 * par + 64)
                S.op(PE, MM(psr3[ps_, slot, :], Wst4[:, g, 0, :], U83[:, g, nsl], True, True), reads=["U8", "Wst"], writes=[f"ps{2 * p}"])
                S.op(PE, MM(psi3[ps_, slot, :], Wst4[:, g, 1, :], U83[:, g, nsl], True, True), reads=["U8", "Wst"], writes=[f"ps{2 * p + 1}"])

        def R1(it):
            hf, gb = its[it]; p = it % 2
            g2s = slice(4 * gb, 4 * gb + 4)
            psr3 = PSt[2 * p][:, 0:256].rearrange("p (g m) -> p g m", g=4); psi3 = PSt[2 * p + 1][:, 0:256].rearrange("p (g m) -> p g m", g=4)
            cosv = COS3[:, g2s, :]; sinv = SIN3[:, g2s, :]
            nr_, ni_ = f"ps{2 * p}", f"ps{2 * p + 1}"
            S.op(DV, TT(v34(st_[p][0]), psr3, cosv, ALU.mult), reads=[nr_, "COS"], writes=[f"st{p}0"])
            S.op(DV, TT(v34(st_[p][1]), psi3, sinv, ALU.mult), reads=[ni_, "SIN"], writes=[f"st{p}1"])
            S.op(DV, TT(v34(st_[p][2]), psi3, cosv, ALU.mult), reads=[ni_, "COS"], writes=[f"st{p}2"])
            S.op(DV, TT(v34(st_[p][3]), psr3, sinv, ALU.mult), reads=[nr_, "SIN"], writes=[f"st{p}3"])
            S.op(PL, TT(PRE[p][0], st_[p][0], st_[p][1], ALU.add), reads=[f"st{p}0", f"st{p}1"], writes=[f"PRE{p}0"])
            S.op(PL, TT(PRE[p][1], st_[p][2], st_[p][3], ALU.subtract), reads=[f"st{p}2", f"st{p}3"], writes=[f"PRE{p}1"])

        def SCAN(it):
            hf, gb = its[it]; p = it % 2
            g2s = slice(4 * gb, 4 * gb + 4)
            if not pre:
                for ri in range(2):
                    S.op(AC, ACT(SPV4[:, ri, g2s, 64 * hf:64 * hf + 1], CAR3[:, ri, g2s].unsqueeze(2), AF.Copy), reads=[f"CAR{gb}"], writes=["SPV"])
            for ri in range(2):
                for slot in range(4):
                    g2 = 4 * gb + slot
                    S.op(DV, lambda e, ri=ri, slot=slot, g2=g2, p=p: e.tensor_tensor_scan(
                        out=v34(SCN[p][ri])[:, slot, :], data0=R8[:, g2:g2 + 1].broadcast_to([128, 64]), data1=v34(PRE[p][ri])[:, slot, :],
                        initial=CAR3[:, ri, g2:g2 + 1], op0=ALU.mult, op1=ALU.add), reads=[f"PRE{p}{ri}", f"CAR{gb}", "R8"], writes=[f"SCN{p}{ri}"])

        def R2(it):
            hf, gb = its[it]; p = it % 2
            g2s = slice(4 * gb, 4 * gb + 4)
            cl = slice(63, 64) if pre else slice(0, 64)
            cosv = COS3[:, g2s, cl]; sinv = SIN3[:, g2s, cl]
            w = lambda a: v34(a)[:, :, cl]
            S.op(DV, TT(w(st_[p][0]), w(SCN[p][0]), cosv, ALU.mult), reads=[f"SCN{p}0", "COS"], writes=[f"st{p}0"])
            S.op(DV, TT(w(st_[p][1]), w(SCN[p][1]), sinv, ALU.mult), reads=[f"SCN{p}1", "SIN"], writes=[f"st{p}1"])
            S.op(DV, TT(w(st_[p][2]), w(SCN[p][1]), cosv, ALU.mult), reads=[f"SCN{p}1", "COS"], writes=[f"st{p}2"])
            S.op(DV, TT(w(st_[p][3]), w(SCN[p][0]), sinv, ALU.mult), reads=[f"SCN{p}0", "SIN"], writes=[f"st{p}3"])
            S.op(PL, TT(w(FUL[p][0]), w(st_[p][0]), w(st_[p][1]), ALU.subtract), reads=[f"st{p}0", f"st{p}1"], writes=[f"FUL{p}0"])
            S.op(PL, TT(w(FUL[p][1]), w(st_[p][2]), w(st_[p][3]), ALU.add), reads=[f"st{p}2", f"st{p}3"], writes=[f"FUL{p}1"])
            for ri in range(2):
                S.op(PL, CP(CAR3[:, ri, g2s], v34(FUL[p][ri])[:, :, 63]), reads=[f"FUL{p}{ri}", f"SCN{p}0", f"SCN{p}1", "SPV"], writes=[f"CAR{gb}"])
                if not pre:
                    S.op(AC, ACT(SPV4[:, ri, g2s, 64 * hf + 1:64 * hf + 64], v34(FUL[p][ri])[:, :, 0:63], AF.Copy), reads=[f"FUL{p}{ri}"], writes=["SPV"])

        if os.environ.get("KSEQ_S"):
            for it in range(8):
                Mm(it); R1(it); SCAN(it); R2(it)
        else:
            Mm(0); Mm(1); R1(0)
            for it in range(8):
                if it + 1 < 8:
                    R1(it + 1)
                SCAN(it)
                R2(it)
                if it + 2 < 8:
                    Mm(it + 2)
        if not pre:
            for gq in range(8):
                py, nm = (PSt[0], "ps0") if gq % 2 == 0 else (PSt[1], "ps1")
                py3 = py[:, :].rearrange("p (g n) -> p g n", g=4)
                for j in range(4):
                    g = 4 * gq + j; par = g % 2; g2 = g // 2
                    ps_ = slice(64 * par, 64 * par + 64)
                    S.op(PE, MM(py3[:, j, :], Wintra3[:, g, :], U83[:, g, :], True, False), reads=["U8", "Wintra"], writes=[nm])
                    S.op(PE, MM(py3[:, j, :], Wcr4[ps_, 0, g2, :], SPV4[ps_, 0, g2, :], False, False), reads=["SPV", "Wcr"], writes=[nm])
                    S.op(PE, MM(py3[:, j, :], Wcr4[ps_, 1, g2, :], SPV4[ps_, 1, g2, :], False, True), reads=["SPV", "Wcr"], writes=[nm])
                S.op(AC, ACT(Y83[:, 4 * gq:4 * gq + 4, :], py3, AF.Gelu_apprx_tanh), reads=[nm], writes=["Y8"])
            for gq in range(4):
                tb_, nm = (ptb, "pt") if gq % 2 == 0 else (ptb2, "pss")
                for j in range(8):
                    S.op(PE, TR(tb_[:, j * 128:(j + 1) * 128], Y83[:, 8 * gq + j, :], identb), reads=["Y8", "identb"], writes=[nm])
                S.op(DV, CP(YT4[:, :, 8 * gq:8 * gq + 8, :].rearrange("p s g c -> p g s c"), tb_.rearrange("p (g s c) -> p g s c", g=8, s=8)), reads=[nm, "U8"], writes=["T"])
            pg0, pg1 = PSt[2], PSt[3]
            pms = [(PSt[0], PSt[1], "ps0", "ps1"), (PSt[6], PSt[7], "pso", "psu")]

            def T1(s):
                b = s % 2
                for kt in range(4):
                    S.op(PE, TR(ptb[:, kt * 128:(kt + 1) * 128], YT3[:, s, kt * 128:(kt + 1) * 128], identb), reads=["T", "identb"], writes=["pt"])
                S.op(AC, ACT(ysT[b], ptb[:, 0:512], AF.Copy), reads=["pt"], writes=[f"ysT{b}"])

            def GLU(s):
                b = s % 2
                ysT3 = ysT[b].rearrange("p (k n) -> p k n", k=4)
                for hfc, bank, nm in ((0, pg0, "ps2"), (1, pg1, "ps3")):
                    for kt in range(4):
                        S.op(PE, MM(bank[:, :], ysT3[:, kt, :], Wglu3[:, kt, hfc * 512:(hfc + 1) * 512], kt == 0, kt == 3), reads=[f"ysT{b}", "Wglu"], writes=[nm])
                S.op(AC, ACT(sig, pg1[:, :], AF.Sigmoid), reads=["ps3"], writes=["sig"])
                S.op(DV, TT(ys2[b], pg0[:, :], sig, ALU.mult), reads=["ps2", "sig"], writes=[f"ys2{b}"])

            def T2(s):
                b = s % 2
                for kt in range(4):
                    S.op(PE, TR(ptb2[:, kt * 128:(kt + 1) * 128], ys2[b][:, kt * 128:(kt + 1) * 128], identb), reads=[f"ys2{b}", "identb"], writes=["pss"])
                S.op(AC, ACT(ys2T[b], ptb2[:, 0:512], AF.Copy), reads=["pss"], writes=[f"ys2T{b}"])

            def WO(s):
                b = s % 2
                pm0, pm1, n0, n1 = pms[b]
                y2T3 = ys2T[b].rearrange("p (k n) -> p k n", k=4)
                for hfc, bank, nm in ((0, pm0, n0), (1, pm1, n1)):
                    for kt in range(8):
                        lhs = yrT3[:, kt, s::8] if kt < 4 else y2T3[:, kt - 4, :]
                        S.op(PE, MM(bank[:, :], lhs, Wout3[:, kt, hfc * 512:(hfc + 1) * 512], kt == 0, kt == 7), reads=["yrT", f"ys2T{b}", "Wout"], writes=[nm])
                if s == 0:
                    S.dma(DMA(xs2[0], xv[:, 0, :]), writes=["xs20"])
                if s + 1 < 8:
                    S.dma(DMA(xs2[(s + 1) % 2], xv[:, s + 1, :]), writes=[f"xs2{(s + 1) % 2}"])
                for hfc, bank, nm in ((0, pm0, n0), (1, pm1, n1)):
                    S.op(AC, ACT(junk2[:, 0:512], bank[:, :], AF.Square, accum_out=ss2[b][:, hfc:hfc + 1]), reads=[nm], writes=["junk2", f"ss2{b}{hfc}"])
                S.op(DV, TT(ss2[b][:, 0:1], ss2[b][:, 0:1], ss2[b][:, 1:2], ALU.add), reads=[f"ss2{b}0", f"ss2{b}1"], writes=[f"ss2{b}0"])
                rsqrt_cols(rs2[b][:, 0:1], ss2[b][:, 0:1], 1.0 / 1024, f"ss2{b}0", f"rs2{b}")
                for hfc, bank, nm in ((0, pm0, n0), (1, pm1, n1)):
                    cs_ = slice(hfc * 512, hfc * 512 + 512)
                    S.op(DV, STT(tm[b][:, cs_], bank[:, :], rs2[b][:, 0:1], GPOST[:, cs_], ALU.mult, ALU.mult), reads=[nm, f"rs2{b}", "GPOST"], writes=[f"tm{b}"])
                S.op(PL, TT(xs2[b], xs2[b], tm[b], ALU.add), reads=[f"tm{b}", f"xs2{b}"], writes=[f"xs2{b}"])
                r0 = 1024 * (sbi - NPRE)
                S.dma(DMA(x1s[r0:r0 + 1024, :].rearrange("(n s) d -> n s d", s=8)[:, s, :], xs2[b]), reads=[f"xs2{b}"], writes=["x1s"])

            if os.environ.get("KSEQ_W"):
                for s in range(8):
                    T1(s); GLU(s); T2(s); WO(s)
            else:
                T1(0); GLU(0); T1(1); T2(0)
                for s in range(1, 8):
                    GLU(s)
                    WO(s - 1)
                    if s + 1 < 8:
                        T1(s + 1)
                    T2(s)
                WO(7)
        S.barrier()

    ar.top = 0
    W1b = ar.bf(8 * 4096); W1b3 = W1b.rearrange("p (k c) -> p k c", k=8)
    W2b = ar.bf(32 * 1024); W2b3 = W2b.rearrange("p (k c) -> p k c", k=32)
    identb2 = ar.bf(128); GPOST2 = ar.f32(1024); g2c = ar.f32(8)
    X1g = ar.f32(4096); X1g3 = X1g.rearrange("p (j d) -> p j d", j=4)
    h2T = ar.bf(8 * 512); h2T3 = h2T.rearrange("p (k t) -> p k t", k=8)
    aT = ar.bf(32 * 512); aT3 = aT.rearrange("p (k t) -> p k t", k=32)
    hb2s = [ar.bf(1024), ar.bf(1024)]; jk = ar.bf(1024); rl = [ar.f32(512), ar.f32(512)]; tmb = ar.f32(1024); sq = ar.f32(4); rq = ar.f32(4); so = ar.f32(2); ro = ar.f32(2)
    mhalf2 = ar.f32(8)
    assert ar.top <= AW, ar.top
    stgB = aT.bitcast(F32) if False else None
    cst = X1g
    S.op(DV, CP(cst[:, 0:8], g2pre), reads=[], writes=["cst"])
    S.op(DV, CP(jk[:, 0:128], identb), reads=[], writes=["jk"])
    S.barrier()
    S.op(DV, CP(g2c, cst[:, 0:8]), reads=["cst"], writes=["g2c"])
    S.op(DV, CP(identb2, jk[:, 0:128]), reads=["jk"], writes=["identb2"])
    S.dma(DMA(GPOST2, dbc(g2post_d)), writes=["GPOST2"])
    S.op(PL, MS(mhalf2, -0.5), writes=["mhalf"])
    mh[0] = mhalf2
    S.barrier()
    w1_v = w1_d.rearrange("(k p) c -> p k c", p=128)
    w2_v = w2_d.rearrange("(k p) c -> p k c", p=128)
    for blk in range(8):
        S.dma(DMA(W1b3[:, :, blk * 512:(blk + 1) * 512], w1_v[:, :, blk * 512:(blk + 1) * 512], max_dma_last_dim=4096), q="pool", writes=[f"W1b{blk}"])
    for k4 in range(8):
        S.dma(DMA(W2b3[:, 4 * k4:4 * k4 + 4, :], w2_v[:, 4 * k4:4 * k4 + 4, :], max_dma_last_dim=4096), q="pool", writes=[f"W2b{k4}"])
    pf = [PSt[0], PSt[1], PSt[2], PSt[3]]
    pmo_pairs = [((PSt[5], PSt[6]), ("pmo0", "pmo1")), ((PSt[7], PSt[4]), ("pm7", "pt"))]
    NG = NT // 512 if STOP is None else 0
    x1v = lambda gi: x1s[512 * gi:512 * gi + 512, :].rearrange("(p j) d -> p j d", j=4)
    outv = lambda gi: out_d[512 * gi:512 * gi + 512, :].rearrange("(p j) d -> p j d", j=4)

    def ld_sq(gi, j):
        S.dma(DMA(X1g3[:, j, :], x1v(gi)[:, j, :]), writes=[f"X1g{j}"])
        S.op(AC, ACT(jk, X1g3[:, j, :], AF.Square, accum_out=sq[:, j:j + 1]), reads=[f"X1g{j}"], writes=["jk", "sq"])

    for j in range(4 if NG > 0 else 0):
        ld_sq(0, j)
    for gi in range(NG):
        rsqrt_cols(rq, sq, 1.0 / 1024, "sq", "rq")
        for j in range(4):
            hb2 = hb2s[j % 2]
            S.op(AC, ACT(hb2, X1g3[:, j, :], AF.Copy, scale=rq[:, j:j + 1]), reads=[f"X1g{j}", "rq"], writes=[f"hb2{j % 2}"])
            for kt in range(8):
                S.op(PE, TR(ptb[:, kt * 128:(kt + 1) * 128], hb2[:, kt * 128:(kt + 1) * 128], identb2), reads=[f"hb2{j % 2}", "identb2"], writes=["pt"])
            S.op(DV, TT(h2T3[:, :, j * 128:(j + 1) * 128], ptb.rearrange("p (k n) -> p k n", k=8), bc(g2c, 2, 128), ALU.mult), reads=["pt", "g2c"], writes=["h2T"])
        for ft in range(32):
            bk = ft % 4
            for kt in range(8):
                S.op(PE, MM(pf[bk][:, :], W1b3[:, kt, ft * 128:(ft + 1) * 128], h2T3[:, kt, :], kt == 0, kt == 7), reads=["h2T", f"W1b{ft // 4}"], writes=[f"pf{bk}"])
            S.op(AC, ACT(rl[ft % 2], pf[bk][:, :], AF.Relu), reads=[f"pf{bk}"], writes=[f"rl{ft % 2}"])
            S.op(PL if ft % 2 else DV, TT(aT3[:, ft, :], rl[ft % 2], rl[ft % 2], ALU.mult), reads=[f"rl{ft % 2}"], writes=["aT"])
        for j in range(4):
            pmo, pmn = pmo_pairs[j % 2]
            for hfc in range(2):
                for kt in range(32):
                    S.op(PE, MM(pmo[hfc][:, :], aT3[:, kt, j * 128:(j + 1) * 128], W2b3[:, kt, hfc * 512:(hfc + 1) * 512], kt == 0, kt == 31), reads=["aT", f"W2b{kt // 4}"], writes=[pmn[hfc]])
            for hfc in range(2):
                S.op(AC, ACT(jk[:, 0:512], pmo[hfc][:, :], AF.Square, accum_out=so[:, hfc:hfc + 1]), reads=[pmn[hfc]], writes=["jk", f"so{hfc}"])
            S.op(DV, TT(so[:, 0:1], so[:, 0:1], so[:, 1:2], ALU.add), reads=["so0", "so1"], writes=["so0"])
            rsqrt_cols(ro[:, 0:1], so[:, 0:1], 1.0 / 1024, "so0", "ro")
            for hfc in range(2):
                cs_ = slice(hfc * 512, hfc * 512 + 512)
                S.op(DV, STT(tmb[:, cs_], pmo[hfc][:, :], ro[:, 0:1], GPOST2[:, cs_], ALU.mult, ALU.mult), reads=[pmn[hfc], "ro", "GPOST2"], writes=["tmb"])
            S.op(PL, TT(X1g3[:, j, :], X1g3[:, j, :], tmb, ALU.add), reads=["tmb", f"X1g{j}"], writes=[f"X1g{j}"])
            S.dma(DMA(outv(gi)[:, j, :], X1g3[:, j, :]), reads=[f"X1g{j}"], writes=["out"])
            if gi + 1 < NG:
                ld_sq(gi + 1, j)
    S.barrier()
    sems = {k: es.enter_context(nc.semaphore(f"s_{k[0]}_{k[1]}")) for k in sorted(S.semkeys)}
    with nc.Block() as block:
        @block.tensor
        def _(e):
            S.replay("pe", e, sems)

        @block.scalar
        def _(e):
            S.replay("act", e, sems)

        @block.vector
        def _(e):
            S.replay("dve", e, sems)

        @block.gpsimd
        def _(e):
            S.replay("pool", e, sems)

        @block.sync
        def _(e):
            S.replay("sp", e, sems)
    es.close()
    return nc


def _run(x, params, NPRE, NMAIN, n_cores, core_plan):
    nc = build(NPRE, NMAIN)
    NT = NMAIN * 1024
    f = lambda a: np.ascontiguousarray(np.asarray(a, dtype=np.float32))
    base = {
        "norm_mix_pre": f(params["norm_mix_pre"]).reshape(8, 128), "norm_mix_post": f(params["norm_mix_post"]).reshape(1, 1024),
        "w_in": f(params["w_in"]).reshape(1024, 2560), "ret_gn_gain": f(params["ret_gn_gain"]).reshape(1, 512),
        "ssm_lambda_re": f(params["ssm_lambda_re"]).reshape(32, 64), "ssm_lambda_im": f(params["ssm_lambda_im"]).reshape(32, 64),
        "ssm_log_dt": f(params["ssm_log_dt"]).reshape(1, 32),
        "ssm_b_re": f(params["ssm_b_re"]).reshape(32, 64, 16), "ssm_b_im": f(params["ssm_b_im"]).reshape(32, 64, 16),
        "ssm_c_re": f(params["ssm_c_re"]).reshape(32, 16, 64), "ssm_c_im": f(params["ssm_c_im"]).reshape(32, 16, 64),
        "ssm_d": f(params["ssm_d"]).reshape(32, 16),
        "w_glu": f(params["w_glu"]).reshape(512, 1024), "w_out": f(params["w_out"]).reshape(1024, 1024),
        "norm_mlp_pre": f(params["norm_mlp_pre"]).reshape(8, 128), "norm_mlp_post": f(params["norm_mlp_post"]).reshape(1, 1024),
        "w_ff1": f(params["w_ff1"]).reshape(1024, 4096), "w_ff2": f(params["w_ff2"]).reshape(4096, 1024),
    }
    in_maps = []
    for (b, st) in core_plan:
        m = dict(base)
        m["x_own"] = f(x[b, st:st + NT])
        if st > 0:
            m["x_pre"] = f(x[b, st - NPRE * 1024:st])
            pb = np.array([st - NPRE * 1024, st], np.float32)
        else:
            m["x_pre"] = np.zeros((max(NPRE, 1) * 1024, 1024), np.float32)
            pb = np.array([0.0, 0.0], np.float32)
        m["posb"] = np.ascontiguousarray(np.broadcast_to(pb[None, :], (128, 2)))
        in_maps.append(m)
    res = run_bass_kernel_spmd(nc, in_maps, core_ids=list(range(n_cores)))
    out = np.zeros(x.shape, np.float32)
    for i, (b, st) in enumerate(core_plan):
        out[b, st:st + NT] = res.results[i]["out"]
    return out


def kernel(x, **params):
    x = np.asarray(x, dtype=np.float32)
    plan = [(b, h * 4096) for b in range(4) for h in range(2)]
    return _run(x, params, 4, 4, 8, plan)
```

```python
import math
import os
STOP = os.environ.get('KSTOP')
from contextlib import ExitStack

import numpy as np
import concourse.bass as bass
import concourse.mybir as mybir
from concourse.bass_utils import run_bass_kernel_spmd

F32 = mybir.dt.float32
BF16 = mybir.dt.bfloat16
I32 = mybir.dt.int32
AF = mybir.ActivationFunctionType
ALU = mybir.AluOpType

ENGS = ("pe", "act", "dve", "pool", "sp")
SAME_ENG_WAITS = os.environ.get('KSAME', '1') == '1'
EPOCH = 30000
NDMA = 8


class Sched:
    def __init__(self):
        self.prog = {e: [] for e in ENGS}
        self.cnt = {e: 0 for e in ENGS}
        self.seen = {e: {} for e in ENGS}
        self.lw = {}
        self.rd = {}
        self.dma_i = {}
        self.dma_val = {}
        self.semkeys = set()

    def _deps(self, reads, writes):
        d = {}

        def add(x):
            if x is None:
                return
            s, v = x
            if d.get(s, 0) < v:
                d[s] = v

        for k in reads:
            add(self.lw.get(k))
        for k in writes:
            add(self.lw.get(k))
            for r in self.rd.get(k, ()):
                add(r)
        return d

    def _emit(self, eng, d, fn, my, inc):
        waits = []
        for s, v in d.items():
            if self.seen[eng].get(s, 0) < v:
                self.seen[eng][s] = v
                if s[0] == eng and (eng == "pe" or not SAME_ENG_WAITS):
                    continue
                waits.append((s, v))
        self.prog[eng].append((waits, fn, my, inc))
        if my is not None:
            self.semkeys.add(my[0])

    def _update(self, reads, writes, my):
        for k in writes:
            self.lw[k] = my
            self.rd[k] = []
        for k in reads:
            self.rd.setdefault(k, []).append(my)

    def op(self, eng, fn, reads=(), writes=()):
        self.nrec = getattr(self, 'nrec', 0) + 1
        if self.nrec > int(os.environ.get('KMAX', '100000000')):
            return
        d = self._deps(reads, writes)
        c = self.cnt[eng]
        self.cnt[eng] = c + 1
        my = ((eng, c // EPOCH), c % EPOCH + 1)
        self._emit(eng, d, fn, my, 1)
        self._update(reads, writes, my)

    def dma(self, fn, reads=(), writes=(), q="sp", slow=False):
        if slow and os.environ.get('KNOSLOW'):
            return
        self.nrec = getattr(self, 'nrec', 0) + 1
        if self.nrec > int(os.environ.get('KMAX', '100000000')):
            return
        d = self._deps(reads, writes)
        i = self.dma_i.get(q, 0)
        self.dma_i[q] = (i + 1) % NDMA
        sk = ("dma_" + q, i)
        pv = self.dma_val.get(sk, 0)
        if pv > 0:
            d[sk] = max(d.get(sk, 0), pv)
        self.dma_val[sk] = pv + 16
        my = (sk, pv + 16)
        self._emit(q, d, fn, my, 16)
        self._update(reads, writes, my)

    def barrier(self):
        allv = {}
        for e in ENGS:
            c = self.cnt[e]
            if c > 0:
                allv[(e, (c - 1) // EPOCH)] = (c - 1) % EPOCH + 1
        for sk, v in self.dma_val.items():
            if v > 0:
                allv[sk] = v
        for e in ENGS:
            waits = []
            for s, v in allv.items():
                if self.seen[e].get(s, 0) < v:
                    self.seen[e][s] = v
                    waits.append((s, v))
            self.prog[e].append((waits, None, None, 0))
        self.lw = {}
        self.rd = {}

    def replay(self, eng, e, sems):
        for waits, fn, my, inc in self.prog[eng]:
            for s, v in waits:
                e.wait_ge(sems[s], v)
            if fn is not None:
                ins = fn(e)
                ins.then_inc(sems[my[0]], inc)


def bc(ap, axis, n):
    a = ap.unsqueeze(axis)
    shp = list(a.shape)
    shp[axis] = n
    return a.broadcast_to(shp)


class Arena:
    def __init__(self, A):
        self.A = A
        self.Ab = A.bitcast(BF16)
        self.Ai = A.bitcast(I32)
        self.top = 0

    def f32(self, n):
        o = self.top
        self.top += n
        return self.A[:, o:o + n]

    def i32(self, n):
        o = self.top
        self.top += n
        return self.Ai[:, o:o + n]

    def bf(self, n):
        o = self.top
        self.top += (n + 1) // 2
        return self.Ab[:, 2 * o:2 * o + n]


LNG = [math.log(1.0 - math.exp(v)) for v in np.linspace(math.log(1.0 / 32), math.log(1.0 / 512), 4)]
GHEAD = [math.exp(128 * v) for v in LNG]
INVF = (np.float32(10000.0) ** (-(np.arange(64, dtype=np.float32) / np.float32(64)))).astype(np.float32)
TWO_PI = 2.0 * math.pi
C1 = 6.28125
C2 = TWO_PI - C1
AW = 52224


DBG = {}


def build(NPRE, NMAIN):
    nc = bass.Bass("TRN2", target_bir_lowering=False)
    NT = NMAIN * 1024
    dr = lambda n, s, dt=F32, kind="ExternalInput": nc.dram_tensor(n, s, dt, kind=kind).ap()
    x_own = dr("x_own", [NT, 1024])
    x_pre = dr("x_pre", [max(NPRE, 1) * 1024, 1024])
    posb = dr("posb", [128, 2])
    g_pre_d = dr("norm_mix_pre", [8, 128]); g_post_d = dr("norm_mix_post", [1, 1024])
    w_in_d = dr("w_in", [1024, 2560]); ggn_d = dr("ret_gn_gain", [1, 512])
    lre_d = dr("ssm_lambda_re", [32, 64]); lim_d = dr("ssm_lambda_im", [32, 64]); ldt_d = dr("ssm_log_dt", [1, 32])
    bre_d = dr("ssm_b_re", [32, 64, 16]); bim_d = dr("ssm_b_im", [32, 64, 16])
    cre_d = dr("ssm_c_re", [32, 16, 64]); cim_d = dr("ssm_c_im", [32, 16, 64]); sd_d = dr("ssm_d", [32, 16])
    w_glu_d = dr("w_glu", [512, 1024]); w_out_d = dr("w_out", [1024, 1024])
    g2pre_d = dr("norm_mlp_pre", [8, 128]); g2post_d = dr("norm_mlp_post", [1, 1024])
    w1_d = dr("w_ff1", [1024, 4096]); w2_d = dr("w_ff2", [4096, 1024])
    out_d = dr("out", [NT, 1024], kind="ExternalOutput")
    x1s = dr("x1s", [NT, 1024], kind="Internal")

    S = Sched()
    es = ExitStack()
    A_t = es.enter_context(nc.sbuf_tensor("arena", [128, AW], F32))
    PSt = [es.enter_context(nc.psum_tensor(f"ps{i}", [128, 512], F32)) for i in range(8)]
    ar = Arena(A_t)

    def psv(i, n=1):
        assert n == 1
        return PSt[i][:, :]

    def dbc(ap1, n=128):
        return bass.AP(ap1.tensor, ap1.offset, [[0, n]] + [list(d) for d in ap1.ap[1:]])

    DV, AC, PL, PE = "dve", "act", "pool", "pe"
    TT = lambda o, a, b, op: (lambda e: e.tensor_tensor(out=o, in0=a, in1=b, op=op))
    TS = lambda o, a, s1, s2, op0, op1=None: (lambda e: e.tensor_scalar(out=o, in0=a, scalar1=s1, scalar2=s2, op0=op0, op1=op1) if op1 is not None
                                              else e.tensor_scalar(out=o, in0=a, scalar1=s1, scalar2=None, op0=op0))
    STT = lambda o, a, s, b, op0, op1: (lambda e: e.scalar_tensor_tensor(out=o, in0=a, scalar=s, in1=b, op0=op0, op1=op1))
    CP = lambda o, a: (lambda e: e.tensor_copy(out=o, in_=a))
    ACT = lambda o, a, f, **kw: (lambda e: e.activation(out=o, in_=a, func=f, **kw))
    MM = lambda o, l, r, st, sp: (lambda e: e.matmul(o, lhsT=l, rhs=r, start=st, stop=sp))
    TR = lambda o, a, idn: (lambda e: e.transpose(o, a, idn))
    DMA = lambda o, a, **kw: (lambda e: e.dma_start(out=o, in_=a, **kw))
    MS = lambda o, v: (lambda e: e.memset(o, v))

    Win = ar.bf(8 * 2560); Win3 = Win.rearrange("p (k c) -> p k c", k=8)
    Wintra = ar.bf(32 * 128); Wintra3 = Wintra.rearrange("p (g c) -> p g c", g=32)
    Wst = ar.bf(32 * 128); Wst4 = Wst.rearrange("p (g r q) -> p g r q", g=32, r=2)
    Wcr = ar.bf(2 * 16 * 128); Wcr4 = Wcr.rearrange("p (r g c) -> p r g c", r=2, g=16)
    COS = ar.f32(16 * 64); COS3 = COS.rearrange("p (g m) -> p g m", g=16)
    SIN = ar.f32(16 * 64); SIN3 = SIN.rearrange("p (g m) -> p g m", g=16)
    R8 = ar.f32(16)
    CAR = ar.f32(32); CAR3 = CAR.rearrange("p (r g) -> p r g", r=2)
    identb = ar.bf(128); identf = ar.f32(128)
    mask01 = ar.f32(128)
    MASKH = ar.f32(512); MASKH3 = MASKH.rearrange("p (h i) -> p h i", h=4)
    zcol = ar.f32(4); xi = ar.f32(4); xi2 = ar.f32(4); pidx = ar.f32(1); pidx1 = ar.f32(1)
    GPOST = ar.f32(1024); GGN = ar.f32(512)
    gpre = ar.f32(8); g2pre = ar.f32(8)
    invf = ar.f32(64)
    iopc = ar.f32(8)
    posc = ar.f32(2)
    Rst = ar.f32(512); Rst3 = Rst.rearrange("p (h d) -> p h d", h=4)
    Rb = ar.bf(512); Rb3 = Rb.rearrange("p (h d) -> p h d", h=4)
    yrT = ar.bf(4 * 1024); yrT3 = yrT.rearrange("p (k t) -> p k t", k=4)
    Tb = ar.bf(4096)
    Rb2 = ar.bf(512)
    Wglu = ar.bf(4096); Wglu3 = Wglu.rearrange("p (k c) -> p k c", k=4)
    Wout = ar.bf(8192); Wout3 = Wout.rearrange("p (k c) -> p k c", k=8)
    mhalf = ar.f32(8)
    P_BASE = ar.top

    m0 = ar.top
    ioi = ar.i32(128); iof = ar.f32(128)
    S.op(PL, lambda e: e.iota(ioi, pattern=[[1, 128]], base=0, channel_multiplier=-1), writes=["ioi"])
    S.op(DV, CP(iof, ioi), reads=["ioi"], writes=["iof"])
    S.op(DV, lambda e: e.tensor_single_scalar(identf, iof, 0.0, op=ALU.is_equal), reads=["iof"], writes=["identf"])
    S.op(DV, CP(identb, identf), reads=["identf"], writes=["identb"])
    S.op(DV, lambda e: e.tensor_single_scalar(mask01, iof, 0.0, op=ALU.is_ge), reads=["iof"], writes=["mask01"])
    pii = ar.i32(1)
    S.op(PL, lambda e: e.iota(pii, pattern=[[0, 1]], base=0, channel_multiplier=1), writes=["pii"])
    S.op(DV, CP(pidx, pii), reads=["pii"], writes=["pidx"])
    S.op(DV, TS(pidx1, pidx, 1.0, None, ALU.add), reads=["pidx"], writes=["pidx1"])
    pm = ar.f32(1)
    S.op(DV, TS(pm, pidx, -1.0, 127.0, ALU.mult, ALU.add), reads=["pidx"], writes=["pm"])
    gcol = ar.f32(4)
    DH = 128.0 ** -0.5
    for h in range(4):
        S.op(AC, ACT(gcol[:, h:h + 1], pidx1, AF.Exp, scale=-LNG[h]), reads=["pidx1"], writes=["gcol"])
        S.op(AC, ACT(zcol[:, h:h + 1], pm, AF.Exp, scale=LNG[h]), reads=["pm"], writes=["zcol"])
        S.op(AC, ACT(xi[:, h:h + 1], pidx1, AF.Exp, scale=LNG[h]), reads=["pidx1"], writes=["xi"])
    S.op(DV, TS(gcol, gcol, DH, None, ALU.mult), reads=["gcol"], writes=["gcol"])
    S.op(DV, TS(zcol, zcol, DH, None, ALU.mult), reads=["zcol"], writes=["zcol"])
    S.op(DV, TT(xi2, xi, xi, ALU.mult), reads=["xi"], writes=["xi2"])
    for h in range(4):
        S.op(DV, TS(MASKH3[:, h, :], mask01, gcol[:, h:h + 1], None, ALU.mult), reads=["mask01", "gcol"], writes=["MASKH"])
    for j in range(64):
        S.op(PL, MS(invf[:, j:j + 1], float(INVF[j])), writes=["invf"])
    ioci = ar.i32(8)
    S.op(PL, lambda e: e.iota(ioci, pattern=[[128, 8]], base=0, channel_multiplier=1), writes=["ioci"])
    S.op(DV, CP(iopc, ioci), reads=["ioci"], writes=["iopc"])
    S.dma(DMA(posc, posb), writes=["posc"])
    S.dma(DMA(GPOST, dbc(g_post_d)), writes=["GPOST"])
    S.dma(DMA(GGN, dbc(ggn_d)), writes=["GGN"])
    g8 = ar.f32(128)
    for (src, dst, nm) in ((g_pre_d, gpre, "gpre"), (g2pre_d, g2pre, "g2pre")):
        S.dma(DMA(g8[0:8, :], src), writes=["g8"])
        S.op(PE, TR(PSt[0][:, 0:8], g8[0:8, :], identf[0:8, 0:8]), reads=["g8", "identf"], writes=["ps0"])
        S.op(DV, CP(dst, PSt[0][:, 0:8]), reads=["ps0"], writes=[nm])
    w_in_v = w_in_d.rearrange("(k p) c -> p k c", p=128)
    wg_v = w_glu_d.rearrange("(k p) c -> p k c", p=128)
    wo_v = w_out_d.rearrange("(k p) c -> p k c", p=128)
    for kt in range(8):
        S.dma(DMA(Win3[:, kt, :], w_in_v[:, kt, :], max_dma_last_dim=4096), q="pool", writes=["Win"])
    LRE = ar.f32(16); LIM = ar.f32(16); DT = ar.f32(16)
    BRE = ar.f32(256); BIM = ar.f32(256); CRE = ar.f32(256); CIM = ar.f32(256)
    Dcol = ar.f32(32)
    for par in range(2):
        ps_ = slice(64 * par, 64 * par + 64)
        for (dst, src, nm) in ((LRE, lre_d, "LRE"), (LIM, lim_d, "LIM")):
            S.dma(DMA(dst[ps_, :], bass.AP(src.tensor, par * 64, [[1, 64], [128, 16]]), allow_slow_non_contiguous=True), slow=True, writes=[nm])
        S.dma(DMA(DT[ps_, :], bass.AP(ldt_d.tensor, par, [[0, 64], [2, 16]]), allow_slow_non_contiguous=True), slow=True, writes=["DT"])
        for (dst, src, nm) in ((BRE, bre_d, "BRE"), (BIM, bim_d, "BIM")):
            S.dma(DMA(dst[ps_, :].rearrange("p (g c) -> p g c", g=16), bass.AP(src.tensor, par * 1024, [[16, 64], [2048, 16], [1, 16]])), writes=[nm])
    for par in range(2):
        ps_ = slice(64 * par, 64 * par + 64)
        for (dst, src, nm) in ((CRE, cre_d, "CRE"), (CIM, cim_d, "CIM")):
            for gg in range(16):
                S.dma(DMA(dst[ps_, gg * 16:(gg + 1) * 16], bass.AP(src.tensor, par * 1024 + gg * 2048, [[1, 64], [64, 16]]), allow_slow_non_contiguous=True),
                      slow=True, writes=[nm], q=("sp" if nm == "CRE" else "act"))
    for sp_ in range(8):
        S.dma(DMA(Dcol[16 * sp_:16 * sp_ + 16, :], bass.AP(sd_d.tensor, 0, [[1, 16], [16, 32]]), allow_slow_non_contiguous=True), slow=True, writes=["Dcol"])
    cnt = [0]

    def tmp(n):
        return ar.f32(n)

    def vop(fn, reads, writes):
        cnt[0] += 1
        S.op(DV, fn, reads=reads, writes=writes)

    a_ = tmp(16); th = tmp(16); rr = tmp(16); kf = tmp(16); ki = ar.i32(16); red = tmp(16); sn = tmp(16); cs = tmp(16); ab = tmp(16)
    vop(TS(LRE, LRE, -1e-4, None, ALU.min), ["LRE"], ["LRE"])
    S.op(AC, ACT(DT, DT, AF.Exp), reads=["DT"], writes=["DT"])
    vop(TT(a_, LRE, DT, ALU.mult), ["LRE", "DT"], ["a_"])
    vop(TT(th, LIM, DT, ALU.mult), ["LIM", "DT"], ["th"])
    S.op(AC, ACT(rr, a_, AF.Exp), reads=["a_"], writes=["rr"])
    vop(TS(kf, th, 1.0 / TWO_PI, None, ALU.mult), ["th"], ["kf"])
    vop(CP(ki, kf), ["kf"], ["ki"])
    vop(CP(kf, ki), ["ki"], ["kf"])
    vop(STT(red, kf, -C1, th, ALU.mult, ALU.add), ["kf", "th"], ["red"])
    vop(STT(red, kf, -C2, red, ALU.mult, ALU.add), ["kf", "red"], ["red"])
    vop(TS(red, red, math.pi, -math.pi, ALU.min, ALU.max), ["red"], ["red"])
    S.op(AC, ACT(sn, red, AF.Sin), reads=["red"], writes=["sn"])
    vop(TS(ab, red, -1.0, None, ALU.mult), ["red"], ["ab"])
    vop(TT(ab, ab, red, ALU.max), ["ab", "red"], ["ab"])
    vop(TS(ab, ab, -1.0, math.pi / 2, ALU.mult, ALU.add), ["ab"], ["ab"])
    S.op(AC, ACT(cs, ab, AF.Sin), reads=["ab"], writes=["cs"])
    PR = tmp(9 * 16); PI = tmp(9 * 16); QR = tmp(8 * 16); QI = tmp(8 * 16)
    PR3 = PR.rearrange("p (j g) -> p j g", g=16); PI3 = PI.rearrange("p (j g) -> p j g", g=16)
    QR3 = QR.rearrange("p (j g) -> p j g", g=16); QI3 = QI.rearrange("p (j g) -> p j g", g=16)
    t1 = tmp(16); t2 = tmp(16); ivr = tmp(16); ivi = tmp(16); r2 = tmp(16)
    vop(MS(PR3[:, 0, :], 1.0), [], ["PR"]); vop(MS(PI3[:, 0, :], 0.0), [], ["PI"])
    vop(MS(QR3[:, 0, :], 1.0), [], ["QR"]); vop(MS(QI3[:, 0, :], 0.0), [], ["QI"])
    vop(TT(PR3[:, 1, :], rr, cs, ALU.mult), ["rr", "cs", "PR"], ["PR"])
    vop(TT(PI3[:, 1, :], rr, sn, ALU.mult), ["rr", "sn", "PI"], ["PI"])
    vop(TT(r2, rr, rr, ALU.mult), ["rr"], ["r2"])
    vop(lambda e: e.reciprocal(out=r2, in_=r2), ["r2"], ["r2"])
    vop(TT(ivr, PR3[:, 1, :], r2, ALU.mult), ["PR", "r2"], ["ivr"])
    vop(TT(ivi, PI3[:, 1, :], r2, ALU.mult), ["PI", "r2"], ["ivi"])
    vop(TS(ivi, ivi, -1.0, None, ALU.mult), ["ivi"], ["ivi"])

    def cmul(outr, outi, ar_, ai_, br_, bi_, rk, wk):
        vop(TT(t1, ar_, br_, ALU.mult), rk, ["t1"]); vop(TT(t2, ai_, bi_, ALU.mult), rk, ["t2"])
        vop(TT(outr, t1, t2, ALU.subtract), ["t1", "t2"] + wk, wk)
        vop(TT(t1, ar_, bi_, ALU.mult), rk + ["t1"], ["t1"]); vop(TT(t2, ai_, br_, ALU.mult), rk + ["t2"], ["t2"])
        vop(TT(outi, t1, t2, ALU.add), ["t1", "t2"] + wk, wk)

    for j in range(1, 8):
        cmul(PR3[:, j + 1, :], PI3[:, j + 1, :], PR3[:, j, :], PI3[:, j, :], PR3[:, 1, :], PI3[:, 1, :], ["PR", "PI"], ["PR", "PI"])
    vop(CP(QR3[:, 1, :], ivr), ["ivr", "QR"], ["QR"]); vop(CP(QI3[:, 1, :], ivi), ["ivi", "QI"], ["QI"])
    for j in range(1, 7):
        cmul(QR3[:, j + 1, :], QI3[:, j + 1, :], QR3[:, j, :], QI3[:, j, :], ivr, ivi, ["QR", "QI", "ivr", "ivi"], ["QR", "QI"])
    nr = tmp(16); ni = tmp(16); den = tmp(16); lb1 = tmp(16); cr = tmp(16); ci = tmp(16)
    vop(TS(lb1, PR3[:, 1, :], -1.0, None, ALU.add), ["PR"], ["lb1"])
    vop(TT(t1, lb1, LRE, ALU.mult), ["lb1", "LRE"], ["t1"]); vop(TT(t2, PI3[:, 1, :], LIM, ALU.mult), ["PI", "LIM"], ["t2"])
    vop(TT(nr, t1, t2, ALU.add), ["t1", "t2"], ["nr"])
    vop(TT(t1, PI3[:, 1, :], LRE, ALU.mult), ["PI", "LRE", "t1"], ["t1"]); vop(TT(t2, lb1, LIM, ALU.mult), ["lb1", "LIM", "t2"], ["t2"])
    vop(TT(ni, t1, t2, ALU.subtract), ["t1", "t2"], ["ni"])
    vop(TT(t1, LRE, LRE, ALU.mult), ["LRE", "t1"], ["t1"]); vop(TT(t2, LIM, LIM, ALU.mult), ["LIM", "t2"], ["t2"])
    vop(TT(den, t1, t2, ALU.add), ["t1", "t2"], ["den"])
    vop(lambda e: e.reciprocal(out=den, in_=den), ["den"], ["den"])
    vop(TT(cr, nr, den, ALU.mult), ["nr", "den"], ["cr"]); vop(TT(ci, ni, den, ALU.mult), ["ni", "den"], ["ci"])
    BBR = tmp(256); BBI = tmp(256); u1 = tmp(256); u2 = tmp(256)
    v3 = lambda a: a.rearrange("p (g c) -> p g c", g=16)
    b16 = lambda a: bc(a, 2, 16)

    def cmul3(outr, outi, sr, si, xr, xi_, rk, wk, neg_im=False):
        vop(TT(v3(u1), v3(xr), b16(sr), ALU.mult), rk, ["u1"]); vop(TT(v3(u2), v3(xi_), b16(si), ALU.mult), rk, ["u2"])
        vop(TT(outr, v3(u1), v3(u2), ALU.subtract), ["u1", "u2"] + wk, wk)
        vop(TT(v3(u1), v3(xi_), b16(sr), ALU.mult), rk + ["u1"], ["u1"]); vop(TT(v3(u2), v3(xr), b16(si), ALU.mult), rk + ["u2"], ["u2"])
        if neg_im:
            vop(STT(outi, v3(u1), -1.0, v3(u2), ALU.mult, ALU.subtract), ["u1", "u2"] + wk, wk)
        else:
            vop(TT(outi, v3(u1), v3(u2), ALU.add), ["u1", "u2"] + wk, wk)

    cmul3(v3(BBR), v3(BBI), cr, ci, BRE, BIM, ["cr", "ci", "BRE", "BIM"], ["BB"])
    EN = ar.bf(16 * 256); EQ = tmp(16 * 256); G = tmp(16 * 2 * 9 * 16)
    EN5 = EN.rearrange("p (g r s c) -> p g r s c", g=16, r=2, s=8)
    EQ5 = EQ.rearrange("p (g r s c) -> p g r s c", g=16, r=2, s=8)
    G5 = G.rearrange("p (g r j c) -> p g r j c", g=16, r=2, j=9)
    for s_ in range(8):
        cmul3(EN5[:, :, 0, s_, :], EN5[:, :, 1, s_, :], PR3[:, 7 - s_, :], PI3[:, 7 - s_, :], BBR, BBI, ["PR", "PI", "BB"], ["EN"])
        cmul3(EQ5[:, :, 0, s_, :], EQ5[:, :, 1, s_, :], QR3[:, s_, :], QI3[:, s_, :], BBR, BBI, ["QR", "QI", "BB"], ["EQ"])
    for j in range(9):
        cmul3(G5[:, :, 0, j, :], G5[:, :, 1, j, :], PR3[:, j, :], PI3[:, j, :], CRE, CIM, ["PR", "PI", "CRE", "CIM"], ["G"], neg_im=True)
    for ri in range(2):
        vop(CP(Wcr4[:, ri, :, :].rearrange("p g (s c) -> p g s c", s=8), G5[:, :, ri, 1:9, :]), ["G"], ["Wcr"])
    Wst5 = Wst.rearrange("p (g a r q) -> p g a r q", g=16, a=2, r=2)
    ENb5 = EN5
    ptbs = [PSt[5].bitcast(BF16), PSt[4].bitcast(BF16)]
    for g2 in range(16):
        for ri in range(2):
            k = ri
            S.op(PE, TR(ptbs[k][:, 0:128], ENb5[:, g2, ri, :, :].rearrange("p s c -> p (s c)"), identb), reads=["EN", "identb"], writes=[f"psb{k}"])
            S.op(AC, ACT(Wst5[:, g2, :, ri, :], ptbs[k][:, 0:128].rearrange("p (a q) -> p a q", a=2), AF.Copy),
                 reads=[f"psb{k}"], writes=["Wst"])
    si_ = ar.i32(128); sf_ = tmp(128); ri_ = ar.i32(1); rf_ = tmp(1); BLK = tmp(128); wtmp = [tmp(128), tmp(128)]
    S.op(PL, lambda e: e.iota(si_, pattern=[[1, 128]], base=0, channel_multiplier=0), writes=["si_"])
    vop(CP(sf_, si_), ["si_"], ["sf_"])
    vop(TS(sf_, sf_, 1.0 / 16, -0.46875, ALU.mult, ALU.add), ["sf_"], ["sf_"])
    vop(CP(si_, sf_), ["sf_"], ["si_"])
    vop(CP(sf_, si_), ["si_"], ["sf_"])
    vop(TS(rf_, pidx, 1.0 / 16, -0.46875, ALU.mult, ALU.add), ["pidx"], ["rf_"])
    vop(CP(ri_, rf_), ["rf_"], ["ri_"])
    vop(CP(rf_, ri_), ["ri_"], ["rf_"])
    vop(TS(BLK, sf_, rf_[:, 0:1], None, ALU.is_ge), ["sf_", "rf_"], ["BLK"])
    for g in range(32):
        par, g2 = g % 2, g // 2
        ps_ = slice(64 * par, 64 * par + 64)
        k = 2 + g % 2
        for ri in range(2):
            S.op(PE, MM(PSt[k][:, 0:128], EQ5[ps_, g2, ri, :, :].rearrange("p s c -> p (s c)"),
                        G5[ps_, g2, ri, 0:8, :].rearrange("p s c -> p (s c)"), ri == 0, ri == 1),
                 reads=["EQ", "G"], writes=[f"ps{k}"])
        S.op(DV, TT(wtmp[g % 2], PSt[k][:, 0:128], BLK, ALU.mult), reads=[f"ps{k}", "BLK"], writes=[f"wtmp{g % 2}"])
        S.op(DV, STT(Wintra3[:, g, :], identf, Dcol[:, g:g + 1], wtmp[g % 2], ALU.mult, ALU.add), reads=[f"wtmp{g % 2}", "identf", "Dcol"], writes=["Wintra"])
    ur = tmp(16); ui = tmp(16); wr = tmp(16); wi = tmp(16); w2r = tmp(16); w2i = tmp(16); a8 = tmp(16)
    vop(TS(a8, a_, 8.0, None, ALU.mult), ["a_"], ["a8"])
    S.op(AC, ACT(R8, a8, AF.Exp), reads=["a8"], writes=["R8"])
    S.op(AC, ACT(a8, a8, AF.Exp, scale=-1.0), reads=["a8"], writes=["a8"])
    vop(TT(ur, PR3[:, 8, :], a8, ALU.mult), ["PR", "a8"], ["ur"]); vop(TT(ui, PI3[:, 8, :], a8, ALU.mult), ["PI", "a8"], ["ui"])
    vop(CP(COS3[:, :, 0], ur), ["ur"], ["COS"]); vop(CP(SIN3[:, :, 0], ui), ["ui"], ["SIN"])
    vop(CP(wr, ur), ["ur"], ["wr"]); vop(CP(wi, ui), ["ui"], ["wi"])
    e1 = tmp(16 * 32); e2 = tmp(16 * 32)
    for k in range(6):
        n = 1 << k
        e1v = e1[:, 0:16 * n].rearrange("p (g m) -> p g m", g=16); e2v = e2[:, 0:16 * n].rearrange("p (g m) -> p g m", g=16)
        wrb = bc(wr, 2, n); wib = bc(wi, 2, n)
        vop(TT(e1v, COS3[:, :, 0:n], wrb, ALU.mult), ["COS", "wr"], ["e1"]); vop(TT(e2v, SIN3[:, :, 0:n], wib, ALU.mult), ["SIN", "wi"], ["e2"])
        vop(TT(COS3[:, :, n:2 * n], e1v, e2v, ALU.subtract), ["e1", "e2", "COS"], ["COS"])
        vop(TT(e1v, COS3[:, :, 0:n], wib, ALU.mult), ["COS", "wi", "e1"], ["e1"]); vop(TT(e2v, SIN3[:, :, 0:n], wrb, ALU.mult), ["SIN", "wr", "e2"], ["e2"])
        vop(TT(SIN3[:, :, n:2 * n], e1v, e2v, ALU.add), ["e1", "e2", "SIN"], ["SIN"])
        if k < 5:
            cmul(w2r, w2i, wr, wi, wr, wi, ["wr", "wi"], ["w2"])
            vop(CP(wr, w2r), ["w2"], ["wr"]); vop(CP(wi, w2i), ["w2"], ["wi"])
    DBG.update(ur=ur, ui=ui, a8=a8, wr=wr, wi=wi, PR=PR, PI=PI, e1=e1, e2=e2)
    DBG.update(Wintra=Wintra, Wst=Wst, Wcr=Wcr, COS=COS, SIN=SIN, R8=R8, MASKH=MASKH, Win=Win, zcol=zcol, xi=xi, gpre=gpre, invf=invf, identb=identb, GPOST=GPOST)
    S.op(PL, MS(mhalf, -0.5), writes=["mhalf"])
    S.op(PL, MS(Rst, 0.0), writes=["R"]); S.op(PL, MS(Rb, 0.0), writes=["Rb0"]); S.op(PL, MS(Rb2, 0.0), writes=["Rb1"]); S.op(PL, MS(CAR, 0.0), writes=["CAR"])
    S.barrier()
    ar.top = P_BASE
    EPS = 1e-6
    pt_t = PSt[4]
    ptb = pt_t.bitcast(BF16)
    pss, pso, psu = PSt[5], PSt[6], PSt[7]
    mA = ar.top
    hT = ar.bf(8 * 1024); hT3 = hT.rearrange("p (k t) -> p k t", k=8)
    xs = [ar.f32(1024), ar.f32(1024), ar.f32(1024)]
    hb = [ar.bf(1024), ar.bf(1024)]
    junk = ar.bf(1024)
    ssq = ar.f32(8); rstd = ar.f32(8)
    rcos = ar.f32(512); rsin = ar.f32(512); rang = ar.f32(512); rk = ar.f32(512); rki = ar.i32(512); rm = ar.f32(512); posf = ar.f32(8)
    rcos3 = rcos.rearrange("p (c j) -> p c j", c=8); rsin3 = rsin.rearrange("p (c j) -> p c j", c=8)
    qkr = [ar.bf(1024), ar.bf(1024)]
    qkT = [ar.bf(1024), ar.bf(1024)]
    vb = [ar.bf(512), ar.bf(512)]; vz = [ar.bf(512), ar.bf(512)]; sTb = ar.bf(512)
    h4 = lambda a: a.rearrange("p (h d) -> p h d", h=4)
    a8v = lambda a: a.rearrange("p (a d) -> p a d", a=8)
    sTb3 = h4(sTb)
    rt = [ar.f32(512) for _ in range(4)]
    yn = ar.f32(512); sgg = [ar.f32(512), ar.f32(512)]; yr = ar.bf(512)
    bst = ar.f32(24); mv = ar.f32(8); rs = ar.f32(4); vtmp = ar.f32(4)
    mR_end = ar.top
    ar.top = mA
    U8 = ar.bf(4096); U83 = U8.rearrange("p (g n) -> p g n", g=32)
    Y8 = ar.bf(4096); Y83 = Y8.rearrange("p (g n) -> p g n", g=32)
    SPV = ar.bf(2 * 16 * 128); SPV4 = SPV.rearrange("p (r g n) -> p r g n", r=2, g=16)
    PRE = [[ar.f32(256), ar.f32(256)] for _ in range(2)]; SCN = [[ar.f32(256), ar.f32(256)] for _ in range(2)]
    st_ = [[ar.f32(256) for _ in range(4)] for _ in range(2)]
    FUL = [[ar.f32(256), ar.f32(256)] for _ in range(2)]
    ysT = [ar.bf(512), ar.bf(512)]; ys2 = [ar.bf(512), ar.bf(512)]; ys2T = [ar.bf(512), ar.bf(512)]
    sig = ar.f32(512); xs2 = [ar.f32(1024), ar.f32(1024)]; tm = [ar.f32(1024), ar.f32(1024)]
    junk2 = ar.bf(1024); ss2 = [ar.f32(2), ar.f32(2)]; rs2 = [ar.f32(2), ar.f32(2)]
    mS_end = ar.top
    assert max(mR_end, mS_end) <= AW, (mR_end, mS_end)
    v34 = lambda a: a.rearrange("p (g m) -> p g m", g=4)
    Tb4 = Tb.rearrange("p (g s c) -> p g s c", g=32, s=8)
    YT4 = Tb.rearrange("p (s g c) -> p s g c", s=8, g=32)
    YT3 = Tb.rearrange("p (s f) -> p s f", s=8)
    Rbs = [Rb, Rb2]

    mh = [mhalf]

    def rsqrt_cols(dst, src, scale, nm_src, nm_dst):
        n = dst.shape[1]
        S.op(DV, TS(dst, src, scale, EPS, ALU.mult, ALU.add), reads=[nm_src], writes=[nm_dst])
        S.op(PL, TT(dst, dst, mh[0][:, 0:n], ALU.pow), reads=[nm_dst, "mhalf"], writes=[nm_dst])

    rcnt = [0]
    for kt in range(4):
        S.dma(DMA(Wglu3[:, kt, :], wg_v[:, kt, :], max_dma_last_dim=4096), q="pool", writes=["Wglu"])
    for kt in range(8):
        S.dma(DMA(Wout3[:, kt, :], wo_v[:, kt, :], max_dma_last_dim=4096), q="pool", writes=["Wout"])
    for sbi in range((NPRE + NMAIN) if STOP is None else int(STOP)):
        pre = sbi < NPRE
        xsrc = x_pre if pre else x_own
        row0 = 1024 * (sbi if pre else sbi - NPRE)
        pcol = 0 if pre else 1
        xv = xsrc[row0:row0 + 1024, :].rearrange("(n s) d -> n s d", s=8)
        S.op(DV, TS(posf, iopc, posc[:, pcol:pcol + 1], float(row0), ALU.add, ALU.add), reads=["iopc", "posc", "rang"], writes=["posf"])
        rang3 = rang.rearrange("p (c j) -> p c j", c=8)
        S.op(DV, TT(rang3, bc(posf, 2, 64), bc(invf, 1, 8), ALU.mult), reads=["posf", "invf", "rcos"], writes=["rang"])
        S.op(DV, TS(rk, rang, 1.0 / TWO_PI, None, ALU.mult), reads=["rang"], writes=["rk"])
        S.op(DV, CP(rki, rk), reads=["rk"], writes=["rki"])
        S.op(DV, CP(rk, rki), reads=["rki"], writes=["rk"])
        S.op(DV, STT(rm, rk, -C1, rang, ALU.mult, ALU.add), reads=["rk", "rang"], writes=["rm"])
        S.op(DV, STT(rm, rk, -C2, rm, ALU.mult, ALU.add), reads=["rk", "rm"], writes=["rm"])
        S.op(DV, TS(rm, rm, math.pi, -math.pi, ALU.min, ALU.max), reads=["rm"], writes=["rm"])
        S.op(AC, ACT(rsin, rm, AF.Sin), reads=["rm"], writes=["rsin"])
        S.op(DV, TS(rk, rm, -1.0, None, ALU.mult), reads=["rm", "rk"], writes=["rk"])
        S.op(DV, TT(rk, rk, rm, ALU.max), reads=["rm", "rk"], writes=["rk"])
        S.op(DV, TS(rk, rk, -1.0, math.pi / 2, ALU.mult, ALU.add), reads=["rk"], writes=["rk"])
        S.op(AC, ACT(rcos, rk, AF.Sin), reads=["rk"], writes=["rcos"])

        def ht1(s):
            b = s % 3
            S.dma(DMA(xs[b], xsrc[row0 + 128 * s:row0 + 128 * s + 128, :]), writes=[f"xs{b}"])
            S.op(AC, ACT(junk, xs[b], AF.Square, accum_out=ssq[:, s:s + 1]), reads=[f"xs{b}"], writes=["junk", f"ssq{s}"])

        def ht2(s):
            b = s % 2; bx = s % 3
            rsqrt_cols(rstd[:, s:s + 1], ssq[:, s:s + 1], 1.0 / 1024, f"ssq{s}", f"rstd{s}")
            S.op(AC, ACT(hb[b], xs[bx], AF.Copy, scale=rstd[:, s:s + 1]), reads=[f"xs{bx}", f"rstd{s}"], writes=[f"hb{b}"])
            tbk, tnm = (ptb, "pt") if s % 2 == 0 else (PSt[5].bitcast(BF16), "pss")
            for kt in range(8):
                S.op(PE, TR(tbk[:, kt * 128:(kt + 1) * 128], hb[b][:, kt * 128:(kt + 1) * 128], identb), reads=[f"hb{b}", "identb"], writes=[tnm])
            S.op(DV, TT(hT3[:, :, 128 * s:128 * s + 128], tbk.rearrange("p (k n) -> p k n", k=8), bc(gpre, 2, 128), ALU.mult), reads=[tnm, "gpre"], writes=["hT"])

        if os.environ.get("KSEQ_H"):
            for s in range(8):
                ht1(s); ht2(s)
        else:
            ht1(0); ht1(1)
            for s in range(8):
                if s + 2 < 8:
                    ht1(s + 2)
                ht2(s)

        tkc = lambda c: slice(128 * c, 128 * c + 128)
        if pre:
            def A(c):
                bk, bv = (PSt[0], PSt[1]) if c % 2 == 0 else (PSt[2], PSt[3])
                nk, nv = ("ps0", "ps1") if c % 2 == 0 else ("ps2", "ps3")
                for (bank, col0, nm) in ((bk, 512, nk), (bv, 1024, nv)):
                    for kt in range(8):
                        S.op(PE, MM(bank[:, :], hT3[:, kt, tkc(c)], Win3[:, kt, col0:col0 + 512], kt == 0, kt == 7), reads=["hT", "Win"], writes=[nm])

            def B(c):
                b = c % 2
                bk, bv = (PSt[0], PSt[1]) if c % 2 == 0 else (PSt[2], PSt[3])
                nk, nv = ("ps0", "ps1") if c % 2 == 0 else ("ps2", "ps3")
                cosb = bc(rcos3[:, c, :], 1, 4); sinb = bc(rsin3[:, c, :], 1, 4)
                xq = bk[:, :].rearrange("p (h a j) -> p h a j", h=4, a=2)
                x1 = xq[:, :, 0, :]; x2 = xq[:, :, 1, :]
                r3 = [t_[:, 0:256].rearrange("p (h j) -> p h j", h=4) for t_ in rt]
                S.op(DV, TT(r3[0], x1, cosb, ALU.mult), reads=[nk, "rcos"], writes=["rt0"])
                S.op(DV, TT(r3[1], x2, sinb, ALU.mult), reads=[nk, "rsin"], writes=["rt1"])
                S.op(DV, TT(r3[2], x1, sinb, ALU.mult), reads=[nk, "rsin"], writes=["rt2"])
                S.op(DV, TT(r3[3], x2, cosb, ALU.mult), reads=[nk, "rcos"], writes=["rt3"])
                q4 = qkr[b].rearrange("p (a h j) -> p a h j", a=8, h=2)
                S.op(PL, TT(q4[:, 4:8, 0, :], r3[0], r3[1], ALU.subtract), reads=["rt0", "rt1"], writes=[f"qkr{b}"])
                S.op(PL, TT(q4[:, 4:8, 1, :], r3[2], r3[3], ALU.add), reads=["rt2", "rt3"], writes=[f"qkr{b}"])
                S.op(DV, TT(h4(vz[b]), h4(bv[:, :]), bc(zcol, 2, 128), ALU.mult), reads=[nv, "zcol"], writes=[f"vz{b}"])

            def ST(c):
                b = c % 2
                psu3 = h4(psu[:, :])
                for h in range(4):
                    S.op(PE, MM(psu3[:, h, :], a8v(qkr[b])[:, 4 + h, :], h4(vz[b])[:, h, :], True, True), reads=[f"qkr{b}", f"vz{b}"], writes=["psu"])
                for h in range(4):
                    S.op(DV, STT(Rst3[:, h, :], Rst3[:, h, :], GHEAD[h], psu3[:, h, :], ALU.mult, ALU.add), reads=["psu", "R"], writes=["R"])

            A(0)
            for c in range(8):
                if c + 1 < 8:
                    A(c + 1)
                B(c)
                ST(c)
            nb = rcnt[0] % 2
            S.op(AC, ACT(Rbs[nb], Rst, AF.Copy), reads=["R"], writes=[f"Rb{nb}"])
        else:

            def A1(c):
                for (col0, off) in ((0, 0), (512, 512)):
                    for kt in range(8):
                        S.op(PE, MM(PSt[off // 512][:, :], hT3[:, kt, tkc(c)], Win3[:, kt, col0:col0 + 512], kt == 0, kt == 7), reads=["hT", "Win"], writes=["ps0" if off == 0 else "ps1"])

            def A2(c):
                for (col0, off) in ((1024, 0), (1536, 512)):
                    for kt in range(8):
                        S.op(PE, MM(PSt[2 + off // 512][:, :], hT3[:, kt, tkc(c)], Win3[:, kt, col0:col0 + 512], kt == 0, kt == 7), reads=["hT", "Win"], writes=["ps2" if off == 0 else "ps3"])

            def B1(c):
                b = c % 2
                cosb = bc(rcos3[:, c, :], 1, 4); sinb = bc(rsin3[:, c, :], 1, 4)
                q4 = qkr[b].rearrange("p (a t j) -> p a t j", a=8, t=2)
                for half in range(2):
                    nm = "ps0" if half == 0 else "ps1"
                    xq = PSt[half][:, :].rearrange("p (a t j) -> p a t j", a=4, t=2)
                    x1 = xq[:, :, 0, :]; x2 = xq[:, :, 1, :]
                    r3 = [t_[:, 256 * half:256 * half + 256].rearrange("p (a j) -> p a j", a=4) for t_ in rt]
                    S.op(DV, TT(r3[0], x1, cosb, ALU.mult), reads=[nm, "rcos"], writes=[f"rt0{half}"])
                    S.op(DV, TT(r3[1], x2, sinb, ALU.mult), reads=[nm, "rsin"], writes=[f"rt1{half}"])
                    S.op(DV, TT(r3[2], x1, sinb, ALU.mult), reads=[nm, "rsin"], writes=[f"rt2{half}"])
                    S.op(DV, TT(r3[3], x2, cosb, ALU.mult), reads=[nm, "rcos"], writes=[f"rt3{half}"])
                    S.op(PL, TT(q4[:, 4 * half:4 * half + 4, 0, :], r3[0], r3[1], ALU.subtract), reads=[f"rt0{half}", f"rt1{half}"], writes=[f"qkr{b}"])
                    S.op(PL, TT(q4[:, 4 * half:4 * half + 4, 1, :], r3[2], r3[3], ALU.add), reads=[f"rt2{half}", f"rt3{half}"], writes=[f"qkr{b}"])

            def B2(c):
                b = c % 2
                for h in range(4):
                    S.op(AC, ACT(h4(vz[b])[:, h, :], h4(PSt[2][:, :])[:, h, :], AF.Copy, scale=zcol[:, h:h + 1]), reads=["ps2", "zcol"], writes=[f"vz{b}"])
                S.op(AC, ACT(vb[b], PSt[2][:, :], AF.Copy), reads=["ps2"], writes=[f"vb{b}"])
                S.op(AC, ACT(sgg[b], PSt[3][:, :], AF.Silu), reads=["ps3"], writes=[f"sgg{b}"])
                S.op(PL, TT(sgg[b], sgg[b], GGN, ALU.mult), reads=[f"sgg{b}", "GGN"], writes=[f"sgg{b}"])

            def Tqk(c):
                b = c % 2
                for a in range(8):
                    S.op(PE, TR(ptb[:, a * 128:(a + 1) * 128], a8v(qkr[b])[:, a, :], identb), reads=[f"qkr{b}", "identb"], writes=["pt"])
                S.op(AC, ACT(qkT[b], ptb[:, :], AF.Copy), reads=["pt"], writes=[f"qkT{b}"])

            def SC(c):
                b = c % 2
                pss3 = h4(pss[:, :]); qT = a8v(qkT[b])
                for h in range(4):
                    S.op(PE, MM(pss3[:, h, :], qT[:, 4 + h, :], qT[:, h, :], True, True), reads=[f"qkT{b}"], writes=["pss"])
                S.op(DV, TT(sTb3, pss3, MASKH3, ALU.mult), reads=["pss", "MASKH"], writes=["sTb"])

            def ST(c):
                b = c % 2
                psu3 = h4(psu[:, :])
                for h in range(4):
                    S.op(PE, MM(psu3[:, h, :], a8v(qkr[b])[:, 4 + h, :], h4(vz[b])[:, h, :], True, True), reads=[f"qkr{b}", f"vz{b}"], writes=["psu"])
                for h in range(4):
                    S.op(DV, STT(Rst3[:, h, :], Rst3[:, h, :], GHEAD[h], psu3[:, h, :], ALU.mult, ALU.add), reads=["psu", "R"], writes=["R"])
                nb = (rcnt[0] + c + 1) % 2
                S.op(AC, ACT(Rbs[nb], Rst, AF.Copy), reads=["R"], writes=[f"Rb{nb}"])

            def OUT(c):
                b = c % 2
                cb = (rcnt[0] + c) % 2
                pso3 = h4(pso[:, :]); qT = a8v(qkT[b])
                for h in range(4):
                    S.op(PE, MM(pso3[:, h, :], sTb3[:, h, :], h4(vb[b])[:, h, :], True, False), reads=["sTb", f"vb{b}"], writes=["pso"])
                    S.op(PE, MM(pso3[:, h, :], qT[:, h, :], h4(Rbs[cb])[:, h, :], False, True), reads=[f"qkT{b}", f"Rb{cb}"], writes=["pso"])

            def GN(c):
                b = c % 2
                pso3 = h4(pso[:, :])
                bst3 = bst.rearrange("p (h k) -> p h k", h=4); mv3 = mv.rearrange("p (h k) -> p h k", h=4)
                for h in range(4):
                    S.op(DV, lambda e, h=h: e.bn_stats(out=bst3[:, h, :], in_=pso3[:, h, :]), reads=["pso"], writes=["bst"])
                    S.op(DV, lambda e, h=h: e.bn_aggr(out=mv3[:, h, :], in_=bst3[:, h, :]), reads=["bst"], writes=["mv"])
                S.op(DV, TT(vtmp, mv3[:, :, 1], xi2, ALU.mult), reads=["mv", "xi2"], writes=["vtmp"])
                rsqrt_cols(rs, vtmp, 1.0, "vtmp", "rs")
                S.op(DV, TT(rs, rs, xi, ALU.mult), reads=["rs", "xi"], writes=["rs"])
                yn3 = h4(yn)
                for h in range(4):
                    S.op(DV, TS(yn3[:, h, :], pso3[:, h, :], mv3[:, h, 0:1], rs[:, h:h + 1], ALU.subtract, ALU.mult), reads=["pso", "mv", "rs"], writes=["yn"])
                S.op(PL, TT(yr, yn, sgg[b], ALU.mult), reads=["yn", f"sgg{b}"], writes=["yr"])

            def Tyr(c):
                for h in range(4):
                    S.op(PE, TR(ptb[:, h * 128:(h + 1) * 128], yr[:, h * 128:(h + 1) * 128], identb), reads=["yr", "identb"], writes=["pt"])
                S.op(AC, ACT(yrT3[:, :, tkc(c)], ptb[:, 0:512].rearrange("p (h i) -> p h i", h=4), AF.Copy), reads=["pt"], writes=["yrT"])

            if os.environ.get("KSEQ_R"):
                for c in range(8):
                    A1(c); A2(c); B1(c); B2(c); Tqk(c); SC(c); ST(c); OUT(c); GN(c); Tyr(c)
            else:
                A1(0); A2(0); B1(0); B2(0)
                for c in range(8):
                    Tqk(c)
                    if c + 1 < 8:
                        A1(c + 1)
                    if c > 0:
                        Tyr(c - 1)
                    SC(c); ST(c)
                    if c + 1 < 8:
                        B1(c + 1)
                        A2(c + 1)
                    OUT(c)
                    if c + 1 < 8:
                        B2(c + 1)
                    GN(c)
                Tyr(7)
        rcnt[0] += (0 if pre else 8)
        for s in range(8):
            bank, nm = (pss, "pss") if s % 2 == 0 else (pso, "pso")
            for kt in range(8):
                S.op(PE, MM(bank[:, :], hT3[:, kt, s::8], Win3[:, kt, 2048:2560], kt == 0, kt == 7), reads=["hT", "Win"], writes=[nm])
            S.op(AC, ACT(Tb4[:, :, s, :], bank[:, :].rearrange("p (g c) -> p g c", g=32), AF.Copy), reads=[nm], writes=["T"])
        S.barrier()
        Tflat = Tb.rearrange("p (g f) -> p g f", g=32)
        ptb2 = PSt[5].bitcast(BF16)
        for gq in range(4):
            tb_, nm = (ptb, "pt") if gq % 2 == 0 else (ptb2, "pss")
            for j in range(8):
                S.op(PE, TR(tb_[:, j * 128:(j + 1) * 128], Tflat[:, 8 * gq + j, :], identb), reads=["T", "identb"], writes=[nm])
            S.op(AC if gq % 2 else DV, (ACT(U83[:, 8 * gq:8 * gq + 8, :], tb_.rearrange("p (g n) -> p g n", g=8), AF.Copy) if gq % 2 else
                                        CP(U83[:, 8 * gq:8 * gq + 8, :], tb_.rearrange("p (g n) -> p g n", g=8))), reads=[nm], writes=["U8"])
        its = [(hf, gb) for hf in range(2) for gb in range(4)]

        def Mm(it):
            hf, gb = its[it]; p = it % 2
            nsl = slice(64 * hf, 64 * hf + 64)
            psr3 = PSt[2 * p][:, 0:256].rearrange("p (g m) -> p g m", g=4); psi3 = PSt[2 * p + 1][:, 0:256].rearrange("p (g m) -> p g m", g=4)
            for j in range(8):
                g = 8 * gb + j; par = j % 2; slot = j // 2
                ps_ = slice(64 * par, 64 * par + 64)
                S.op(PE, MM(psr3[ps_, slot, :], Wst4[:, g, 0, :], U83[:, g, nsl], True, True), reads=["U8", "Wst"], writes=[f"ps{2 * p}"])
                S.op(PE, MM(psi3[ps_, slot, :], Wst4[:, g, 1, :], U83[:, g, nsl], True, True), reads=["U8", "Wst"], writes=[f"ps{2 * p + 1}"])

        def R1(it):
            hf, gb = its[it]; p = it % 2
            g2s = slice(4 * gb, 4 * gb + 4)
            psr3 = PSt[2 * p][:, 0:256].rearrange("p (g m) -> p g m", g=4); psi3 = PSt[2 * p + 1][:, 0:256].rearrange("p (g m) -> p g m", g=4)
            cosv = COS3[:, g2s, :]; sinv = SIN3[:, g2s, :]
            nr_, ni_ = f"ps{2 * p}", f"ps{2 * p + 1}"
            S.op(DV, TT(v34(st_[p][0]), psr3, cosv, ALU.mult), reads=[nr_, "COS"], writes=[f"st{p}0"])
            S.op(DV, TT(v34(st_[p][1]), psi3, sinv, ALU.mult), reads=[ni_, "SIN"], writes=[f"st{p}1"])
            S.op(DV, TT(v34(st_[p][2]), psi3, cosv, ALU.mult), reads=[ni_, "COS"], writes=[f"st{p}2"])
            S.op(DV, TT(v34(st_[p][3]), psr3, sinv, ALU.mult), reads=[nr_, "SIN"], writes=[f"st{p}3"])
            S.op(PL, TT(PRE[p][0], st_[p][0], st_[p][1], ALU.add), reads=[f"st{p}0", f"st{p}1"], writes=[f"PRE{p}0"])
            S.op(PL, TT(PRE[p][1], st_[p][2], st_[p][3], ALU.subtract), reads=[f"st{p}2", f"st{p}3"], writes=[f"PRE{p}1"])

        def SCAN(it):
            hf, gb = its[it]; p = it % 2
            g2s = slice(4 * gb, 4 * gb + 4)
            if not pre:
                for ri in range(2):
                    S.op(AC, ACT(SPV4[:, ri, g2s, 64 * hf:64 * hf + 1], CAR3[:, ri, g2s].unsqueeze(2), AF.Copy), reads=[f"CAR{gb}"], writes=["SPV"])
            for ri in range(2):
                for slot in range(4):
                    g2 = 4 * gb + slot
                    S.op(DV, lambda e, ri=ri, slot=slot, g2=g2, p=p: e.tensor_tensor_scan(
                        out=v34(SCN[p][ri])[:, slot, :], data0=R8[:, g2:g2 + 1].broadcast_to([128, 64]), data1=v34(PRE[p][ri])[:, slot, :],
                        initial=CAR3[:, ri, g2:g2 + 1], op0=ALU.mult, op1=ALU.add), reads=[f"PRE{p}{ri}", f"CAR{gb}", "R8"], writes=[f"SCN{p}{ri}"])

        def R2(it):
            hf, gb = its[it]; p = it % 2
            g2s = slice(4 * gb, 4 * gb + 4)
            cl = slice(63, 64) if pre else slice(0, 64)
            cosv = COS3[:, g2s, cl]; sinv = SIN3[:, g2s, cl]
            w = lambda a: v34(a)[:, :, cl]
            S.op(DV, TT(w(st_[p][0]), w(SCN[p][0]), cosv, ALU.mult), reads=[f"SCN{p}0", "COS"], writes=[f"st{p}0"])
            S.op(DV, TT(w(st_[p][1]), w(SCN[p][1]), sinv, ALU.mult), reads=[f"SCN{p}1", "SIN"], writes=[f"st{p}1"])
            S.op(DV, TT(w(st_[p][2]), w(SCN[p][1]), cosv, ALU.mult), reads=[f"SCN{p}1", "COS"], writes=[f"st{p}2"])
            S.op(DV, TT(w(st_[p][3]), w(SCN[p][0]), sinv, ALU.mult), reads=[f"SCN{p}0", "SIN"], writes=[f"st{p}3"])
            S.op(PL, TT(w(FUL[p][0]), w(st_[p][0]), w(st_[p][1]), ALU.subtract), reads=[f"st{p}0", f"st{p}1"], writes=[f"FUL{p}0"])
            S.op(PL, TT(w(FUL[p][1]), w(st_[p][2]), w(st_[p][3]), ALU.add), reads=[f"st{p}2", f"st{p}3"], writes=[f"FUL{p}1"])
            for ri in range(2):
                S.op(PL, CP(CAR3[:, ri, g2s], v34(FUL[p][ri])[:, :, 63]), reads=[f"FUL{p}{ri}", f"SCN{p}0", f"SCN{p}1", "SPV"], writes=[f"CAR{gb}"])
                if not pre:
                    S.op(AC, ACT(SPV4[:, ri, g2s, 64 * hf + 1:64 * hf + 64], v34(FUL[p][ri])[:, :, 0:63], AF.Copy), reads=[f"FUL{p}{ri}"], writes=["SPV"])

        if os.environ.get("KSEQ_S"):
            for it in range(8):
                Mm(it); R1(it); SCAN(it); R2(it)
        else:
            Mm(0); Mm(1); R1(0)
            for it in range(8):
                if it + 1 < 8:
                    R1(it + 1)
                SCAN(it)
                R2(it)
                if it + 2 < 8:
                    Mm(it + 2)
        if not pre:
            for gq in range(8):
                py, nm = (PSt[0], "ps0") if gq % 2 == 0 else (PSt[1], "ps1")
                py3 = py[:, :].rearrange("p (g n) -> p g n", g=4)
                for j in range(4):
                    g = 4 * gq + j; par = g % 2; g2 = g // 2
                    ps_ = slice(64 * par, 64 * par + 64)
                    S.op(PE, MM(py3[:, j, :], Wintra3[:, g, :], U83[:, g, :], True, False), reads=["U8", "Wintra"], writes=[nm])
                    S.op(PE, MM(py3[:, j, :], Wcr4[ps_, 0, g2, :], SPV4[ps_, 0, g2, :], False, False), reads=["SPV", "Wcr"], writes=[nm])
                    S.op(PE, MM(py3[:, j, :], Wcr4[ps_, 1, g2, :], SPV4[ps_, 1, g2, :], False, True), reads=["SPV", "Wcr"], writes=[nm])
                S.op(AC, ACT(Y83[:, 4 * gq:4 * gq + 4, :], py3, AF.Gelu_apprx_tanh), reads=[nm], writes=["Y8"])
            for gq in range(4):
                tb_, nm = (ptb, "pt") if gq % 2 == 0 else (ptb2, "pss")
                for j in range(8):
                    S.op(PE, TR(tb_[:, j * 128:(j + 1) * 128], Y83[:, 8 * gq + j, :], identb), reads=["Y8", "identb"], writes=[nm])
                S.op(DV, CP(YT4[:, :, 8 * gq:8 * gq + 8, :].rearrange("p s g c -> p g s c"), tb_.rearrange("p (g s c) -> p g s c", g=8, s=8)), reads=[nm, "U8"], writes=["T"])
            pg0, pg1 = PSt[2], PSt[3]
            pms = [(PSt[0], PSt[1], "ps0", "ps1"), (PSt[6], PSt[7], "pso", "psu")]

            def T1(s):
                b = s % 2
                for kt in range(4):
                    S.op(PE, TR(ptb[:, kt * 128:(kt + 1) * 128], YT3[:, s, kt * 128:(kt + 1) * 128], identb), reads=["T", "identb"], writes=["pt"])
                S.op(AC, ACT(ysT[b], ptb[:, 0:512], AF.Copy), reads=["pt"], writes=[f"ysT{b}"])

            def GLU(s):
                b = s % 2
                ysT3 = ysT[b].rearrange("p (k n) -> p k n", k=4)
                for hfc, bank, nm in ((0, pg0, "ps2"), (1, pg1, "ps3")):
                    for kt in range(4):
                        S.op(PE, MM(bank[:, :], ysT3[:, kt, :], Wglu3[:, kt, hfc * 512:(hfc + 1) * 512], kt == 0, kt == 3), reads=[f"ysT{b}", "Wglu"], writes=[nm])
                S.op(AC, ACT(sig, pg1[:, :], AF.Sigmoid), reads=["ps3"], writes=["sig"])
                S.op(DV, TT(ys2[b], pg0[:, :], sig, ALU.mult), reads=["ps2", "sig"], writes=[f"ys2{b}"])

            def T2(s):
                b = s % 2
                for kt in range(4):
                    S.op(PE, TR(ptb2[:, kt * 128:(kt + 1) * 128], ys2[b][:, kt * 128:(kt + 1) * 128], identb), reads=[f"ys2{b}", "identb"], writes=["pss"])
                S.op(AC, ACT(ys2T[b], ptb2[:, 0:512], AF.Copy), reads=["pss"], writes=[f"ys2T{b}"])

            def WO(s):
                b = s % 2
                pm0, pm1, n0, n1 = pms[b]
                y2T3 = ys2T[b].rearrange("p (k n) -> p k n", k=4)
                for hfc, bank, nm in ((0, pm0, n0), (1, pm1, n1)):
                    for kt in range(8):
                        lhs = yrT3[:, kt, s::8] if kt < 4 else y2T3[:, kt - 4, :]
                        S.op(PE, MM(bank[:, :], lhs, Wout3[:, kt, hfc * 512:(hfc + 1) * 512], kt == 0, kt == 7), reads=["yrT", f"ys2T{b}", "Wout"], writes=[nm])
                if s == 0:
                    S.dma(DMA(xs2[0], xv[:, 0, :]), writes=["xs20"])
                if s + 1 < 8:
                    S.dma(DMA(xs2[(s + 1) % 2], xv[:, s + 1, :]), writes=[f"xs2{(s + 1) % 2}"])
                for hfc, bank, nm in ((0, pm0, n0), (1, pm1, n1)):
                    S.op(AC, ACT(junk2[:, 0:512], bank[:, :], AF.Square, accum_out=ss2[b][:, hfc:hfc + 1]), reads=[nm], writes=["junk2", f"ss2{b}{hfc}"])
                S.op(DV, TT(ss2[b][:, 0:1], ss2[b][:, 0:1], ss2[b][:, 1:2], ALU.add), reads=[f"ss2{b}0", f"ss2{b}1"], writes=[f"ss2{b}0"])
                rsqrt_cols(rs2[b][:, 0:1], ss2[b][:, 0:1], 1.0 / 1024, f"ss2{b}0", f"rs2{b}")
                for hfc, bank, nm in ((0, pm0, n0), (1, pm1, n1)):
                    cs_ = slice(hfc * 512, hfc * 512 + 512)
                    S.op(DV, STT(tm[b][:, cs_], bank[:, :], rs2[b][:, 0:1], GPOST[:, cs_], ALU.mult, ALU.mult), reads=[nm, f"rs2{b}", "GPOST"], writes=[f"tm{b}"])
                S.op(PL, TT(xs2[b], xs2[b], tm[b], ALU.add), reads=[f"tm{b}", f"xs2{b}"], writes=[f"xs2{b}"])
                r0 = 1024 * (sbi - NPRE)
                S.dma(DMA(x1s[r0:r0 + 1024, :].rearrange("(n s) d -> n s d", s=8)[:, s, :], xs2[b]), reads=[f"xs2{b}"], writes=["x1s"])

            if os.environ.get("KSEQ_W"):
                for s in range(8):
                    T1(s); GLU(s); T2(s); WO(s)
            else:
                T1(0); GLU(0); T1(1); T2(0)
                for s in range(1, 8):
                    GLU(s)
                    WO(s - 1)
                    if s + 1 < 8:
                        T1(s + 1)
                    T2(s)
                WO(7)
        S.barrier()

    ar.top = 0
    W1b = ar.bf(8 * 4096); W1b3 = W1b.rearrange("p (k c) -> p k c", k=8)
    W2b = ar.bf(32 * 1024); W2b3 = W2b.rearrange("p (k c) -> p k c", k=32)
    identb2 = ar.bf(128); GPOST2 = ar.f32(1024); g2c = ar.f32(8)
    X1g = ar.f32(4096); X1g3 = X1g.rearrange("p (j d) -> p j d", j=4)
    h2T = ar.bf(8 * 512); h2T3 = h2T.rearrange("p (k t) -> p k t", k=8)
    aT = ar.bf(32 * 512); aT3 = aT.rearrange("p (k t) -> p k t", k=32)
    hb2s = [ar.bf(1024), ar.bf(1024)]; jk = ar.bf(1024); rl = [ar.f32(512), ar.f32(512)]; tmb = ar.f32(1024); sq = ar.f32(4); rq = ar.f32(4); so = ar.f32(2); ro = ar.f32(2)
    mhalf2 = ar.f32(8)
    assert ar.top <= AW, ar.top
    stgB = aT.bitcast(F32) if False else None
    cst = X1g
    S.op(DV, CP(cst[:, 0:8], g2pre), reads=[], writes=["cst"])
    S.op(DV, CP(jk[:, 0:128], identb), reads=[], writes=["jk"])
    S.barrier()
    S.op(DV, CP(g2c, cst[:, 0:8]), reads=["cst"], writes=["g2c"])
    S.op(DV, CP(identb2, jk[:, 0:128]), reads=["jk"], writes=["identb2"])
    S.dma(DMA(GPOST2, dbc(g2post_d)), writes=["GPOST2"])
    S.op(PL, MS(mhalf2, -0.5), writes=["mhalf"])
    mh[0] = mhalf2
    S.barrier()
    w1_v = w1_d.rearrange("(k p) c -> p k c", p=128)
    w2_v = w2_d.rearrange("(k p) c -> p k c", p=128)
    for blk in range(8):
        S.dma(DMA(W1b3[:, :, blk * 512:(blk + 1) * 512], w1_v[:, :, blk * 512:(blk + 1) * 512], max_dma_last_dim=4096), q="pool", writes=[f"W1b{blk}"])
    for k4 in range(8):
        S.dma(DMA(W2b3[:, 4 * k4:4 * k4 + 4, :], w2_v[:, 4 * k4:4 * k4 + 4, :], max_dma_last_dim=4096), q="pool", writes=[f"W2b{k4}"])
    pf = [PSt[0], PSt[1], PSt[2], PSt[3]]
    pmo_pairs = [((PSt[5], PSt[6]), ("pmo0", "pmo1")), ((PSt[7], PSt[4]), ("pm7", "pt"))]
    NG = NT // 512 if STOP is None else 0
    x1v = lambda gi: x1s[512 * gi:512 * gi + 512, :].rearrange("(p j) d -> p j d", j=4)
    outv = lambda gi: out_d[512 * gi:512 * gi + 512, :].rearrange("(p j) d -> p j d", j=4)

    def ld_sq(gi, j):
        S.dma(DMA(X1g3[:, j, :], x1v(gi)[:, j, :]), writes=[f"X1g{j}"])
        S.op(AC, ACT(jk, X1g3[:, j, :], AF.Square, accum_out=sq[:, j:j + 1]), reads=[f"X1g{j}"], writes=["jk", "sq"])

    for j in range(4 if NG > 0 else 0):
        ld_sq(0, j)
    for gi in range(NG):
        rsqrt_cols(rq, sq, 1.0 / 1024, "sq", "rq")
        for j in range(4):
            hb2 = hb2s[j % 2]
            S.op(AC, ACT(hb2, X1g3[:, j, :], AF.Copy, scale=rq[:, j:j + 1]), reads=[f"X1g{j}", "rq"], writes=[f"hb2{j % 2}"])
            for kt in range(8):
                S.op(PE, TR(ptb[:, kt * 128:(kt + 1) * 128], hb2[:, kt * 128:(kt + 1) * 128], identb2), reads=[f"hb2{j % 2}", "identb2"], writes=["pt"])
            S.op(DV, TT(h2T3[:, :, j * 128:(j + 1) * 128], ptb.rearrange("p (k n) -> p k n", k=8), bc(g2c, 2, 128), ALU.mult), reads=["pt", "g2c"], writes=["h2T"])
        for ft in range(32):
            bk = ft % 4
            for kt in range(8):
                S.op(PE, MM(pf[bk][:, :], W1b3[:, kt, ft * 128:(ft + 1) * 128], h2T3[:, kt, :], kt == 0, kt == 7), reads=["h2T", f"W1b{ft // 4}"], writes=[f"pf{bk}"])
            S.op(AC, ACT(rl[ft % 2], pf[bk][:, :], AF.Relu), reads=[f"pf{bk}"], writes=[f"rl{ft % 2}"])
            S.op(DV, TT(aT3[:, ft, :], rl[ft % 2], rl[ft % 2], ALU.mult), reads=[f"rl{ft % 2}"], writes=["aT"])
        for j in range(4):
            pmo, pmn = pmo_pairs[j % 2]
            for hfc in range(2):
                for kt in range(32):
                    S.op(PE, MM(pmo[hfc][:, :], aT3[:, kt, j * 128:(j + 1) * 128], W2b3[:, kt, hfc * 512:(hfc + 1) * 512], kt == 0, kt == 31), reads=["aT", f"W2b{kt // 4}"], writes=[pmn[hfc]])
            for hfc in range(2):
                S.op(AC, ACT(jk[:, 0:512], pmo[hfc][:, :], AF.Square, accum_out=so[:, hfc:hfc + 1]), reads=[pmn[hfc]], writes=["jk", f"so{hfc}"])
            S.op(DV, TT(so[:, 0:1], so[:, 0:1], so[:, 1:2], ALU.add), reads=["so0", "so1"], writes=["so0"])
            rsqrt_cols(ro[:, 0:1], so[:, 0:1], 1.0 / 1024, "so0", "ro")
            for hfc in range(2):
                cs_ = slice(hfc * 512, hfc * 512 + 512)
                S.op(DV, STT(tmb[:, cs_], pmo[hfc][:, :], ro[:, 0:1], GPOST2[:, cs_], ALU.mult, ALU.mult), reads=[pmn[hfc], "ro", "GPOST2"], writes=["tmb"])
            S.op(PL, TT(X1g3[:, j, :], X1g3[:, j, :], tmb, ALU.add), reads=["tmb", f"X1g{j}"], writes=[f"X1g{j}"])
            S.dma(DMA(outv(gi)[:, j, :], X1g3[:, j, :]), reads=[f"X1g{j}"], writes=["out"])
            if gi + 1 < NG:
                ld_sq(gi + 1, j)
    S.barrier()
    sems = {k: es.enter_context(nc.semaphore(f"s_{k[0]}_{k[1]}")) for k in sorted(S.semkeys)}
    with nc.Block() as block:
        @block.tensor
        def _(e):
            S.replay("pe", e, sems)

        @block.scalar
        def _(e):
            S.replay("act", e, sems)

        @block.vector
        def _(e):
            S.replay("dve", e, sems)

        @block.gpsimd
        def _(e):
            S.replay("pool", e, sems)

        @block.sync
        def _(e):
            S.replay("sp", e, sems)
    es.close()
    return nc


def _run(x, params, NPRE, NMAIN, n_cores, core_plan):
    nc = build(NPRE, NMAIN)
    NT = NMAIN * 1024
    f = lambda a: np.ascontiguousarray(np.asarray(a, dtype=np.float32))
    base = {
        "norm_mix_pre": f(params["norm_mix_pre"]).reshape(8, 128), "norm_mix_post": f(params["norm_mix_post"]).reshape(1, 1024),
        "w_in": f(params["w_in"]).reshape(1024, 2560), "ret_gn_gain": f(params["ret_gn_gain"]).reshape(1, 512),
        "ssm_lambda_re": f(params["ssm_lambda_re"]).reshape(32, 64), "ssm_lambda_im": f(params["ssm_lambda_im"]).reshape(32, 64),
        "ssm_log_dt": f(params["ssm_log_dt"]).reshape(1, 32),
        "ssm_b_re": f(params["ssm_b_re"]).reshape(32, 64, 16), "ssm_b_im": f(params["ssm_b_im"]).reshape(32, 64, 16),
        "ssm_c_re": f(params["ssm_c_re"]).reshape(32, 16, 64), "ssm_c_im": f(params["ssm_c_im"]).reshape(32, 16, 64),
        "ssm_d": f(params["ssm_d"]).reshape(32, 16),
        "w_glu": f(params["w_glu"]).reshape(512, 1024), "w_out": f(params["w_out"]).reshape(1024, 1024),
        "norm_mlp_pre": f(params["norm_mlp_pre"]).reshape(8, 128), "norm_mlp_post": f(params["norm_mlp_post"]).reshape(1, 1024),
        "w_ff1": f(params["w_ff1"]).reshape(1024, 4096), "w_ff2": f(params["w_ff2"]).reshape(4096, 1024),
    }
    in_maps = []
    for (b, st) in core_plan:
        m = dict(base)
        m["x_own"] = f(x[b, st:st + NT])
        if st > 0:
            m["x_pre"] = f(x[b, st - NPRE * 1024:st])
            pb = np.array([st - NPRE * 1024, st], np.float32)
        else:
            m["x_pre"] = np.zeros((max(NPRE, 1) * 1024, 1024), np.float32)
            pb = np.array([0.0, 0.0], np.float32)
        m["posb"] = np.ascontiguousarray(np.broadcast_to(pb[None, :], (128, 2)))
        in_maps.append(m)
    res = run_bass_kernel_spmd(nc, in_maps, core_ids=list(range(n_cores)))
    out = np.zeros(x.shape, np.float32)
    for i, (b, st) in enumerate(core_plan):
        out[b, st:st + NT] = res.results[i]["out"]
    return out


def kernel(x, **params):
    x = np.asarray(x, dtype=np.float32)
    plan = [(b, h * 4096) for b in range(4) for h in range(2)]
    return _run(x, params, 4, 4, 8, plan)
```
